# Optimizing a Trainium2 kernel written in Bass

```python
import math
import jax, jax.numpy as jnp
from jax import lax
import numpy as np

D_MODEL = 1024
BATCH = 8
SEQ = 2048
DEPTH = 4
DEC_BATCH = 128
DEC_SEQ = 1
PAST_LEN = 8192
PAGE_SIZE = 128

N_EVEN = (DEPTH + 1) // 2
N_ODD = DEPTH // 2

S5_WIDTH = D_MODEL // 2
S5_GROUP = 16
S5_GROUPS = S5_WIDTH // S5_GROUP
S5_STATE = 64
SWA_HEADS = 8
SWA_KV_HEADS = 2
SWA_HEAD_DIM = 64
SWA_WIDTH = SWA_HEADS * SWA_HEAD_DIM
WINDOW = 128
SWA_BLOCK = WINDOW
D_IN_EVEN = S5_WIDTH + SWA_WIDTH + 2 * SWA_KV_HEADS * SWA_HEAD_DIM
D_MIX_EVEN = S5_WIDTH + SWA_WIDTH
SSD_INNER = 2 * D_MODEL
SSD_HEAD_DIM = 64
SSD_HEADS = SSD_INNER // SSD_HEAD_DIM
SSD_GROUPS = 4
SSD_STATE = 128
SSD_CONV = 4
SSD_CHUNK = 128
SSD_CONV_DIM = SSD_INNER + 2 * SSD_GROUPS * SSD_STATE
D_IN_ODD = SSD_INNER + SSD_CONV_DIM + SSD_HEADS
PEER_HEADS = 8
PEER_KEYS = 128
PEER_EXPERTS = PEER_KEYS * PEER_KEYS
PEER_DK = 256
PEER_TOPK = 16
PEER_BLOCK = 256
DN_ALPHA = (2 * DEPTH) ** 0.25
DN_BETA = (8 * DEPTH) ** -0.25
LN_EPS = 1e-5
RMS_EPS = 1e-5
NEG_INF = -1e30

kernel_name = 'hybrid_s5_swa_ssd_peer_step'


def _layernorm(x, g, b):
    xf = x.astype(jnp.float32)
    mu = jnp.mean(xf, axis=-1, keepdims=True)
    var = jnp.mean(jnp.square(xf - mu), axis=-1, keepdims=True)
    y = (xf - mu) * lax.rsqrt(var + LN_EPS) * g.astype(jnp.float32) + b.astype(jnp.float32)
    return y.astype(x.dtype)


def _cmul(ar, ai, br, bi):
    return ar * br - ai * bi, ar * bi + ai * br


def _s5(u, h0_re, h0_im, lam_re, lam_im, log_dt, b_re, b_im, c_re, c_im, d_skip, w_glu):
    f32 = jnp.float32
    bsz, L, _ = u.shape
    uf = u.astype(f32)
    ug = uf.reshape(bsz, L, S5_GROUPS, S5_GROUP)
    lr, li = lam_re.astype(f32), lam_im.astype(f32)
    dt = jnp.exp(log_dt.astype(f32))[:, None]
    mag = jnp.exp(lr * dt)
    ab_re, ab_im = mag * jnp.cos(li * dt), mag * jnp.sin(li * dt)
    den = lr * lr + li * li
    nr, ni = ab_re - 1.0, ab_im
    coef_re = (nr * lr + ni * li) / den
    coef_im = (ni * lr - nr * li) / den
    bb_re, bb_im = _cmul(coef_re[..., None], coef_im[..., None], b_re.astype(f32), b_im.astype(f32))
    bu_re = jnp.einsum('blgc,gpc->blgp', ug, bb_re)
    bu_im = jnp.einsum('blgc,gpc->blgp', ug, bb_im)
    h0r, h0i = _cmul(ab_re, ab_im, h0_re.astype(f32), h0_im.astype(f32))
    bu_re = bu_re.at[:, 0].add(h0r)
    bu_im = bu_im.at[:, 0].add(h0i)
    a_re = jnp.broadcast_to(ab_re, bu_re.shape)
    a_im = jnp.broadcast_to(ab_im, bu_im.shape)

    def combine(e1, e2):
        a1r, a1i, b1r, b1i = e1
        a2r, a2i, b2r, b2i = e2
        ar, ai = _cmul(a2r, a2i, a1r, a1i)
        tr, ti = _cmul(a2r, a2i, b1r, b1i)
        return ar, ai, tr + b2r, ti + b2i

    _, _, h_re, h_im = lax.associative_scan(combine, (a_re, a_im, bu_re, bu_im), axis=1)
    y = (jnp.einsum('blgp,gcp->blgc', h_re, c_re.astype(f32))
         - jnp.einsum('blgp,gcp->blgc', h_im, c_im.astype(f32)))
    y = y.reshape(bsz, L, S5_WIDTH) + d_skip.astype(f32) * uf
    g = jax.nn.gelu(y)
    out = g * jax.nn.sigmoid(g @ w_glu.astype(f32))
    return out.astype(u.dtype), h_re[:, -1].astype(h0_re.dtype), h_im[:, -1].astype(h0_im.dtype)


def _alibi_slopes():
    return 2.0 ** (-8.0 * jnp.arange(1, SWA_HEADS + 1, dtype=jnp.float32) / SWA_HEADS)


def _swa(q, k, v, k_buf, v_buf, start, sinks):
    f32 = jnp.float32
    bsz, L = q.shape[0], q.shape[1]
    w = k_buf.shape[1]
    rep = SWA_HEADS // SWA_KV_HEADS
    k_cat = jnp.concatenate([k_buf.astype(k.dtype), k], axis=1)
    v_cat = jnp.concatenate([v_buf.astype(v.dtype), v], axis=1)
    new_k, new_v = k_cat[:, -w:], v_cat[:, -w:]
    nb = -(-L // SWA_BLOCK)
    lp = nb * SWA_BLOCK
    pad = lp - L
    qp = jnp.pad(q, ((0, 0), (0, pad), (0, 0), (0, 0)))
    kp = jnp.pad(k_cat, ((0, 0), (0, pad), (0, 0), (0, 0)))
    vp = jnp.pad(v_cat, ((0, 0), (0, pad), (0, 0), (0, 0)))
    qb = qp.reshape(bsz, nb, SWA_BLOCK, SWA_KV_HEADS, rep, SWA_HEAD_DIM).astype(f32)
    kb = kp.reshape(bsz, nb + 1, SWA_BLOCK, SWA_KV_HEADS, SWA_HEAD_DIM)
    vb = vp.reshape(bsz, nb + 1, SWA_BLOCK, SWA_KV_HEADS, SWA_HEAD_DIM)
    kb = jnp.concatenate([kb[:, :-1], kb[:, 1:]], axis=2).astype(f32)
    vb = jnp.concatenate([vb[:, :-1], vb[:, 1:]], axis=2).astype(f32)
    qpos = start + jnp.arange(lp, dtype=jnp.int32).reshape(nb, SWA_BLOCK)
    kpos_all = start - w + jnp.arange(w + lp, dtype=jnp.int32).reshape(nb + 1, SWA_BLOCK)
    kpos = jnp.concatenate([kpos_all[:-1], kpos_all[1:]], axis=1)
    dist = qpos[:, :, None] - kpos[:, None, :]
    valid = (dist >= 0) & (dist < WINDOW) & (kpos[:, None, :] >= 0)
    s = jnp.einsum('bnqgrd,bnkgd->bngrqk', qb, kb) * (SWA_HEAD_DIM ** -0.5)
    slopes = _alibi_slopes().reshape(SWA_KV_HEADS, rep)[None, None, :, :, None, None]
    s = s - slopes * dist[None, :, None, None].astype(f32)
    s = jnp.where(valid[None, :, None, None], s, NEG_INF)
    sink = sinks.astype(f32).reshape(SWA_KV_HEADS, rep)[None, None, :, :, None, None]
    m = jnp.maximum(jnp.max(s, axis=-1, keepdims=True), sink)
    p = jnp.exp(s - m)
    p = p / (jnp.sum(p, axis=-1, keepdims=True) + jnp.exp(sink - m))
    o = jnp.einsum('bngrqk,bnkgd->bnqgrd', p, vb)
    o = o.reshape(bsz, lp, SWA_WIDTH)[:, :L]
    return o.astype(q.dtype), new_k.astype(k_buf.dtype), new_v.astype(v_buf.dtype)


def _even_mixer(x, h_re, h_im, k_buf, v_buf, start, w_in, lam_re, lam_im, log_dt,
                b_re, b_im, c_re, c_im, s5_d, w_glu, sinks, w_out):
    bsz, L, _ = x.shape
    kv_w = SWA_KV_HEADS * SWA_HEAD_DIM
    proj = x @ w_in
    u_a, q, k, v = jnp.split(proj, [S5_WIDTH, S5_WIDTH + SWA_WIDTH, S5_WIDTH + SWA_WIDTH + kv_w], axis=-1)
    y_a, h_re, h_im = _s5(u_a, h_re, h_im, lam_re, lam_im, log_dt, b_re, b_im, c_re, c_im, s5_d, w_glu)
    q = q.reshape(bsz, L, SWA_HEADS, SWA_HEAD_DIM)
    k = k.reshape(bsz, L, SWA_KV_HEADS, SWA_HEAD_DIM)
    v = v.reshape(bsz, L, SWA_KV_HEADS, SWA_HEAD_DIM)
    y_b, k_buf, v_buf = _swa(q, k, v, k_buf, v_buf, start, sinks)
    out = jnp.concatenate([y_a, y_b], axis=-1) @ w_out
    return out, h_re, h_im, k_buf, v_buf


def _causal_conv(xbc, buf, w, b):
    L = xbc.shape[1]
    xp = jnp.concatenate([buf.astype(xbc.dtype), xbc], axis=1)
    y = xp[:, 0:L] * w[0]
    for tap in range(1, SSD_CONV):
        y = y + xp[:, tap:tap + L] * w[tap]
    return jax.nn.silu(y + b), xp[:, -(SSD_CONV - 1):].astype(buf.dtype)


def _ssd(x, dt, a, bm, cm, s0):
    bsz, L = x.shape[0], x.shape[1]
    r = SSD_HEADS // SSD_GROUPS
    q = min(SSD_CHUNK, L)
    nc = -(-L // q)
    pad = nc * q - L
    x = jnp.pad(x, ((0, 0), (0, pad), (0, 0), (0, 0))).reshape(bsz, nc, q, SSD_GROUPS, r, SSD_HEAD_DIM)
    dt = jnp.pad(dt, ((0, 0), (0, pad), (0, 0))).reshape(bsz, nc, q, SSD_GROUPS, r)
    bm = jnp.pad(bm, ((0, 0), (0, pad), (0, 0), (0, 0))).reshape(bsz, nc, q, SSD_GROUPS, SSD_STATE)
    cm = jnp.pad(cm, ((0, 0), (0, pad), (0, 0), (0, 0))).reshape(bsz, nc, q, SSD_GROUPS, SSD_STATE)
    acs = jnp.cumsum(dt * a.reshape(SSD_GROUPS, r), axis=2)
    seg = acs[:, :, :, None] - acs[:, :, None, :]
    causal = (jnp.arange(q)[:, None] >= jnp.arange(q)[None, :])[None, None, :, :, None, None]
    decay = jnp.where(causal, jnp.exp(jnp.where(causal, seg, 0.0)), 0.0)
    cb = jnp.einsum('bctgn,bcsgn->bctsg', cm, bm)
    dx = dt[..., None] * x
    y_diag = jnp.einsum('bctsgr,bcsgrp->bctgrp', cb[..., None] * decay, dx)
    dstate = jnp.exp(acs[:, :, -1:] - acs)
    states = jnp.einsum('bcsgn,bcsgrp->bcgrpn', bm, dstate[..., None] * dx)
    chunk_decay = jnp.exp(acs[:, :, -1])

    def step(carry, inp):
        dec, st = inp
        return dec[..., None, None] * carry + st, carry

    init = s0.reshape(bsz, SSD_GROUPS, r, SSD_HEAD_DIM, SSD_STATE)
    final, s_in = lax.scan(step, init, (jnp.moveaxis(chunk_decay, 1, 0), jnp.moveaxis(states, 1, 0)))
    s_in = jnp.moveaxis(s_in, 0, 1)
    y_off = jnp.einsum('bctgn,bcgrpn->bctgrp', cm, s_in) * jnp.exp(acs)[..., None]
    y = (y_diag + y_off).reshape(bsz, nc * q, SSD_HEADS, SSD_HEAD_DIM)[:, :L]
    return y, final.reshape(bsz, SSD_HEADS, SSD_HEAD_DIM, SSD_STATE)


def _odd_mixer(x, ssm_state, conv_buf, w_in, conv_w, conv_b, dt_bias, a_log, d_skip, norm_w, w_out):
    f32 = jnp.float32
    bsz, L, _ = x.shape
    proj = x @ w_in
    z, xbc, dt_raw = jnp.split(proj, [SSD_INNER, SSD_INNER + SSD_CONV_DIM], axis=-1)
    xbc, conv_buf = _causal_conv(xbc, conv_buf, conv_w, conv_b)
    xs, bm, cm = jnp.split(xbc.astype(f32), [SSD_INNER, SSD_INNER + SSD_GROUPS * SSD_STATE], axis=-1)
    xs = xs.reshape(bsz, L, SSD_HEADS, SSD_HEAD_DIM)
    bm = bm.reshape(bsz, L, SSD_GROUPS, SSD_STATE)
    cm = cm.reshape(bsz, L, SSD_GROUPS, SSD_STATE)
    dt = jax.nn.softplus(dt_raw.astype(f32) + dt_bias.astype(f32))
    a = -jnp.exp(a_log.astype(f32))
    y, new_state = _ssd(xs, dt, a, bm, cm, ssm_state.astype(f32))
    y = y + d_skip.astype(f32)[:, None] * xs
    y = y.reshape(bsz, L, SSD_INNER) * jax.nn.silu(z.astype(f32))
    y = y * lax.rsqrt(jnp.mean(jnp.square(y), axis=-1, keepdims=True) + RMS_EPS) * norm_w.astype(f32)
    out = y.astype(x.dtype) @ w_out
    return out, new_state.astype(ssm_state.dtype), conv_buf


def _peer(x, w_q, k1, k2, u_tab, v_tab):
    shape = x.shape
    xt = x.reshape(-1, D_MODEL)
    t = xt.shape[0]
    blk = min(PEER_BLOCK, t)
    nb = -(-t // blk)
    xb = jnp.pad(xt, ((0, nb * blk - t), (0, 0))).reshape(nb, blk, D_MODEL)
    half = PEER_DK // 2

    def block(xc):
        q = (xc @ w_q).reshape(blk, PEER_HEADS, PEER_DK)
        s1 = jnp.einsum('thd,hkd->thk', q[..., :half], k1).astype(jnp.float32)
        s2 = jnp.einsum('thd,hkd->thk', q[..., half:], k2).astype(jnp.float32)
        v1, i1 = lax.top_k(s1, PEER_TOPK)
        v2, i2 = lax.top_k(s2, PEER_TOPK)
        cand = (v1[..., :, None] + v2[..., None, :]).reshape(blk, PEER_HEADS, PEER_TOPK * PEER_TOPK)
        cidx = (i1[..., :, None] * PEER_KEYS + i2[..., None, :]).reshape(blk, PEER_HEADS, PEER_TOPK * PEER_TOPK)
        sv, pos = lax.top_k(cand, PEER_TOPK)
        idx = jnp.take_along_axis(cidx, pos, axis=-1)
        gates = jax.nn.softmax(sv, axis=-1)
        u_sel = jnp.take(u_tab, idx, axis=0)
        h = jnp.einsum('thkd,td->thk', u_sel, xc).astype(jnp.float32)
        act = (jax.nn.gelu(h) * gates).astype(xc.dtype)
        v_sel = jnp.take(v_tab, idx, axis=0)
        return jnp.einsum('thk,thkd->td', act, v_sel)

    y = lax.map(block, xb).reshape(nb * blk, D_MODEL)[:t]
    return y.reshape(shape).astype(x.dtype)


def _trunk(x, s5_re, s5_im, k_buf, v_buf, ssm, conv, start, even_w, odd_w, chan_w):
    (w_in_e, lam_re, lam_im, log_dt, b_re, b_im, c_re, c_im, s5_d, w_glu, sinks, w_out_e) = even_w
    (w_in_o, conv_w, conv_b, dt_bias, a_log, ssd_d, norm_w, w_out_o) = odd_w
    (ln1_g, ln1_b, ln2_g, ln2_b, w_q, k1, k2, u_tab, v_tab) = chan_w
    n_re, n_im, n_k, n_v, n_ssm, n_conv = [], [], [], [], [], []
    for layer in range(DEPTH):
        i = layer // 2
        if layer % 2 == 0:
            mix, hr, hi, kb, vb = _even_mixer(x, s5_re[i], s5_im[i], k_buf[i], v_buf[i], start, w_in_e[i],
                                              lam_re[i], lam_im[i], log_dt[i], b_re[i], b_im[i], c_re[i],
                                              c_im[i], s5_d[i], w_glu[i], sinks[i], w_out_e[i])
            n_re.append(hr)
            n_im.append(hi)
            n_k.append(kb)
            n_v.append(vb)
        else:
            mix, st, cb = _odd_mixer(x, ssm[i], conv[i], w_in_o[i], conv_w[i], conv_b[i], dt_bias[i],
                                     a_log[i], ssd_d[i], norm_w[i], w_out_o[i])
            n_ssm.append(st)
            n_conv.append(cb)
        x = _layernorm(DN_ALPHA * x + mix, ln1_g[layer], ln1_b[layer])
        x = _layernorm(DN_ALPHA * x + _peer(x, w_q[layer], k1[layer], k2[layer], u_tab[layer], v_tab[layer]),
                       ln2_g[layer], ln2_b[layer])
    return (x, jnp.stack(n_re), jnp.stack(n_im), jnp.stack(n_k), jnp.stack(n_v),
            jnp.stack(n_ssm), jnp.stack(n_conv))


def setup_inputs(seed: int = 0) -> dict:
    key = jax.random.key(seed)
    ks = iter(jax.random.split(key, 48))
    f32 = jnp.float32

    def nrm(shape, scale):
        return jax.random.normal(next(ks), shape, f32) * scale

    w_buf = min(WINDOW, PAST_LEN)
    inp = {}
    inp['x_prompt'] = nrm((BATCH, SEQ, D_MODEL), 1.0)
    inp['x_sample'] = nrm((DEC_BATCH, DEC_SEQ, D_MODEL), 1.0)
    inp['state_s5_re'] = nrm((N_EVEN, DEC_BATCH, S5_GROUPS, S5_STATE), 0.3)
    inp['state_s5_im'] = nrm((N_EVEN, DEC_BATCH, S5_GROUPS, S5_STATE), 0.3)
    inp['cache_swa_k'] = nrm((N_EVEN, DEC_BATCH, w_buf, SWA_KV_HEADS, SWA_HEAD_DIM), 1.0)
    inp['cache_swa_v'] = nrm((N_EVEN, DEC_BATCH, w_buf, SWA_KV_HEADS, SWA_HEAD_DIM), 1.0)
    inp['state_ssd'] = nrm((N_ODD, DEC_BATCH, SSD_HEADS, SSD_HEAD_DIM, SSD_STATE), 0.1)
    inp['state_conv'] = nrm((N_ODD, DEC_BATCH, SSD_CONV - 1, SSD_CONV_DIM), 1.0)
    inp['w_in_even'] = nrm((N_EVEN, D_MODEL, D_IN_EVEN), D_MODEL ** -0.5)
    inp['s5_lambda_re'] = -0.5 + nrm((N_EVEN, S5_GROUPS, S5_STATE), 0.01)
    inp['s5_lambda_im'] = (math.pi * jnp.arange(S5_STATE, dtype=f32))[None, None, :] + nrm((N_EVEN, S5_GROUPS, S5_STATE), 0.01)
    inp['s5_log_dt'] = jax.random.uniform(next(ks), (N_EVEN, S5_GROUPS), f32, math.log(1e-3), math.log(1e-1))
    inp['s5_b_re'] = nrm((N_EVEN, S5_GROUPS, S5_STATE, S5_GROUP), (2 * S5_GROUP) ** -0.5)
    inp['s5_b_im'] = nrm((N_EVEN, S5_GROUPS, S5_STATE, S5_GROUP), (2 * S5_GROUP) ** -0.5)
    inp['s5_c_re'] = nrm((N_EVEN, S5_GROUPS, S5_GROUP, S5_STATE), (2 * S5_STATE) ** -0.5)
    inp['s5_c_im'] = nrm((N_EVEN, S5_GROUPS, S5_GROUP, S5_STATE), (2 * S5_STATE) ** -0.5)
    inp['s5_d'] = nrm((N_EVEN, S5_WIDTH), 1.0)
    inp['s5_w_glu'] = nrm((N_EVEN, S5_WIDTH, S5_WIDTH), S5_WIDTH ** -0.5)
    inp['swa_sinks'] = nrm((N_EVEN, SWA_HEADS), 0.5)
    inp['w_out_even'] = nrm((N_EVEN, D_MIX_EVEN, D_MODEL), DN_BETA * D_MIX_EVEN ** -0.5)
    inp['w_in_odd'] = nrm((N_ODD, D_MODEL, D_IN_ODD), D_MODEL ** -0.5)
    inp['ssd_conv_w'] = nrm((N_ODD, SSD_CONV, SSD_CONV_DIM), SSD_CONV ** -0.5)
    inp['ssd_conv_b'] = nrm((N_ODD, SSD_CONV_DIM), 0.01)
    dt0 = jnp.exp(jax.random.uniform(next(ks), (N_ODD, SSD_HEADS), f32, math.log(1e-3), math.log(1e-1)))
    inp['ssd_dt_bias'] = dt0 + jnp.log(-jnp.expm1(-dt0))
    inp['ssd_a_log'] = jnp.log(jax.random.uniform(next(ks), (N_ODD, SSD_HEADS), f32, 1.0, 16.0))
    inp['ssd_d'] = 1.0 + nrm((N_ODD, SSD_HEADS), 0.1)
    inp['ssd_norm_w'] = 1.0 + nrm((N_ODD, SSD_INNER), 0.01)
    inp['w_out_odd'] = nrm((N_ODD, SSD_INNER, D_MODEL), DN_BETA * SSD_INNER ** -0.5)
    inp['ln1_g'] = 1.0 + nrm((DEPTH, D_MODEL), 0.01)
    inp['ln1_b'] = nrm((DEPTH, D_MODEL), 0.01)
    inp['ln2_g'] = 1.0 + nrm((DEPTH, D_MODEL), 0.01)
    inp['ln2_b'] = nrm((DEPTH, D_MODEL), 0.01)
    inp['peer_w_q'] = nrm((DEPTH, D_MODEL, PEER_HEADS * PEER_DK), D_MODEL ** -0.5)
    inp['peer_k1'] = nrm((DEPTH, PEER_HEADS, PEER_KEYS, PEER_DK // 2), (PEER_DK // 2) ** -0.5)
    inp['peer_k2'] = nrm((DEPTH, PEER_HEADS, PEER_KEYS, PEER_DK // 2), (PEER_DK // 2) ** -0.5)
    inp['peer_u'] = nrm((DEPTH, PEER_EXPERTS, D_MODEL), D_MODEL ** -0.5)
    inp['peer_v'] = nrm((DEPTH, PEER_EXPERTS, D_MODEL), DN_BETA * PEER_HEADS ** -0.5)
    return inp


def reference(x_prompt, x_sample, state_s5_re, state_s5_im, cache_swa_k, cache_swa_v, state_ssd, state_conv,
              w_in_even, s5_lambda_re, s5_lambda_im, s5_log_dt, s5_b_re, s5_b_im, s5_c_re, s5_c_im, s5_d,
              s5_w_glu, swa_sinks, w_out_even,
              w_in_odd, ssd_conv_w, ssd_conv_b, ssd_dt_bias, ssd_a_log, ssd_d, ssd_norm_w, w_out_odd,
              ln1_g, ln1_b, ln2_g, ln2_b, peer_w_q, peer_k1, peer_k2, peer_u, peer_v):
    even_w = (w_in_even, s5_lambda_re, s5_lambda_im, s5_log_dt, s5_b_re, s5_b_im, s5_c_re, s5_c_im,
              s5_d, s5_w_glu, swa_sinks, w_out_even)
    odd_w = (w_in_odd, ssd_conv_w, ssd_conv_b, ssd_dt_bias, ssd_a_log, ssd_d, ssd_norm_w, w_out_odd)
    chan_w = (ln1_g, ln1_b, ln2_g, ln2_b, peer_w_q, peer_k1, peer_k2, peer_u, peer_v)
    bp = x_prompt.shape[0]
    dty = x_prompt.dtype
    z_s5 = jnp.zeros((N_EVEN, bp, S5_GROUPS, S5_STATE), dty)
    z_kv = jnp.zeros((N_EVEN, bp, WINDOW, SWA_KV_HEADS, SWA_HEAD_DIM), dty)
    z_ssm = jnp.zeros((N_ODD, bp, SSD_HEADS, SSD_HEAD_DIM, SSD_STATE), dty)
    z_conv = jnp.zeros((N_ODD, bp, SSD_CONV - 1, SSD_CONV_DIM), dty)
    y_prompt, p_s5_re, p_s5_im, p_swa_k, p_swa_v, p_ssd, p_conv = _trunk(
        x_prompt, z_s5, z_s5, z_kv, z_kv, z_ssm, z_conv, 0, even_w, odd_w, chan_w)
    y_sample, s_s5_re, s_s5_im, s_swa_k, s_swa_v, s_ssd, s_conv = _trunk(
        x_sample, state_s5_re, state_s5_im, cache_swa_k, cache_swa_v, state_ssd, state_conv, PAST_LEN,
        even_w, odd_w, chan_w)
    return (y_prompt, y_sample, p_s5_re, p_s5_im, p_swa_k, p_swa_v, p_ssd, p_conv,
            s_s5_re, s_s5_im, s_swa_k, s_swa_v, s_ssd, s_conv)
```

```python
import numpy as np
from contextlib import ExitStack
import concourse.bass as bass
import concourse.mybir as mybir
from concourse.bass_utils import run_bass_kernel_spmd

F32 = mybir.dt.float32
BF16 = mybir.dt.bfloat16
I32 = mybir.dt.int32
U32 = mybir.dt.uint32
AF = mybir.ActivationFunctionType
ALU = mybir.AluOpType
AX = mybir.AxisListType

SEM_GEN = 20000
NDSEM = 24


class Buf:
    def __init__(self, t, name=""):
        self.t = t
        self.name = name
        self.w = None
        self.r = {}

    def __getitem__(self, idx):
        return V(self, self.t[idx])

    def ap(self):
        return V(self, self.t.ap() if hasattr(self.t, "ap") else self.t[:])


class V:
    def __init__(self, buf, ap):
        self.buf = buf
        self.ap = ap

    def __getitem__(self, idx):
        return V(self.buf, self.ap[idx])

    def re(self, s, **kw):
        return V(self.buf, self.ap.rearrange(s, **kw))

    def bc(self, shape):
        return V(self.buf, self.ap.to_broadcast(shape))

    def bitcast(self, dt):
        return V(self.buf, self.ap.bitcast(dt))


class Prog:
    ENG = ("pe", "dve", "act", "pool", "sp")

    def __init__(self, nc, es):
        self.nc = nc
        self.es = es
        self.eng = {"pe": nc.tensor, "dve": nc.vector, "act": nc.scalar, "pool": nc.gpsimd, "sp": nc.sync}
        self.ops = {e: [] for e in self.ENG}
        self.cnt = {e: 0 for e in self.ENG}
        self.sems = {e: [] for e in self.ENG}
        self.seen = {e: {} for e in self.ENG}
        self.dsem = {}
        self.dcur = {e: 0 for e in self.ENG}
        self.nsem = 0
        self.out_tokens = []
        self.ninstr = 0

    def sb(self, name, shape, dt=F32):
        return Buf(self.es.enter_context(self.nc.sbuf_tensor(name, list(shape), dt)), name)

    def ps(self, name, shape, dt=F32):
        b = Buf(self.es.enter_context(self.nc.psum_tensor(name, list(shape), dt)), name)
        b.excl = True
        return b

    def dram(self, name, shape, dt=F32, kind="Internal"):
        return Buf(self.nc.dram_tensor(name, list(shape), dt, kind=kind), name)

    def _newsem(self, name):
        self.nsem += 1
        return self.es.enter_context(self.nc.semaphore(name))

    def _esem(self, e, gen):
        while len(self.sems[e]) <= gen:
            self.sems[e].append(self._newsem(f"s_{e}_{len(self.sems[e])}"))
        return self.sems[e][gen]

    def _need(self, e, tok):
        if tok is None:
            return
        sem, val, key, src = tok
        if self.seen[e].get(key, 0) >= val:
            return
        self.seen[e][key] = val
        self.eng[e].wait_ge(sem, val)

    def _deps(self, e, ins, outs, pe_accum=False):
        for v in ins:
            self._need(e, v.buf.w)
            if getattr(v.buf, "excl", False):
                for t in v.buf.r.values():
                    if t[3] != e:
                        self._need(e, t)
        for v in outs:
            b = v.buf
            if not (pe_accum and b.w is not None and b.w[3] == "pe" and e == "pe"):
                self._need(e, b.w)
            for t in b.r.values():
                self._need(e, t)

    def _commit(self, tok, ins, outs):
        for v in ins:
            if v.buf not in [o.buf for o in outs]:
                v.buf.r[tok[2]] = tok
        for v in outs:
            v.buf.w = tok
            v.buf.r = {}

    def op(self, e, fn, ins=(), outs=(), pe_accum=False):
        if getattr(self, "mute", None) and self.mute["on"]:
            return None
        ins = [v for v in ins if isinstance(v, V)]
        outs = [v for v in outs if isinstance(v, V)]
        self._deps(e, ins, outs, pe_accum)
        gen = self.cnt[e] // SEM_GEN
        sem = self._esem(e, gen)
        self.cnt[e] += 1
        val = self.cnt[e] - gen * SEM_GEN
        fn(self.eng[e]).then_inc(sem, 1)
        tok = (sem, val, (e, gen), e)
        self._commit(tok, ins, outs)
        self.ninstr += 1
        return tok

    def dma(self, q, out, in_, fn=None, is_output=False, **kw):
        if getattr(self, "mute", None) and self.mute["on"]:
            return None
        ins = [in_] + list(kw.pop("extra_ins", []))
        outs = [out]
        self._deps(q, ins, outs)
        ring = self.dsem.setdefault(q, [])
        if len(ring) < NDSEM:
            ring.append([self._newsem(f"d_{q}_{len(ring)}"), 0])
            slot = len(ring) - 1
        else:
            slot = self.dcur[q] % NDSEM
        self.dcur[q] += 1
        ent = ring[slot]
        sem, uses = ent
        key = ("d", q, slot)
        if uses > 0:
            self._need(q, (sem, 16 * uses, key, "dma"))
        ent[1] = uses + 1
        if fn is None:
            fn = lambda eng, o=out.ap, i=in_.ap, kw=kw: eng.dma_start(out=o, in_=i, **kw)
        fn(self.eng[q]).then_inc(sem, 16)
        tok = (sem, 16 * (uses + 1), key, "dma")
        self._commit(tok, ins, outs)
        if is_output:
            self.out_tokens.append(tok)
        self.ninstr += 1
        return tok

    def finish(self):
        for tok in self.out_tokens:
            self._need("sp", tok)

    def mm(self, out, lhsT, rhs, start=True, stop=True):
        return self.op("pe", lambda g: g.matmul(out.ap, lhsT.ap, rhs.ap, start=start, stop=stop),
                       ins=[lhsT, rhs], outs=[out], pe_accum=not start)

    def tr(self, out, in_, ident):
        return self.op("pe", lambda g: g.transpose(out.ap, in_.ap, ident.ap), ins=[in_, ident], outs=[out])

    def act(self, out, in_, func, bias=None, scale=None, accum=None, e="act"):
        kw = {}
        ins = [in_]
        outs = [out]
        if bias is not None:
            kw["bias"] = bias.ap if isinstance(bias, V) else bias
            ins.append(bias)
        if scale is not None:
            kw["scale"] = scale.ap if isinstance(scale, V) else scale
            ins.append(scale)
        if accum is not None:
            kw["accum_out"] = accum.ap
            outs.append(accum)
        return self.op("act", lambda g: g.activation(out.ap, in_.ap, func, **kw), ins=ins, outs=outs)

    def tt(self, out, a, b, op, e="dve"):
        return self.op(e, lambda g: g.tensor_tensor(out.ap, a.ap, b.ap, op), ins=[a, b], outs=[out])

    def ts(self, out, a, s1, s2, op0, op1=None, accum=None, e="dve"):
        ins = [a, s1, s2]
        outs = [out] + ([accum] if accum is not None else [])
        x1 = s1.ap if isinstance(s1, V) else s1
        x2 = s2.ap if isinstance(s2, V) else s2
        kw = {}
        if op1 is not None:
            kw["op1"] = op1
        if accum is not None:
            kw["accum_out"] = accum.ap
        return self.op(e, lambda g: g.tensor_scalar(out.ap, a.ap, x1, x2, op0, **kw), ins=ins, outs=outs)

    def stt(self, out, a, s, b, op0, op1, accum=None):
        x = s.ap if isinstance(s, V) else s
        kw = {"accum_out": accum.ap} if accum is not None else {}
        outs = [out] + ([accum] if accum is not None else [])
        return self.op("dve", lambda g: g.scalar_tensor_tensor(out.ap, a.ap, x, b.ap, op0, op1, **kw),
                       ins=[a, s, b], outs=outs)

    def ttr(self, out, a, b, op0, op1, accum, scale=1.0, scalar=0.0):
        return self.op("dve", lambda g: g.tensor_tensor_reduce(out.ap, a.ap, b.ap, scale, scalar, op0, op1, accum.ap),
                       ins=[a, b], outs=[out, accum])

    def copy(self, out, in_, e="dve"):
        if e == "act":
            return self.op("act", lambda g: g.copy(out.ap, in_.ap), ins=[in_], outs=[out])
        return self.op(e, lambda g: g.tensor_copy(out.ap, in_.ap), ins=[in_], outs=[out])

    def memset(self, out, val, e="dve"):
        return self.op(e, lambda g: g.memset(out.ap, val), outs=[out])

    def red(self, out, in_, op, axis=AX.X, e="dve"):
        return self.op(e, lambda g: g.tensor_reduce(out.ap, in_.ap, axis, op), ins=[in_], outs=[out])

    def recip(self, out, in_):
        return self.op("dve", lambda g: g.reciprocal(out.ap, in_.ap), ins=[in_], outs=[out])

HP = [0, 4, 1, 5, 2, 6, 3, 7]
ALPHA = float(8 ** 0.25)
NTILES = 17
NEG = -1.0e30
TWO_PI = 6.283185307179586


class _Stop(Exception):
    pass


def build_nc(n_layers=4, debug=None, small=False):
    mute = {"on": False}

    def ck(tag):
        if debug == tag:
            mute["on"] = True

    nc = bass.Bass("TRN2", target_bir_lowering=False)
    D = {}

    def din(name, shape, dt=F32):
        D[name] = Buf(nc.dram_tensor(name, list(shape), dt, kind="ExternalInput"), name)

    def dout(name, shape):
        D[name] = Buf(nc.dram_tensor(name, list(shape), F32, kind="ExternalOutput"), name)

    for name, shape in IN_SHAPES:
        if small and name == "peer_v":
            shape = [4, 128, 1024]
        if small and name == "peer_ut":
            shape = [4, 1024, 128]
        if small == 2 and name in ("w_in_o", "w_q", "st_ssd", "w_out_o"):
            shape = [shape[0], 128] + list(shape[2:]) if name != "st_ssd" else [2, 16, 128, 128]
        din(name, shape)
    for name, shape in OUT_SHAPES:
        dout(name, shape)

    with ExitStack() as es:
        P = Prog(nc, es)
        P.mute = mute
        xdram = Buf(nc.dram_tensor("xscratch", [NTILES, 128, 1024], F32, kind=("ExternalOutput" if debug else "Internal")), "xscratch")
        xd = [Buf(xdram.t, f"xd{i}")[i] for i in range(NTILES)]

        def bcast_rows(src_buf, row, ncols, nparts=128):
            return V(src_buf, src_buf.t[row:row + 1, 0:ncols].to_broadcast([nparts, ncols]))

        identf = P.sb("identf", [128, 128]); P.dma("sp", identf[:, :], D["c_ident"][:, :])
        identb = P.sb("identb", [128, 128], BF16); P.copy(identb[:, :], identf[:, :])
        triT = P.sb("triT", [128, 128]); P.dma("sp", triT[:, :], D["c_tri"][:, :])
        e127 = P.sb("e127", [128, 128]); P.dma("sp", e127[:, :], D["c_e127"][:, :])
        b255f = P.sb("b255f", [128, 255]); P.dma("sp", b255f[:, :], D["c_b255"][:, :])
        b255 = P.sb("b255", [128, 255], BF16); P.copy(b255[:, :], b255f[:, :])
        kkp = P.sb("kkp", [128, 1]); P.dma("sp", kkp[:, :], D["c_kk"][:, :])
        kkn = P.sb("kkn", [128, 1]); P.ts(kkn[:, :], kkp[:, :], -1.0, None, ALU.mult)
        one1 = P.sb("one1", [128, 1]); P.memset(one1[:, :], 1.0)
        mone1 = P.sb("mone1", [128, 1]); P.memset(mone1[:, :], -1.0)

        banks = [P.ps(f"bank{i}", [128, 512]) for i in range(8)]
        rot = {"i": 0}

        def pbank():
            b = banks[rot["i"] % 6]
            rot["i"] += 1
            return b

        ev = {"i": 0}

        def evac(out, in_, e=None):
            if e is None:
                e = "act" if ev["i"] % 2 == 0 else "dve"
                ev["i"] += 1
            return P.copy(out, in_, e=e)

        uid = {"i": 0}

        def ring(sc, name, shape, dt, n):
            uid["i"] += 1
            name = f"{name}_{uid['i']}_"
            bufs = [Buf(sc.enter_context(nc.sbuf_tensor(f"{name}{i}", list(shape), dt)), f"{name}{i}") for i in range(n)]
            st = {"i": 0}

            def nxt():
                b = bufs[st["i"] % n]
                st["i"] += 1
                return b
            return nxt

        def sbuf(sc, name, shape, dt=F32):
            uid["i"] += 1
            name = f"{name}_{uid['i']}"
            return Buf(sc.enter_context(nc.sbuf_tensor(name, list(shape), dt)), name)

        def barrier():
            if mute["on"]:
                return
            toks = []
            for e in P.ENG:
                if P.cnt[e] > 0:
                    gen = (P.cnt[e] - 1) // SEM_GEN
                    toks.append((P.sems[e][gen], P.cnt[e] - gen * SEM_GEN, (e, gen), e))
            for q, rg in P.dsem.items():
                for slot, (sem, uses) in enumerate(rg):
                    if uses > 0:
                        toks.append((sem, 16 * uses, ("d", q, slot), "dma"))
            for e in P.ENG:
                for t in toks:
                    if t[3] != e or t[3] == "dma":
                        P._need(e, t)

        def transposes(dst, srcs, n, dt):
            per = 8 if dt == BF16 else 4
            ident = identb if dt == BF16 else identf
            i = 0
            while i < len(srcs):
                grp = srcs[i:i + per]
                bk = pbank()
                bv = bk[:, :].bitcast(BF16) if dt == BF16 else bk[:, :]
                for j, s in enumerate(grp):
                    w = s.ap.shape[1]
                    P.tr(bv[0:w, j * 128:j * 128 + n], s, ident[0:n, 0:n])
                wmax = max(s.ap.shape[1] for s in grp)
                evac(dst[0:wmax, i:i + len(grp), 0:n],
                     bv[0:wmax, 0:len(grp) * 128].re("p (j t) -> p j t", t=128)[:, :, 0:n])
                i += per

        def make_xT(sc_ring_xb, xT, xt, n):
            xb = sc_ring_xb()
            P.copy(xb[0:n, :], xt[0:n, :], e="pool")
            transposes(xT[:, :, :], [xb[0:n, k * 128:(k + 1) * 128] for k in range(8)], n, BF16)

        def xsrc(layer_first, ti):
            if layer_first:
                if ti < 16:
                    return D["x_p"][ti * 128:(ti + 1) * 128, :]
                return D["x_s"][:, :]
            return xd[ti][0:(128 if ti < 16 else 16), :]

        def layernorm(sc_tmp, x_old, mix_views, n, g_bc, b_bc, out_buf):
            s = sc_tmp["ln_s"]()
            for hh in range(2):
                P.stt(s[0:n, hh * 512:(hh + 1) * 512], x_old[0:n, hh * 512:(hh + 1) * 512], ALPHA, mix_views[hh],
                      ALU.mult, ALU.add)
            st = sc_tmp["ln_st"]()
            for hh in range(2):
                P.op("dve", lambda g, hh=hh: g.bn_stats(st.t[0:n, hh * 6:(hh + 1) * 6], s.t[0:n, hh * 512:(hh + 1) * 512]),
                     ins=[s[:, :]], outs=[st[:, :]])
            mv = sc_tmp["ln_mv"]()
            P.op("dve", lambda g: g.bn_aggr(mv.t[0:n, 0:2], st.t[0:n, 0:12]), ins=[st[:, :]], outs=[mv[:, :]])
            P.ts(mv[0:n, 2:3], mv[0:n, 1:2], 1e-5, None, ALU.add)
            P.act(mv[0:n, 3:4], mv[0:n, 2:3], AF.Sqrt)
            P.recip(mv[0:n, 4:5], mv[0:n, 3:4])
            P.ts(s[0:n, :], s[0:n, :], mv[0:n, 0:1], mv[0:n, 4:5], ALU.subtract, ALU.mult)
            P.tt(s[0:n, :], s[0:n, :], g_bc[0:n, :], ALU.mult, e="pool")
            P.tt(s[0:n, :], s[0:n, :], b_bc[0:n, :], ALU.add, e="pool")
            return s

        def even_phase(layer):
            li = layer // 2
            first = layer == 0
            with ExitStack() as sc:
                w_in = sbuf(sc, "e_win", [128, 8, 1280], BF16)
                w_out = sbuf(sc, "e_wout", [128, 8, 1024], BF16)
                w_glu = sbuf(sc, "e_wglu", [128, 4, 512], BF16)
                c_re = sbuf(sc, "e_cre", [128, 512], BF16)
                c_im = sbuf(sc, "e_cim", [128, 512], BF16)
                bb_re = sbuf(sc, "e_bbre", [128, 4, 512], BF16)
                bb_im = sbuf(sc, "e_bbim", [128, 4, 512], BF16)
                d_bc = sbuf(sc, "e_dbc", [128, 512])
                sk_p = sbuf(sc, "e_skp", [128, 8])
                sk_s = sbuf(sc, "e_sks", [128, 1])
                lng = sbuf(sc, "e_lng", [128, 1024]); lnb = sbuf(sc, "e_lnb", [128, 1024])
                bias_p = sbuf(sc, "e_biasp", [128, 8, 256])
                bias_s = sbuf(sc, "e_biass", [128, 128])
                tabs = {k: sbuf(sc, f"e_tab_{k}", [128, 2048]) for k in ("pr", "pi")}
                P.dma("pool", w_in[:, :, :], D["w_in_e"][li].re("(k p) n -> p k n", p=128))
                P.dma("pool", w_out[:, :, :], D["w_out_e"][li].re("(k p) n -> p k n", p=128))
                P.dma("pool", w_glu[:, :, :], D["w_glu"][li].re("(k p) n -> p k n", p=128))
                P.dma("pool", c_re[:, :], D["cblk_re"][li])
                with ExitStack() as s0:
                    cimf = sbuf(s0, "e_cimf", [128, 512])
                    P.dma("sp", cimf[:, :], D["cblk_im"][li])
                    P.ts(c_im[:, :], cimf[:, :], -1.0, None, ALU.mult)
                    barrier()
                P.dma("sp", d_bc[:, :], bcast_rows(D["s5_d"], li, 512))
                P.dma("sp", sk_p[:, :], bcast_rows(D["sinks_p"], li, 8))
                P.dma("sp", sk_s[:, :], D["sinks_s"][li].re("(p o) -> p o", o=1))
                P.dma("sp", lng[:, :], bcast_rows(D["ln1_g"], layer, 1024))
                P.dma("sp", lnb[:, :], bcast_rows(D["ln1_b"], layer, 1024))
                P.dma("sp", bias_p[:, :, :], D["c_bias_p"][:, :].re("p (h c) -> p h c", h=8))
                P.dma("sp", bias_s[:, :], D["c_bias_s"][:, :])

                def build_tables(kk_pos, kk_neg, with_bb):
                    with ExitStack() as s2:
                        T_ = lambda nm: sbuf(s2, f"e_su_{nm}", [128, 256])
                        lr, lim, dt_, ldt, lid, ang, tq, sn, cs, mg, t1, t2, t3 = [T_(n_) for n_ in
                            ("lr", "li", "dt", "ldt", "lid", "ang", "tq", "sn", "cs", "mg", "t1", "t2", "t3")]
                        tqi = sbuf(s2, "e_su_tqi", [128, 256], I32)
                        bsr = sbuf(s2, "e_su_bsr", [128, 256]); bsi = sbuf(s2, "e_su_bsi", [128, 256])
                        bbr_flat = bb_re[:, :, :].re("p c n -> p (c n)")
                        bbi_flat = bb_im[:, :, :].re("p c n -> p (c n)")
                        for c in range(8):
                            cs_ = slice(c * 256, (c + 1) * 256)
                            P.dma("sp", lr[:, :], V(D["lam_re"], D["lam_re"].t[li:li + 1, cs_].to_broadcast([128, 256])))
                            P.dma("sp", lim[:, :], V(D["lam_im"], D["lam_im"].t[li:li + 1, cs_].to_broadcast([128, 256])))
                            P.dma("sp", dt_[:, :], V(D["log_dt"], D["log_dt"].t[li:li + 1, cs_].to_broadcast([128, 256])))
                            P.act(dt_[:, :], dt_[:, :], AF.Exp)
                            P.tt(ldt[:, :], lr[:, :], dt_[:, :], ALU.mult)
                            P.tt(lid[:, :], lim[:, :], dt_[:, :], ALU.mult)

                            def sincos(kk, shift, outt):
                                P.ts(ang[:, :], lid[:, :], kk[:, 0:1], shift, ALU.mult, ALU.add)
                                P.ts(tq[:, :], ang[:, :], 1.0 / TWO_PI, None, ALU.mult)
                                P.copy(tqi[:, :], tq[:, :])
                                P.copy(tq[:, :], tqi[:, :])
                                P.stt(ang[:, :], tq[:, :], -TWO_PI, ang[:, :], ALU.mult, ALU.add)
                                P.ts(ang[:, :], ang[:, :], 3.1415925, -3.1415925, ALU.min, ALU.max)
                                P.act(outt[:, :], ang[:, :], AF.Sin)

                            def cplx_tab(kk, tre, tim):
                                sincos(kk, 0.0, sn)
                                sincos(kk, 1.5707963267948966, cs)
                                P.act(mg[:, :], ldt[:, :], AF.Exp, scale=kk[:, 0:1])
                                P.tt(tre[:, cs_], mg[:, :], cs[:, :], ALU.mult)
                                P.tt(tim[:, cs_], mg[:, :], sn[:, :], ALU.mult)

                            cplx_tab(kk_pos, tabs["pr"], tabs["pi"])
                            if kk_neg is not None:
                                cplx_tab(kk_neg, tabs["nr"], tabs["ni"])
                            if with_bb:
                                sincos(one1, 0.0, sn)
                                sincos(one1, 1.5707963267948966, cs)
                                P.act(mg[:, :], ldt[:, :], AF.Exp)
                                P.tt(t1[:, :], mg[:, :], cs[:, :], ALU.mult)
                                P.tt(t2[:, :], mg[:, :], sn[:, :], ALU.mult)
                                P.ts(t1[:, :], t1[:, :], -1.0, None, ALU.add)
                                P.tt(t3[:, :], lr[:, :], lr[:, :], ALU.mult)
                                P.tt(tq[:, :], lim[:, :], lim[:, :], ALU.mult)
                                P.tt(t3[:, :], t3[:, :], tq[:, :], ALU.add)
                                P.recip(t3[:, :], t3[:, :])
                                P.tt(tq[:, :], t1[:, :], lr[:, :], ALU.mult)
                                P.tt(ang[:, :], t2[:, :], lim[:, :], ALU.mult)
                                P.tt(tq[:, :], tq[:, :], ang[:, :], ALU.add)
                                P.tt(cs[:, :], tq[:, :], t3[:, :], ALU.mult)
                                P.tt(tq[:, :], t2[:, :], lr[:, :], ALU.mult)
                                P.tt(ang[:, :], t1[:, :], lim[:, :], ALU.mult)
                                P.tt(tq[:, :], tq[:, :], ang[:, :], ALU.subtract)
                                P.tt(sn[:, :], tq[:, :], t3[:, :], ALU.mult)
                                P.dma("sp", bsr[:, :], D["bblk_re"][li, :, cs_])
                                P.dma("sp", bsi[:, :], D["bblk_im"][li, :, cs_])
                                P.tt(t1[:, :], cs[:, :], bsr[:, :], ALU.mult)
                                P.tt(t2[:, :], sn[:, :], bsi[:, :], ALU.mult)
                                P.tt(bbr_flat[:, cs_], t1[:, :], t2[:, :], ALU.subtract)
                                P.tt(t1[:, :], cs[:, :], bsi[:, :], ALU.mult)
                                P.tt(t2[:, :], sn[:, :], bsr[:, :], ALU.mult)
                                P.tt(bbi_flat[:, cs_], t1[:, :], t2[:, :], ALU.add)
                    barrier()

                x_r = ring(sc, "e_x", [128, 1024], F32, 2)
                xb_r = ring(sc, "e_xb", [128, 1024], BF16, 2)
                xT_r = ring(sc, "e_xT", [128, 8, 128], BF16, 2)
                u_r = ring(sc, "e_u", [128, 512], F32, 2)
                ub_r = ring(sc, "e_ub", [128, 512], BF16, 2)
                uT_r = ring(sc, "e_uT", [128, 4, 128], BF16, 2)
                tmp_r = ring(sc, "e_tmp", [128, 512], F32, 4)
                w_r = ring(sc, "e_w", [128, 512], F32, 2)
                hre = [sbuf(sc, f"e_hre{c}", [128, 512]) for c in range(4)]
                him = [sbuf(sc, f"e_him{c}", [128, 512]) for c in range(4)]
                hT_re = sbuf(sc, "e_hTre", [128, 16, 128], BF16)
                hT_im = sbuf(sc, "e_hTim", [128, 16, 128], BF16)
                y_r = ring(sc, "e_y", [128, 512], F32, 3)
                gb_r = ring(sc, "e_gb", [128, 512], BF16, 1)
                gT_r = ring(sc, "e_gT", [128, 4, 128], BF16, 1)
                mix_r = ring(sc, "e_mix", [128, 1024], BF16, 1)
                mixT_r = ring(sc, "e_mixT", [128, 8, 128], BF16, 1)
                qb_r = ring(sc, "e_qb", [128, 512], BF16, 2)
                kvf_r = ring(sc, "e_kvf", [128, 256], F32, 2)
                sm_r = ring(sc, "e_sm", [128, 8], F32, 8)
                lnr = {"ln_s": ring(sc, "e_lns", [128, 1024], F32, 1), "ln_st": ring(sc, "e_lnst", [128, 12], F32, 2),
                       "ln_mv": ring(sc, "e_lnmv", [128, 8], F32, 2)}
                smid = ExitStack()
                sc.callback(smid.close)
                for k in ("nr", "ni"):
                    tabs[k] = sbuf(smid, f"e_tab_{k}", [128, 2048])
                build_tables(kkp, kkn, True)
                ck("e_setup")

                with ExitStack() as s3:
                    qT_r = ring(s3, "e_qT", [128, 4, 128], BF16, 2)
                    kvb = [sbuf(s3, f"e_kvb{i}", [128, 256], BF16) for i in range(2)]
                    kT = [sbuf(s3, f"e_kT{i}", [128, 1, 128], BF16) for i in range(2)]
                    sb_r = ring(s3, "e_sb", [128, 256], F32, 4)
                    pb_r = ring(s3, "e_pb", [128, 256], BF16, 4)
                    pT_r = ring(s3, "e_pT", [128, 2, 128], BF16, 4)

                    def s5_tail_and_rest(ti, n, xt, xT, pj_q, pj_kv, u, sample):
                        transposes(hT_re[:, :, :], [hre[c][0:n, k * 128:(k + 1) * 128] for c in range(4) for k in range(4)], n, F32)
                        transposes(hT_im[:, :, :], [him[c][0:n, k * 128:(k + 1) * 128] for c in range(4) for k in range(4)], n, F32)
                        py = pbank()
                        for kc in range(16):
                            P.mm(py[0:n, kc * 32:(kc + 1) * 32], hT_re[:, kc, 0:n], c_re[:, kc * 32:(kc + 1) * 32], start=True, stop=False)
                            P.mm(py[0:n, kc * 32:(kc + 1) * 32], hT_im[:, kc, 0:n], c_im[:, kc * 32:(kc + 1) * 32], start=False, stop=True)
                        du = y_r()
                        P.tt(du[0:n, :], d_bc[0:n, :], u[0:n, :], ALU.mult, e="pool")
                        y2 = y_r()
                        P.tt(y2[0:n, :], py[0:n, :], du[0:n, :], ALU.add)
                        g = y_r()
                        P.act(g[0:n, :], y2[0:n, :], AF.Gelu_apprx_tanh)
                        gb = gb_r()
                        P.copy(gb[0:n, :], g[0:n, :], e="pool")
                        gT = gT_r()
                        transposes(gT[:, :, :], [gb[0:n, k * 128:(k + 1) * 128] for k in range(4)], n, BF16)
                        pz = pbank()
                        for k in range(4):
                            P.mm(pz[0:n, :], gT[:, k, 0:n], w_glu[:, k, :], start=(k == 0), stop=(k == 3))
                        sg = tmp_r()
                        P.act(sg[0:n, :], pz[0:n, :], AF.Sigmoid)
                        mix = mix_r()
                        P.tt(mix[0:n, 0:512], g[0:n, :], sg[0:n, :], ALU.mult)
                        return mix

                    def out_and_ln(ti, n, xt, mixT):
                        po = [banks[6], banks[7]]
                        for hh in range(2):
                            for k in range(8):
                                P.mm(po[hh][0:n, :], mixT[:, k, 0:n], w_out[:, k, hh * 512:(hh + 1) * 512], start=(k == 0), stop=(k == 7))
                        xo = layernorm(lnr, xt, [po[0][0:n, :], po[1][0:n, :]], n, lng, lnb, None)
                        P.dma("sp", xd[ti][0:n, :], xo[0:n, :])

                    for ti in range(16):
                        if debug and debug.startswith("e_t") and ti >= int(debug[3:]):
                            mute["on"] = True
                        n = 128
                        xt = x_r()
                        P.dma("sp", xt[0:n, :], xsrc(first, ti))
                        xT = xT_r()
                        make_xT(xb_r, xT, xt, n)
                        ck("ck_xT")
                        pj = [pbank(), pbank(), pbank()]
                        for j, (c0, cw) in enumerate(((0, 512), (512, 512), (1024, 256))):
                            for k in range(8):
                                P.mm(pj[j][0:n, 0:cw], xT[:, k, 0:n], w_in[:, k, c0:c0 + cw], start=(k == 0), stop=(k == 7))
                        ck("ck_mm")
                        u = u_r(); ub = ub_r()
                        if debug != "ck_ev_b":
                            evac(u[0:n, :], pj[0][0:n, :], e="act")
                        if debug != "ck_ev_a":
                            evac(ub[0:n, :], pj[0][0:n, :], e="dve")
                        ck("ck_ev"); ck("ck_ev_a"); ck("ck_ev_b")
                        qb = qb_r()
                        P.act(qb[0:n, :], pj[1][0:n, :], AF.Copy, scale=0.125)
                        kvf = kvf_r()
                        evac(kvf[0:n, :], pj[2][0:n, 0:256], e="act")
                        cur = ti % 2
                        evac(kvb[cur][0:n, :], pj[2][0:n, 0:256], e="dve")
                        uT = uT_r()
                        transposes(uT[:, :, :], [ub[0:n, k * 128:(k + 1) * 128] for k in range(4)], n, BF16)
                        ck("ck_proj")
                        for c in range(4):
                            cs_ = slice(c * 512, (c + 1) * 512)
                            pre, pim = pbank(), pbank()
                            P.mm(pre[0:n, :], uT[:, c, 0:n], bb_re[:, c, :])
                            P.mm(pim[0:n, :], uT[:, c, 0:n], bb_im[:, c, :])
                            t1, t2, wre, wim = tmp_r(), tmp_r(), w_r(), w_r()
                            P.tt(t1[:, :], tabs["nr"][:, cs_], pre[:, :], ALU.mult)
                            P.tt(t2[:, :], tabs["ni"][:, cs_], pim[:, :], ALU.mult)
                            P.tt(wre[:, :], t1[:, :], t2[:, :], ALU.subtract, e="pool")
                            t3, t4 = tmp_r(), tmp_r()
                            P.tt(t3[:, :], tabs["nr"][:, cs_], pim[:, :], ALU.mult)
                            P.tt(t4[:, :], tabs["ni"][:, cs_], pre[:, :], ALU.mult)
                            P.tt(wim[:, :], t3[:, :], t4[:, :], ALU.add, e="pool")
                            cre, cim = pbank(), pbank()
                            P.mm(cre[:, :], triT[:, :], wre[:, :], start=True, stop=(ti == 0))
                            P.mm(cim[:, :], triT[:, :], wim[:, :], start=True, stop=(ti == 0))
                            if ti > 0:
                                P.mm(cre[:, :], e127[:, :], hre[c][:, :], start=False, stop=True)
                                P.mm(cim[:, :], e127[:, :], him[c][:, :], start=False, stop=True)
                            t1, t2 = tmp_r(), tmp_r()
                            P.tt(t1[:, :], tabs["pr"][:, cs_], cre[:, :], ALU.mult)
                            P.tt(t2[:, :], tabs["pi"][:, cs_], cim[:, :], ALU.mult)
                            P.tt(hre[c][:, :], t1[:, :], t2[:, :], ALU.subtract, e="pool")
                            t3, t4 = tmp_r(), tmp_r()
                            P.tt(t3[:, :], tabs["pr"][:, cs_], cim[:, :], ALU.mult)
                            P.tt(t4[:, :], tabs["pi"][:, cs_], cre[:, :], ALU.mult)
                            P.tt(him[c][:, :], t3[:, :], t4[:, :], ALU.add, e="pool")
                        ck("ck_scan")
                        mix = s5_tail_and_rest(ti, n, xt, xT, None, None, u, False)
                        ck("ck_s5")
                        qT = qT_r()
                        transposes(qT[:, :, :], [qb[0:n, k * 128:(k + 1) * 128] for k in range(4)], n, BF16)
                        transposes(kT[cur][:, :, :], [kvb[cur][0:n, 0:128]], n, BF16)
                        for j8 in range(8):
                            jp, half = j8 // 2, j8 % 2
                            ps_ = slice(half * 64, (half + 1) * 64)
                            S = pbank()
                            c0 = 0 if ti > 0 else 128
                            if ti > 0:
                                P.mm(S[:, 0:128], qT[ps_, jp, :], kT[1 - cur][ps_, 0, :])
                            P.mm(S[:, 128:256], qT[ps_, jp, :], kT[cur][ps_, 0, :])
                            sbv = sb_r()
                            P.tt(sbv[:, c0:256], S[:, c0:256], bias_p[:, j8, c0:256], ALU.add)
                            sm = sm_r()
                            P.red(sm[:, 0:1], sbv[:, c0:256], ALU.max)
                            P.ts(sm[:, 1:2], sm[:, 0:1], sk_p[:, j8:j8 + 1], -1.0, ALU.max, ALU.mult)
                            pbv = pb_r()
                            P.act(pbv[:, c0:256], sbv[:, c0:256], AF.Exp, bias=sm[:, 1:2], accum=sm[:, 2:3])
                            P.act(sm[:, 3:4], sm[:, 1:2], AF.Exp, bias=sk_p[:, j8:j8 + 1])
                            P.tt(sm[:, 4:5], sm[:, 2:3], sm[:, 3:4], ALU.add)
                            P.recip(sm[:, 5:6], sm[:, 4:5])
                            pT = pT_r()
                            srcs = ([pbv[:, 0:128]] if ti > 0 else []) + [pbv[:, 128:256]]
                            transposes(pT[:, (0 if ti > 0 else 1):2, :], srcs, n, BF16)
                            po_ = pbank()
                            if ti > 0:
                                P.mm(po_[:, 0:64], pT[:, 0, :], kvb[1 - cur][:, 128 + half * 64:128 + (half + 1) * 64], start=True, stop=False)
                            P.mm(po_[:, 0:64], pT[:, 1, :], kvb[cur][:, 128 + half * 64:128 + (half + 1) * 64], start=(ti == 0), stop=True)
                            P.ts(mix[:, 512 + j8 * 64:512 + (j8 + 1) * 64], po_[:, 0:64], sm[:, 5:6], None, ALU.mult)
                        ck("ck_swa")
                        mixT = mixT_r()
                        transposes(mixT[:, :, :], [mix[0:n, k * 128:(k + 1) * 128] for k in range(8)], n, BF16)
                        out_and_ln(ti, n, xt, mixT)
                        ck("ck_ln")
                        if ti == 15:
                            for c in range(4):
                                P.dma("sp", D["p_s5_re"][li:li + 1, c * 512:(c + 1) * 512], hre[c][127:128, :], is_output=True)
                                P.dma("sp", D["p_s5_im"][li:li + 1, c * 512:(c + 1) * 512], him[c][127:128, :], is_output=True)
                            P.dma("sp", D["p_swa_k"][li], kvf[:, 0:128], is_output=True)
                            P.dma("sp", D["p_swa_v"][li], kvf[:, 128:256], is_output=True)

                    barrier()
                    s3.close()
                    smid.close()
                    build_tables(one1, None, False)
                    n = 16
                    ti = 16
                    xt = x_r()
                    P.dma("sp", xt[0:n, :], xsrc(first, ti))
                    xT = xT_r()
                    make_xT(xb_r, xT, xt, n)
                    pj = [pbank(), pbank(), pbank()]
                    for j, (c0, cw) in enumerate(((0, 512), (512, 512), (1024, 256))):
                        for k in range(8):
                            P.mm(pj[j][0:n, 0:cw], xT[:, k, 0:n], w_in[:, k, c0:c0 + cw], start=(k == 0), stop=(k == 7))
                    u = u_r(); ub = ub_r()
                    evac(u[0:n, :], pj[0][0:n, :], e="act")
                    evac(ub[0:n, :], pj[0][0:n, :], e="dve")
                    qb = qb_r()
                    P.act(qb[0:n, :], pj[1][0:n, :], AF.Copy, scale=0.125)
                    kvf = kvf_r()
                    evac(kvf[0:n, :], pj[2][0:n, 0:256], e="act")
                    uT = uT_r()
                    transposes(uT[:, :, :], [ub[0:n, k * 128:(k + 1) * 128] for k in range(4)], n, BF16)
                    for c in range(4):
                        cs_ = slice(c * 512, (c + 1) * 512)
                        pre, pim = pbank(), pbank()
                        P.mm(pre[0:n, :], uT[:, c, 0:n], bb_re[:, c, :])
                        P.mm(pim[0:n, :], uT[:, c, 0:n], bb_im[:, c, :])
                        h0r, h0i = w_r(), w_r()
                        P.dma("sp", h0r[0:n, :], D["st_s5_re"][li, :, cs_])
                        P.dma("sp", h0i[0:n, :], D["st_s5_im"][li, :, cs_])
                        t1, t2, t3, t4 = tmp_r(), tmp_r(), tmp_r(), tmp_r()
                        P.tt(t1[0:n, :], tabs["pr"][0:n, cs_], h0r[0:n, :], ALU.mult)
                        P.tt(t2[0:n, :], tabs["pi"][0:n, cs_], h0i[0:n, :], ALU.mult)
                        P.tt(t1[0:n, :], t1[0:n, :], t2[0:n, :], ALU.subtract)
                        P.tt(hre[c][0:n, :], t1[0:n, :], pre[0:n, :], ALU.add)
                        P.tt(t3[0:n, :], tabs["pr"][0:n, cs_], h0i[0:n, :], ALU.mult)
                        P.tt(t4[0:n, :], tabs["pi"][0:n, cs_], h0r[0:n, :], ALU.mult)
                        P.tt(t3[0:n, :], t3[0:n, :], t4[0:n, :], ALU.add)
                        P.tt(him[c][0:n, :], t3[0:n, :], pim[0:n, :], ALU.add)
                        P.dma("sp", D["s_s5_re"][li, :, cs_], hre[c][0:n, :], is_output=True)
                        P.dma("sp", D["s_s5_im"][li, :, cs_], him[c][0:n, :], is_output=True)
                    mix = s5_tail_and_rest(ti, n, xt, xT, None, None, u, True)
                    mixT = mixT_r()
                    transposes(mixT[:, 0:4, :], [mix[0:n, k * 128:(k + 1) * 128] for k in range(4)], n, BF16)
                    with ExitStack() as s4:
                        KN = sbuf(s4, "e_KN", [128, 16, 128]); VN = KN
                        KNb = sbuf(s4, "e_KNb", [128, 16, 128], BF16); VNb = sbuf(s4, "e_VNb", [128, 16, 128], BF16)
                        KTs = sbuf(s4, "e_KTs", [128, 16, 128], BF16)
                        qTs = sbuf(s4, "e_qTs", [128, 4, 16], BF16)
                        sst = sbuf(s4, "e_sst", [128, 128]); ssb = sbuf(s4, "e_ssb", [128, 128])
                        Pn = sbuf(s4, "e_Pn", [128, 128], BF16); PT = sbuf(s4, "e_PT", [128, 1, 128], BF16)
                        for (XN, cache, outn, c0) in ((KN, "cache_k", "s_swa_k", 0), (VN, "cache_v", "s_swa_v", 128)):
                            P.dma("sp", XN[0:127, :, :], D[cache][li].re("b r f -> r b f")[1:128])
                            P.dma("sp", D[outn][li].re("b r f -> r b f")[0:127], XN[0:127, :, :], is_output=True)
                            P.dma("sp", D[outn][li].re("b r f -> r b f")[127], kvf[0:16, c0:c0 + 128], is_output=True)
                            P.dma("sp", XN[127:128, :, :], D[outn][li].re("b r f -> r b f")[127:128])
                            P.copy((KNb if c0 == 0 else VNb)[:, :, :], XN[:, :, :], e="pool")
                        transposes(KTs[:, :, :], [KNb[:, b, :] for b in range(16)], 128, BF16)
                        transposes(qTs[:, :, :], [qb[0:n, k * 128:(k + 1) * 128] for k in range(4)], n, BF16)
                        pst = pbank()
                        for b in range(16):
                            for half in range(2):
                                ps_ = slice(half * 64, (half + 1) * 64)
                                P.mm(pst[:, b * 8 + half:b * 8 + 8:2], KTs[ps_, b, :], qTs[ps_, :, b])
                        evac(sst[:, :], pst[:, 0:128])
                        pS = pbank()
                        P.tr(pS[:, 0:128], sst[:, :], identf[:, :])
                        P.tt(ssb[:, :], pS[:, 0:128], bias_s[:, :], ALU.add)
                        sm = sm_r()
                        P.red(sm[:, 0:1], ssb[:, :], ALU.max)
                        P.ts(sm[:, 1:2], sm[:, 0:1], sk_s[:, 0:1], -1.0, ALU.max, ALU.mult)
                        P.act(ssb[:, :], ssb[:, :], AF.Exp, bias=sm[:, 1:2], accum=sm[:, 2:3])
                        P.act(sm[:, 3:4], sm[:, 1:2], AF.Exp, bias=sk_s[:, 0:1])
                        P.tt(sm[:, 4:5], sm[:, 2:3], sm[:, 3:4], ALU.add)
                        P.recip(sm[:, 5:6], sm[:, 4:5])
                        P.ts(Pn[:, :], ssb[:, :], sm[:, 5:6], None, ALU.mult)
                        transposes(PT[:, :, :], [Pn[:, :]], 128, BF16)
                        poT = pbank()
                        for b in range(16):
                            for half in range(2):
                                ps_ = slice(half * 64, (half + 1) * 64)
                                P.mm(poT[ps_, 0:64].re("p (j b) -> p j b", b=16)[:, :, b], VNb[:, b, ps_], PT[:, 0, b * 8 + half:b * 8 + 8:2])
                        evac(mixT[:, 4:8, 0:16], poT[:, 0:64].re("p (j b) -> p j b", b=16))
                    out_and_ln(ti, n, xt, mixT)
            barrier()

        def peer_phase(layer):
            last = layer == 3
            with ExitStack() as sc:
                wq = sbuf(sc, "p_wq", [128, 8, 2048], BF16)
                k1t = sbuf(sc, "p_k1t", [128, 8, 128], BF16); k2t = sbuf(sc, "p_k2t", [128, 8, 128], BF16)
                lng = sbuf(sc, "p_lng", [128, 1024]); lnb = sbuf(sc, "p_lnb", [128, 1024])
                io16 = sbuf(sc, "p_io16", [128, 16])
                P.dma("pool", wq[:, :, :], D["w_q"][layer].re("(k p) n -> p k n", p=128))
                P.dma("pool", k1t[:, :, :], D["k1t"][layer].re("d (h k) -> d h k", h=8))
                P.dma("pool", k2t[:, :, :], D["k2t"][layer].re("d (h k) -> d h k", h=8))
                P.dma("sp", lng[:, :], bcast_rows(D["ln2_g"], layer, 1024))
                P.dma("sp", lnb[:, :], bcast_rows(D["ln2_b"], layer, 1024))
                P.dma("sp", io16[:, :], D["c_iota16"][:, :])
                x_r = ring(sc, "p_x", [128, 1024], F32, 2)
                xb_r = ring(sc, "p_xb", [128, 1024], BF16, 2)
                xT_r = ring(sc, "p_xT", [128, 8, 128], BF16, 2)
                qT = sbuf(sc, "p_qT", [128, 16, 128], BF16)
                scs = sbuf(sc, "p_sc", [128, 16, 128]); sc2 = sbuf(sc, "p_sc2", [128, 128])
                v16 = sbuf(sc, "p_v16", [128, 16, 16]); i16 = sbuf(sc, "p_i16", [128, 16, 16], U32); i16f = sbuf(sc, "p_i16f", [128, 16, 16])
                cand = sbuf(sc, "p_cand", [128, 8, 256]); cand2 = sbuf(sc, "p_cand2", [128, 256])
                big1 = sbuf(sc, "p_big1", [128, 8, 256]); big2 = sbuf(sc, "p_big2", [128, 8, 256])
                sv = sbuf(sc, "p_sv", [128, 8, 16]); pos = sbuf(sc, "p_pos", [128, 8, 16], U32)
                posf = sbuf(sc, "p_posf", [128, 8, 16]); ikf = sbuf(sc, "p_ikf", [128, 8, 16]); jkf = sbuf(sc, "p_jkf", [128, 8, 16])
                iki = sbuf(sc, "p_iki", [128, 8, 16], I32)
                e1 = sbuf(sc, "p_e1", [128, 8, 16]); e2 = sbuf(sc, "p_e2", [128, 8, 16])
                gat = sbuf(sc, "p_gat", [128, 8, 16]); gsum = sbuf(sc, "p_gsum", [128, 8])
                idxT = sbuf(sc, "p_idxT", [128, 128], I32); gateT = sbuf(sc, "p_gateT", [128, 128])
                hTall = sbuf(sc, "p_hT", [128, 128]); actT = sbuf(sc, "p_actT", [128, 128])
                junk_r = ring(sc, "p_junk", [128, 1024], BF16, 2)
                lhs_r = ring(sc, "p_lhs", [128, 128], BF16, 4)
                U_r = ring(sc, "p_U", [128, 1024], BF16, 6)
                V_r = ring(sc, "p_V", [128, 1024], BF16, 6)
                lnr = {"ln_s": ring(sc, "p_lns", [128, 1024], F32, 2), "ln_st": ring(sc, "p_lnst", [128, 12], F32, 2),
                       "ln_mv": ring(sc, "p_lnmv", [128, 8], F32, 2)}
                utab = D["peer_u"].t.ap().rearrange("l e d -> (l e) d")
                vtab = D["peer_v"].t.ap().rearrange("l e d -> (l e) d")
                nexp = D["peer_u"].t.shape[1]

                for ti in range(NTILES):
                    if debug is not None and debug.startswith("peerfirst") and len(debug) > 9 and ti not in (0, 16):
                        continue
                    n = 128 if ti < 16 else 16
                    xt = x_r()
                    P.dma("sp", xt[0:n, :], xsrc(debug is not None and debug.startswith("peerfirst"), ti))
                    xT = xT_r()
                    xb = xb_r()
                    P.copy(xb[0:n, :], xt[0:n, :], e="pool")
                    transposes(xT[:, :, :], [xb[0:n, k * 128:(k + 1) * 128] for k in range(8)], n, BF16)
                    for f0 in range(0, 16, 4):
                        bk = pbank()
                        for f in range(f0, f0 + 4):
                            for k in range(8):
                                P.mm(bk[:, (f - f0) * 128:(f - f0) * 128 + n], wq[:, k, f * 128:(f + 1) * 128], xT[:, k, 0:n],
                                     start=(k == 0), stop=(k == 7))
                        evac(qT[:, f0:f0 + 4, 0:n], bk[:, :].re("p (j t) -> p j t", t=128)[:, :, 0:n])
                    for f0 in range(0, 16, 4):
                        bk = pbank()
                        for f in range(f0, f0 + 4):
                            kt = k1t if f % 2 == 0 else k2t
                            P.mm(bk[0:n, (f - f0) * 128:(f - f0 + 1) * 128], qT[:, f, 0:n], kt[:, f // 2, :])
                        evac(scs[0:n, f0:f0 + 4, :], bk[0:n, :].re("p (j t) -> p j t", t=128))
                    for f in range(16):
                        P.op("dve", lambda g, f=f: g.max(v16.t[0:n, f, 0:8], scs.t[0:n, f, :]), ins=[scs[:, :, :]], outs=[v16[:, :, :]])
                        P.op("dve", lambda g, f=f: g.max_index(i16.t[0:n, f, 0:8], v16.t[0:n, f, 0:8], scs.t[0:n, f, :]),
                             ins=[scs[:, :, :], v16[:, :, :]], outs=[i16[:, :, :]])
                        P.op("dve", lambda g, f=f: g.match_replace(sc2.t[0:n, :], v16.t[0:n, f, 0:8], scs.t[0:n, f, :], NEG),
                             ins=[scs[:, :, :], v16[:, :, :]], outs=[sc2[:, :]])
                        P.op("dve", lambda g, f=f: g.max(v16.t[0:n, f, 8:16], sc2.t[0:n, :]), ins=[sc2[:, :]], outs=[v16[:, :, :]])
                        P.op("dve", lambda g, f=f: g.max_index(i16.t[0:n, f, 8:16], v16.t[0:n, f, 8:16], sc2.t[0:n, :]),
                             ins=[sc2[:, :], v16[:, :, :]], outs=[i16[:, :, :]])
                    P.copy(i16f[0:n, :, :], i16[0:n, :, :])
                    v4 = v16[0:n, :, :].re("p (h two) k -> p h two k", two=2)
                    i4 = i16f[0:n, :, :].re("p (h two) k -> p h two k", two=2)
                    P.ts(i4[:, :, 0, :], i4[:, :, 0, :], 128.0, None, ALU.mult)
                    c4 = cand[0:n, :, :].re("p h (i j) -> p h i j", j=16)
                    P.tt(c4, v4[:, :, 0, :].re("p h (i o) -> p h i o", o=1).bc([n, 8, 16, 16]),
                         v4[:, :, 1, :].re("p h (o j) -> p h o j", o=1).bc([n, 8, 16, 16]), ALU.add)
                    for h in range(8):
                        P.op("dve", lambda g, h=h: g.max(sv.t[0:n, h, 0:8], cand.t[0:n, h, :]), ins=[cand[:, :, :]], outs=[sv[:, :, :]])
                        P.op("dve", lambda g, h=h: g.max_index(pos.t[0:n, h, 0:8], sv.t[0:n, h, 0:8], cand.t[0:n, h, :]),
                             ins=[cand[:, :, :], sv[:, :, :]], outs=[pos[:, :, :]])
                        P.op("dve", lambda g, h=h: g.match_replace(cand2.t[0:n, :], sv.t[0:n, h, 0:8], cand.t[0:n, h, :], NEG),
                             ins=[cand[:, :, :], sv[:, :, :]], outs=[cand2[:, :]])
                        P.op("dve", lambda g, h=h: g.max(sv.t[0:n, h, 8:16], cand2.t[0:n, :]), ins=[cand2[:, :]], outs=[sv[:, :, :]])
                        P.op("dve", lambda g, h=h: g.max_index(pos.t[0:n, h, 8:16], sv.t[0:n, h, 8:16], cand2.t[0:n, :]),
                             ins=[cand2[:, :], sv[:, :, :]], outs=[pos[:, :, :]])
                    P.copy(posf[0:n, :, :], pos[0:n, :, :])
                    P.ts(ikf[0:n, :, :], posf[0:n, :, :], -7.5, 1.0 / 16, ALU.add, ALU.mult)
                    P.copy(iki[0:n, :, :], ikf[0:n, :, :])
                    P.copy(ikf[0:n, :, :], iki[0:n, :, :])
                    P.stt(jkf[0:n, :, :], ikf[0:n, :, :], -16.0, posf[0:n, :, :], ALU.mult, ALU.add)
                    b1 = big1[0:n, :, :].re("p h (k i) -> p h k i", i=16)
                    b2 = big2[0:n, :, :].re("p h (k i) -> p h k i", i=16)
                    io4 = io16[0:n, :].re("p (a b i) -> p a b i", a=1, b=1).bc([n, 8, 16, 16])
                    for (kf, tab, eo) in ((ikf, 0, e1), (jkf, 1, e2)):
                        P.tt(b1, kf[0:n, :, :].re("p h (k o) -> p h k o", o=1).bc([n, 8, 16, 16]), io4, ALU.is_equal)
                        P.tt(b2, b1, i4[:, :, tab, :].re("p h (o i) -> p h o i", o=1).bc([n, 8, 16, 16]), ALU.mult, e="pool")
                        P.red(eo[0:n, :, :], b2, ALU.add)
                    P.stt(e1[0:n, :, :], e1[0:n, :, :], float(layer * nexp), e2[0:n, :, :], ALU.add, ALU.add)
                    P.tt(gat[0:n, :, :], sv[0:n, :, :], sv[0:n, :, 0:1].bc([n, 8, 16]), ALU.subtract)
                    P.act(gat[0:n, :, :], gat[0:n, :, :], AF.Exp)
                    P.red(gsum[0:n, :], gat[0:n, :, :], ALU.add)
                    P.recip(gsum[0:n, :], gsum[0:n, :])
                    P.tt(gat[0:n, :, :], gat[0:n, :, :], gsum[0:n, :].re("p (h o) -> p h o", o=1).bc([n, 8, 16]), ALU.mult)
                    bk = pbank()
                    P.tr(bk[:, 0:n], e1[0:n, :, :].re("p h k -> p (h k)"), identf[0:n, 0:n])
                    P.tr(bk[:, 128:128 + n], gat[0:n, :, :].re("p h k -> p (h k)"), identf[0:n, 0:n])
                    evac(idxT[:, 0:n], bk[:, 0:n], e="dve")
                    evac(gateT[:, 0:n], bk[:, 128:128 + n], e="act")
                    for t in range(n):
                        Us = U_r()
                        P.dma("pool", Us[:, :], D["peer_u"][layer], extra_ins=[idxT[:, :]],
                              fn=lambda g, Us=Us, t=t: g.indirect_dma_start(
                                  out=Us.t[:, :], out_offset=None, in_=utab,
                                  in_offset=bass.IndirectOffsetOnAxis(ap=idxT.t[:, t:t + 1], axis=0)))
                        px = [pbank(), pbank()]
                        for hh in range(2):
                            P.mm(px[hh][:, :], identb[0:n, t:t + 1].bc([n, 128]), xb[0:n, hh * 512:(hh + 1) * 512])
                        jk = junk_r()
                        P.stt(jk[:, 0:512], Us[:, 0:512], 1.0, px[0][:, :], ALU.mult, ALU.mult, accum=hTall[:, t:t + 1])
                        P.stt(jk[:, 512:1024], Us[:, 512:1024], 1.0, px[1][:, :], ALU.mult, ALU.mult, accum=actT[:, t:t + 1])
                    P.tt(hTall[:, 0:n], hTall[:, 0:n], actT[:, 0:n], ALU.add)
                    P.act(actT[:, 0:n], hTall[:, 0:n], AF.Gelu_apprx_tanh)
                    P.tt(actT[:, 0:n], actT[:, 0:n], gateT[:, 0:n], ALU.mult)
                    po = [banks[6], banks[7]]
                    for t in range(n):
                        Vs = V_r()
                        P.dma("pool", Vs[:, :], D["peer_v"][layer], extra_ins=[idxT[:, :]],
                              fn=lambda g, Vs=Vs, t=t: g.indirect_dma_start(
                                  out=Vs.t[:, :], out_offset=None, in_=vtab,
                                  in_offset=bass.IndirectOffsetOnAxis(ap=idxT.t[:, t:t + 1], axis=0)))
                        lh = lhs_r()
                        P.ts(lh[:, :], b255[:, 127 - t:255 - t], actT[:, t:t + 1], None, ALU.mult, e="pool")
                        for hh in range(2):
                            P.mm(po[hh][:, :], lh[:, :], Vs[:, hh * 512:(hh + 1) * 512], start=(t == 0), stop=(t == n - 1))
                    xo = layernorm(lnr, xt, [po[0][0:n, :], po[1][0:n, :]], n, lng, lnb, None)
                    if last:
                        if ti < 16:
                            P.dma("sp", D["y_p"][ti * 128:(ti + 1) * 128, :], xo[0:n, :], is_output=True)
                        else:
                            P.dma("sp", D["y_s"][:, :], xo[0:n, :], is_output=True)
                    else:
                        P.dma("sp", xd[ti][0:n, :], xo[0:n, :])
            barrier()

        def peer_phase_dense(layer):
            last = layer == 3
            TL = [0, 16] if (debug is not None and debug.startswith("peerfirst") and len(debug) > 9) else list(range(NTILES))
            NTL = len(TL)
            NCOL = NTL * 128
            with ExitStack() as sc:
                lng = sbuf(sc, "p_lng", [128, 1024]); lnb = sbuf(sc, "p_lnb", [128, 1024])
                P.dma("sp", lng[:, :], bcast_rows(D["ln2_g"], layer, 1024))
                P.dma("sp", lnb[:, :], bcast_rows(D["ln2_b"], layer, 1024))
                xT_all = sbuf(sc, "p_xTall", [128, 8, NCOL], BF16)
                P.memset(xT_all[:, :, (NTL - 1) * 128:NCOL], 0.0, e="pool")
                sa = ExitStack()
                sc.callback(sa.close)
                wq = sbuf(sa, "p_wq", [128, 8, 2048], BF16)
                k1t = sbuf(sa, "p_k1t", [128, 8, 128], BF16); k2t = sbuf(sa, "p_k2t", [128, 8, 128], BF16)
                io16 = sbuf(sa, "p_io16", [128, 16])
                io128 = sbuf(sa, "p_io128", [128, 128]); ioc128 = sbuf(sa, "p_ioc128", [128, 128])
                P.dma("pool", wq[:, :, :], D["w_q"][layer].re("(k p) n -> p k n", p=128))
                P.dma("pool", k1t[:, :, :], D["k1t"][layer].re("d (h k) -> d h k", h=8))
                P.dma("pool", k2t[:, :, :], D["k2t"][layer].re("d (h k) -> d h k", h=8))
                P.dma("sp", io16[:, :], D["c_iota16"][:, :])
                P.dma("sp", io128[:, :], D["c_iota128"][:, :])
                P.ts(ioc128[:, :], io128[:, :], 128.0, None, ALU.mult)
                x_r = ring(sa, "p_x", [128, 1024], F32, 2)
                xb_r = ring(sa, "p_xb", [128, 1024], BF16, 2)
                sc2 = sbuf(sa, "p_sc2", [128, 128])
                v16 = sbuf(sa, "p_v16", [128, 16, 16]); i16 = sbuf(sa, "p_i16", [128, 16, 16], U32); i16f = sbuf(sa, "p_i16f", [128, 16, 16])
                cand = sbuf(sa, "p_cand", [128, 8, 256]); cand2 = sbuf(sa, "p_cand2", [128, 256])
                big1 = sbuf(sa, "p_big1", [128, 8, 256]); big2 = sbuf(sa, "p_big2", [128, 8, 256])
                sv = sbuf(sa, "p_sv", [128, 8, 16]); pos = sbuf(sa, "p_pos", [128, 8, 16], U32)
                posf = sbuf(sa, "p_posf", [128, 8, 16]); ikf = sbuf(sa, "p_ikf", [128, 8, 16]); jkf = sbuf(sa, "p_jkf", [128, 8, 16])
                iki = sbuf(sa, "p_iki", [128, 8, 16], I32)
                e1 = sbuf(sa, "p_e1", [128, 8, 16]); e2 = sbuf(sa, "p_e2", [128, 8, 16])
                gat = sbuf(sa, "p_gat", [128, 8, 16]); gsum = sbuf(sa, "p_gsum", [128, 8])
                slotT = sbuf(sa, "p_slotT", [128, 3, 128])
                nslot = sbuf(sa, "p_nslot", [128, 128])
                ab_r = ring(sa, "p_ab", [128, 128], F32, 12)
                At_r = ring(sa, "p_At", [128, 128], BF16, 12)
                Bt_r = ring(sa, "p_Bt", [128, 128], BF16, 12)
                WGt_r = ring(sa, "p_WGt", [128, 128, 128], BF16, 1)
                WG = P.dram(f"wg_scratch_{layer}", [NTL, 128, 128, 128], BF16)

                qT_r2 = ring(sa, "p_qT2", [128, 16, 128], BF16, 2)
                scs_r2 = ring(sa, "p_sc2b", [128, 16, 128], F32, 2)

                def front(tix):
                    ti = TL[tix]
                    n = 128 if ti < 16 else 16
                    qT = qT_r2(); scs = scs_r2()
                    xt = x_r()
                    P.dma("sp", xt[0:n, :], xsrc(debug is not None and debug.startswith("peerfirst"), ti))
                    xb = xb_r()
                    P.copy(xb[0:n, :], xt[0:n, :], e="pool")
                    xT = xT_all[:, :, tix * 128:(tix + 1) * 128]
                    transposes(xT, [xb[0:n, k * 128:(k + 1) * 128] for k in range(8)], n, BF16)
                    for f0 in range(0, 16, 4):
                        bk = pbank()
                        for f in range(f0, f0 + 4):
                            for k in range(8):
                                P.mm(bk[:, (f - f0) * 128:(f - f0) * 128 + n], wq[:, k, f * 128:(f + 1) * 128], xT[:, k, 0:n],
                                     start=(k == 0), stop=(k == 7))
                        evac(qT[:, f0:f0 + 4, 0:n], bk[:, :].re("p (j t) -> p j t", t=128)[:, :, 0:n])
                    for f0 in range(0, 16, 4):
                        bk = pbank()
                        for f in range(f0, f0 + 4):
                            kt = k1t if f % 2 == 0 else k2t
                            P.mm(bk[0:n, (f - f0) * 128:(f - f0 + 1) * 128], qT[:, f, 0:n], kt[:, f // 2, :])
                        evac(scs[0:n, f0:f0 + 4, :], bk[0:n, :].re("p (j t) -> p j t", t=128))
                    return dict(n=n, scs=scs, tix=tix)

                def mid(cx):
                    n, scs, tix = cx["n"], cx["scs"], cx["tix"]
                    for f in range(16):
                        P.op("dve", lambda g, f=f: g.max(v16.t[0:n, f, 0:8], scs.t[0:n, f, :]), ins=[scs[:, :, :]], outs=[v16[:, :, :]])
                        P.op("dve", lambda g, f=f: g.max_index(i16.t[0:n, f, 0:8], v16.t[0:n, f, 0:8], scs.t[0:n, f, :]),
                             ins=[scs[:, :, :], v16[:, :, :]], outs=[i16[:, :, :]])
                        P.op("dve", lambda g, f=f: g.match_replace(sc2.t[0:n, :], v16.t[0:n, f, 0:8], scs.t[0:n, f, :], NEG),
                             ins=[scs[:, :, :], v16[:, :, :]], outs=[sc2[:, :]])
                        P.op("dve", lambda g, f=f: g.max(v16.t[0:n, f, 8:16], sc2.t[0:n, :]), ins=[sc2[:, :]], outs=[v16[:, :, :]])
                        P.op("dve", lambda g, f=f: g.max_index(i16.t[0:n, f, 8:16], v16.t[0:n, f, 8:16], sc2.t[0:n, :]),
                             ins=[sc2[:, :], v16[:, :, :]], outs=[i16[:, :, :]])
                    P.copy(i16f[0:n, :, :], i16[0:n, :, :])
                    v4 = v16[0:n, :, :].re("p (h two) k -> p h two k", two=2)
                    i4 = i16f[0:n, :, :].re("p (h two) k -> p h two k", two=2)
                    c4 = cand[0:n, :, :].re("p h (i j) -> p h i j", j=16)
                    P.tt(c4, v4[:, :, 0, :].re("p h (i o) -> p h i o", o=1).bc([n, 8, 16, 16]),
                         v4[:, :, 1, :].re("p h (o j) -> p h o j", o=1).bc([n, 8, 16, 16]), ALU.add)
                    for h in range(8):
                        P.op("dve", lambda g, h=h: g.max(sv.t[0:n, h, 0:8], cand.t[0:n, h, :]), ins=[cand[:, :, :]], outs=[sv[:, :, :]])
                        P.op("dve", lambda g, h=h: g.max_index(pos.t[0:n, h, 0:8], sv.t[0:n, h, 0:8], cand.t[0:n, h, :]),
                             ins=[cand[:, :, :], sv[:, :, :]], outs=[pos[:, :, :]])
                        P.op("dve", lambda g, h=h: g.match_replace(cand2.t[0:n, :], sv.t[0:n, h, 0:8], cand.t[0:n, h, :], NEG),
                             ins=[cand[:, :, :], sv[:, :, :]], outs=[cand2[:, :]])
                        P.op("dve", lambda g, h=h: g.max(sv.t[0:n, h, 8:16], cand2.t[0:n, :]), ins=[cand2[:, :]], outs=[sv[:, :, :]])
                        P.op("dve", lambda g, h=h: g.max_index(pos.t[0:n, h, 8:16], sv.t[0:n, h, 8:16], cand2.t[0:n, :]),
                             ins=[cand2[:, :], sv[:, :, :]], outs=[pos[:, :, :]])
                    P.copy(posf[0:n, :, :], pos[0:n, :, :])
                    P.ts(ikf[0:n, :, :], posf[0:n, :, :], -7.5, 1.0 / 16, ALU.add, ALU.mult)
                    P.copy(iki[0:n, :, :], ikf[0:n, :, :])
                    P.copy(ikf[0:n, :, :], iki[0:n, :, :])
                    P.stt(jkf[0:n, :, :], ikf[0:n, :, :], -16.0, posf[0:n, :, :], ALU.mult, ALU.add)
                    b1 = big1[0:n, :, :].re("p h (k i) -> p h k i", i=16)
                    b2 = big2[0:n, :, :].re("p h (k i) -> p h k i", i=16)
                    io4 = io16[0:n, :].re("p (a b i) -> p a b i", a=1, b=1).bc([n, 8, 16, 16])
                    for (kf, tab, eo) in ((ikf, 0, e1), (jkf, 1, e2)):
                        P.tt(b1, kf[0:n, :, :].re("p h (k o) -> p h k o", o=1).bc([n, 8, 16, 16]), io4, ALU.is_equal)
                        P.tt(b2, b1, i4[:, :, tab, :].re("p h (o i) -> p h o i", o=1).bc([n, 8, 16, 16]), ALU.mult, e="pool")
                        P.red(eo[0:n, :, :], b2, ALU.add)
                    P.tt(gat[0:n, :, :], sv[0:n, :, :], sv[0:n, :, 0:1].bc([n, 8, 16]), ALU.subtract)
                    P.act(gat[0:n, :, :], gat[0:n, :, :], AF.Exp)
                    P.red(gsum[0:n, :], gat[0:n, :, :], ALU.add)
                    P.recip(gsum[0:n, :], gsum[0:n, :])
                    P.tt(gat[0:n, :, :], gat[0:n, :, :], gsum[0:n, :].re("p (h o) -> p h o", o=1).bc([n, 8, 16]), ALU.mult)

                def back(cx):
                    n, tix = cx["n"], cx["tix"]
                    bk = pbank()
                    P.tr(bk[:, 0:n], e1[0:n, :, :].re("p h k -> p (h k)"), identf[0:n, 0:n])
                    P.tr(bk[:, 128:128 + n], e2[0:n, :, :].re("p h k -> p (h k)"), identf[0:n, 0:n])
                    P.tr(bk[:, 256:256 + n], gat[0:n, :, :].re("p h k -> p (h k)"), identf[0:n, 0:n])
                    evac(slotT[:, :, 0:n], bk[:, 0:384].re("p (j t) -> p j t", t=128)[:, :, 0:n], e="act")
                    P.ts(nslot[:, 0:n], slotT[:, 1, 0:n], -1.0, None, ALU.mult)
                    WGt = WGt_r()
                    if n < 128:
                        P.memset(WGt[:, :, :], 0.0, e="pool")
                    for t0 in range(0, n, 4):
                        pG = pbank()
                        for t in range(t0, t0 + 4):
                            At = At_r(); Bt = Bt_r()
                            P.ts(At[:, :], io128[:, :], slotT[:, 0, t:t + 1], slotT[:, 2, t:t + 1], ALU.is_equal, ALU.mult)
                            if t % 3 == 2:
                                P.ts(Bt[:, :], io128[:, :], slotT[:, 1, t:t + 1], None, ALU.is_equal)
                            else:
                                ab = ab_r()
                                P.act(ab[:, :], io128[:, :], AF.Abs, bias=nslot[:, t:t + 1])
                                P.act(Bt[:, :], ab[:, :], AF.Relu, scale=-1.0, bias=1.0)
                            P.mm(pG[:, (t - t0) * 128:(t - t0 + 1) * 128], Bt[:, :], At[:, :])
                        evac(WGt[:, :, t0:t0 + 4], pG[:, :].re("p (t c) -> p c t", c=128))
                    P.dma("sp", WG[tix], WGt[:, :, :])

                cxs = {0: front(0)}
                for tix in range(NTL):
                    if tix + 1 < NTL:
                        cxs[tix + 1] = front(tix + 1)
                    mid(cxs[tix])
                    back(cxs[tix])
                barrier()
                sa.close()

                G = 4
                acc = [sbuf(sc, f"p_acc{i}", [128, 1024]) for i in range(NTL)]
                UT_r = ring(sc, "p_UT", [128, 8, G * 128], BF16, 2)
                Vg_r = ring(sc, "p_Vg", [128, G, 1024], BF16, 2)
                WGg_r = ring(sc, "p_WGg", [128, NTL, G, 128], BF16, 1)
                gb_r = ring(sc, "p_gb", [128, NCOL], BF16, 2)
                Ag_r = ring(sc, "p_Ag", [128, NCOL], BF16, G + 1)
                x_r = ring(sc, "p_x2", [128, 1024], F32, 2)
                lnr = {"ln_s": ring(sc, "p_lns", [128, 1024], F32, 1), "ln_st": ring(sc, "p_lnst", [128, 12], F32, 2),
                       "ln_mv": ring(sc, "p_lnmv", [128, 8], F32, 2)}
                ut_l = D["peer_ut"][layer].re("(k p) e -> p k e", p=128)
                v_l = D["peer_v"][layer]
                nblk = [(c0, min(512, NCOL - c0)) for c0 in range(0, NCOL, 512)]
                tblocks = [list(range(i, min(i + 3, NTL))) for i in range(0, NTL, 3)]
                for gi in range(128 // G):
                    c0 = gi * G
                    UTg = UT_r(); Vg = Vg_r(); WGg = WGg_r()
                    P.dma("pool", UTg[:, :, :], ut_l[:, :, c0 * 128:(c0 + G) * 128])
                    P.dma("pool", Vg[:, :, :], v_l[c0 * 128:(c0 + G) * 128, :].re("(c e) d -> e c d", e=128))
                    P.dma("sp", WGg[:, :, :, :], WG[:, :, c0:c0 + G, :].re("i e c t -> e i c t"))
                    Ags = []
                    for ci in range(G):
                        gb = gb_r()
                        for bi, (n0, nw) in enumerate(nblk):
                            ph = banks[6 + bi % 2]
                            for k in range(8):
                                P.mm(ph[:, 0:nw], UTg[:, k, ci * 128:(ci + 1) * 128], xT_all[:, k, n0:n0 + nw], start=(k == 0), stop=(k == 7))
                            P.act(gb[:, n0:n0 + nw], ph[:, 0:nw], AF.Gelu_apprx_tanh)
                        Ag = Ag_r()
                        P.tt(Ag[:, :].re("p (i t) -> p i t", t=128), gb[:, :].re("p (i t) -> p i t", t=128), WGg[:, :, ci, :], ALU.mult, e="pool")
                        Ags.append(Ag)
                    for tb in tblocks:
                        for j, tix in enumerate(tb):
                            for hh in range(2):
                                for ci in range(G):
                                    P.mm(banks[j * 2 + hh][:, :], Ags[ci][:, tix * 128:(tix + 1) * 128], Vg[:, ci, hh * 512:(hh + 1) * 512],
                                         start=(ci == 0), stop=(ci == G - 1))
                        for j, tix in enumerate(tb):
                            for hh in range(2):
                                hs = slice(hh * 512, (hh + 1) * 512)
                                if gi == 0:
                                    evac(acc[tix][:, hs], banks[j * 2 + hh][:, :])
                                else:
                                    P.tt(acc[tix][:, hs], acc[tix][:, hs], banks[j * 2 + hh][:, :], ALU.add)
                for tix, ti in enumerate(TL):
                    n = 128 if ti < 16 else 16
                    xt = x_r()
                    P.dma("sp", xt[0:n, :], xsrc(debug is not None and debug.startswith("peerfirst"), ti))
                    xo = layernorm(lnr, xt, [acc[tix][0:n, 0:512], acc[tix][0:n, 512:1024]], n, lng, lnb, None)
                    if last:
                        if ti < 16:
                            P.dma("sp", D["y_p"][ti * 128:(ti + 1) * 128, :], xo[0:n, :], is_output=True)
                        else:
                            P.dma("sp", D["y_s"][:, :], xo[0:n, :], is_output=True)
                    else:
                        P.dma("sp", xd[ti][0:n, :], xo[0:n, :])
            barrier()

        def odd_phase(layer):
            li = layer // 2
            NB = 2
            TB = NB * 128
            win = D["w_in_o"][li]

            with ExitStack() as sc:
                w_out = sbuf(sc, "o_wout", [128, 16, 1024], BF16)
                cw = sbuf(sc, "o_cw", [128, 96]); cb = sbuf(sc, "o_cb", [128, 24])
                dtb = sbuf(sc, "o_dtb", [128, 32]); aneg = sbuf(sc, "o_aneg", [128, 32]); dsk = sbuf(sc, "o_dsk", [128, 32])
                normw = sbuf(sc, "o_normw", [128, 2048])
                lng = sbuf(sc, "o_lng", [128, 1024]); lnb = sbuf(sc, "o_lnb", [128, 1024])
                maskneg = sbuf(sc, "o_mask", [128, 128])
                P.dma("pool", w_out[:, :, :], D["w_out_o"][li].re("(k p) n -> p k n", p=128))
                P.dma("sp", cw[:, :], D["conv_w"][li]); P.dma("sp", cb[:, :], D["conv_b"][li])
                P.dma("sp", dtb[:, :], bcast_rows(D["dt_bias"], li, 32))
                P.dma("sp", aneg[:, :], bcast_rows(D["a_log"], li, 32))
                P.act(aneg[:, :], aneg[:, :], AF.Exp)
                P.ts(aneg[:, :], aneg[:, :], -1.0, None, ALU.mult)
                P.dma("sp", dsk[:, :], bcast_rows(D["ssd_d"], li, 32))
                P.dma("sp", normw[:, :], bcast_rows(D["norm_w"], li, 2048))
                P.dma("sp", lng[:, :], bcast_rows(D["ln1_g"], layer, 1024))
                P.dma("sp", lnb[:, :], bcast_rows(D["ln1_b"], layer, 1024))
                P.dma("sp", maskneg[:, :], D["c_maskneg"][:, :])
                Wb_r = ring(sc, "o_wb", [128, 8, 512], BF16, 2)
                dg_r = ring(sc, "o_dg", [128, 4, 128], BF16, 4)
                lnr = {"ln_s": ring(sc, "o_lns", [128, 1024], F32, 1), "ln_st": ring(sc, "o_lnst", [128, 12], F32, 2),
                       "ln_mv": ring(sc, "o_lnmv", [128, 8], F32, 2)}
                x_r = ring(sc, "o_x", [128, 1024], F32, 2)
                xb_r = ring(sc, "o_xb", [128, 1024], BF16, 2)
                sm_r = ring(sc, "o_sm", [128, 32], F32, 12)
                y_r = ring(sc, "o_y", [128, 2048], F32, 2)
                yb_r = ring(sc, "o_yb", [128, 2048], BF16, 1)
                ynT_r = ring(sc, "o_ynT", [128, 16, 128], BF16, 1)

                wbf = P.dram(f"wino_bf16_{layer}", [1024, 5152], BF16)
                with ExitStack() as s0:
                    stg_r = ring(s0, "o_stg", [128, 8, 512], F32, 2)
                    stb_r = ring(s0, "o_stb", [128, 8, 512], BF16, 2)
                    for blk in range(11):
                        c0 = blk * 512
                        cwid = min(512, 5152 - c0)
                        stg = stg_r(); stb = stb_r()
                        P.dma("sp", stg[:, :, 0:cwid], win[:, c0:c0 + cwid].re("(k p) n -> p k n", p=128))
                        P.copy(stb[:, :, 0:cwid], stg[:, :, 0:cwid], e=("act" if blk % 2 == 0 else "dve"))
                        P.dma("sp", wbf[:, c0:c0 + cwid].re("(k p) n -> p k n", p=128), stb[:, :, 0:cwid])
                    barrier()

                def load_w(c0, cwid):
                    Wb = Wb_r()
                    P.dma("sp", Wb[:, :, 0:cwid], wbf[:, c0:c0 + cwid].re("(k p) n -> p k n", p=128))
                    return Wb

                def softplus_dt(pdt, n):
                    a, b_, c_, d_ = sm_r(), sm_r(), sm_r(), sm_r()
                    P.tt(a[0:n, :], pdt, dtb[0:n, :], ALU.add)
                    P.stt(b_[0:n, :], a[0:n, :], -1.0, a[0:n, :], ALU.mult, ALU.max)
                    P.act(c_[0:n, :], b_[0:n, :], AF.Exp, scale=-1.0)
                    P.act(c_[0:n, :], c_[0:n, :], AF.Ln, bias=1.0)
                    P.stt(d_[0:n, :], a[0:n, :], 0.0, c_[0:n, :], ALU.max, ALU.add)
                    return d_

                def make_dg(ct):
                    dg = dg_r()
                    for tap in range(4):
                        P.ts(dg[:, tap, :], identb[:, :], cw[:, ct * 4 + tap:ct * 4 + tap + 1], None, ALU.mult)
                    return dg

                def gate_norm_out(ti, n, y, xs_view, zs_view, xt):
                    t = y_r()
                    P.tt(t[0:n, :].re("p (h q) -> p h q", q=64), xs_view, dsk[0:n, :].re("p (h o) -> p h o", o=1).bc([n, 32, 64]),
                         ALU.mult, e="pool")
                    P.tt(y[0:n, :], y[0:n, :], t[0:n, :], ALU.add, e="pool")
                    P.tt(y[0:n, :], y[0:n, :], zs_view, ALU.mult)
                    ss = sm_r()
                    P.act(t[0:n, :], y[0:n, :], AF.Square, accum=ss[0:n, 0:1])
                    P.ts(ss[0:n, 1:2], ss[0:n, 0:1], 1.0 / 2048, 1e-5, ALU.mult, ALU.add)
                    P.act(ss[0:n, 2:3], ss[0:n, 1:2], AF.Sqrt)
                    P.recip(ss[0:n, 3:4], ss[0:n, 2:3])
                    yb = yb_r()
                    P.stt(yb[0:n, :], y[0:n, :], ss[0:n, 3:4], normw[0:n, :], ALU.mult, ALU.mult)
                    ynT = ynT_r()
                    transposes(ynT[:, :, :], [yb[0:n, k * 128:(k + 1) * 128] for k in range(16)], n, BF16)
                    po = [banks[6], banks[7]]
                    for hh in range(2):
                        for k in range(16):
                            P.mm(po[hh][0:n, :], ynT[:, k, 0:n], w_out[:, k, hh * 512:(hh + 1) * 512], start=(k == 0), stop=(k == 15))
                    xo = layernorm(lnr, xt, [po[0][0:n, :], po[1][0:n, :]], n, lng, lnb, None)
                    P.dma("sp", xd[ti][0:n, :], xo[0:n, :])

                with ExitStack() as s3:
                    xT = sbuf(s3, "o_xT", [128, 8, TB], BF16)
                    xbc_r = ring(s3, "o_xbc", [128, 3 + TB], BF16, 4)
                    halo = sbuf(s3, "o_halo", [128, 24, 3], BF16)
                    P.memset(halo[:, :, :], 0.0)
                    xaT = sbuf(s3, "o_xaT", [128, 24, TB], BF16)
                    zs = sbuf(s3, "o_zs", [128, NB, 2048], BF16)
                    raw3 = sbuf(s3, "o_raw3", [128, 24, 3])
                    dts = [sbuf(s3, f"o_dt{j}", [128, 32]) for j in range(NB)]
                    ST = [sbuf(s3, f"o_ST{g}", [128, 512]) for g in range(4)]
                    STb = [sbuf(s3, f"o_STb{g}", [128, 512], BF16) for g in range(4)]
                    xs_r = ring(s3, "o_xs", [128, 2048], BF16, 2)
                    Bt_r = ring(s3, "o_Bt", [128, 512], BF16, 2)
                    dxb_r = ring(s3, "o_dxb", [128, 2048], BF16, 2)
                    dxs_r = ring(s3, "o_dxs", [128, 2048], BF16, 2)
                    cbT_r = ring(s3, "o_cbT", [128, 128], F32, 2)
                    dAb_r = ring(s3, "o_dAb", [128, 128], F32, 6)
                    seg_r = ring(s3, "o_seg", [128, 128], F32, 6)
                    L_r = ring(s3, "o_L", [128, 128], F32, 6)
                    MT_r = ring(s3, "o_MT", [128, 128], BF16, 6)
                    t1_r = ring(s3, "o_t1", [128, 512], F32, 2)

                    def ssd_tile(ti, j, xt):
                        tsl = slice(j * 128, (j + 1) * 128)
                        dt = dts[j]
                        xs = xs_r()
                        transposes(xs[:, :].re("t (c q) -> t c q", q=128), [xaT[:, ct, tsl] for ct in range(16)], 128, BF16)
                        Bt = Bt_r()
                        transposes(Bt[:, :].re("t (c q) -> t c q", q=128), [xaT[:, 16 + g, tsl] for g in range(4)], 128, BF16)
                        dA, acs, nacs, eacs, eal, decs = sm_r(), sm_r(), sm_r(), sm_r(), sm_r(), sm_r()
                        P.tt(dA[:, :], dt[:, :], aneg[:, :], ALU.mult)
                        pa = pbank()
                        P.mm(pa[:, 0:32], triT[:, :], dA[:, :])
                        evac(acs[:, :], pa[:, 0:32], e="dve")
                        P.ts(nacs[:, :], acs[:, :], -1.0, None, ALU.mult)
                        P.act(eacs[:, :], acs[:, :], AF.Exp)
                        pl = pbank()
                        P.mm(pl[:, 0:32], e127[:, :], acs[:, :])
                        P.act(eal[:, :], pl[:, 0:32], AF.Exp)
                        P.tt(decs[:, :], pl[:, 0:32], acs[:, :], ALU.subtract)
                        P.act(decs[:, :], decs[:, :], AF.Exp)
                        xs3 = xs[:, :].re("p (h q) -> p h q", q=64)
                        dxb = dxb_r(); dxs = dxs_r()
                        P.tt(dxb[:, :].re("p (h q) -> p h q", q=64), xs3, dt[:, :].re("p (h o) -> p h o", o=1).bc([128, 32, 64]), ALU.mult)
                        P.tt(dxs[:, :].re("p (h q) -> p h q", q=64), dxb[:, :].re("p (h q) -> p h q", q=64),
                             decs[:, :].re("p (h o) -> p h o", o=1).bc([128, 32, 64]), ALU.mult, e="pool")
                        y = y_r()
                        for g in range(4):
                            pcb = pbank()
                            P.mm(pcb[:, 0:128], xaT[:, 16 + g, tsl], xaT[:, 20 + g, tsl])
                            cbT = cbT_r()
                            evac(cbT[:, :], pcb[:, 0:128])
                            pyd, pyo = banks[6], banks[7]
                            for hl in range(8):
                                h = g * 8 + hl
                                dAb = dAb_r()
                                P.copy(dAb[:, :], dA[:, h:h + 1].bc([128, 128]), e="pool")
                                pbc = pbank()
                                P.mm(pbc[:, 0:128], dAb[:, :], triT[:, :])
                                seg = seg_r()
                                P.tt(seg[:, :], pbc[:, 0:128], maskneg[:, :], ALU.add)
                                L = L_r()
                                P.act(L[:, :], seg[:, :], AF.Exp, bias=nacs[:, h:h + 1])
                                MT = MT_r()
                                P.tt(MT[:, :], L[:, :], cbT[:, :], ALU.mult, e="pool")
                                P.mm(pyd[:, hl * 64:(hl + 1) * 64], MT[:, :], dxb[:, h * 64:(h + 1) * 64])
                            gs = slice(g * 512, (g + 1) * 512)
                            if ti > 0:
                                P.mm(pyo[:, :], xaT[:, 20 + g, tsl], STb[g][:, :])
                                t1 = t1_r()
                                P.tt(t1[:, :].re("p (h q) -> p h q", q=64), pyo[:, :].re("p (h q) -> p h q", q=64),
                                     eacs[:, g * 8:(g + 1) * 8].re("p (h o) -> p h o", o=1).bc([128, 8, 64]), ALU.mult)
                                P.tt(y[:, gs], t1[:, :], pyd[:, :], ALU.add)
                            else:
                                evac(y[:, gs], pyd[:, :])
                            pst = pbank()
                            P.mm(pst[:, :], Bt[:, g * 128:(g + 1) * 128], dxs[:, gs])
                            if ti > 0:
                                t2 = t1_r()
                                P.tt(t2[:, :].re("p (h q) -> p h q", q=64), ST[g][:, :].re("p (h q) -> p h q", q=64),
                                     eal[:, g * 8:(g + 1) * 8].re("p (h o) -> p h o", o=1).bc([128, 8, 64]), ALU.mult, e="pool")
                                P.tt(ST[g][:, :], t2[:, :], pst[:, :], ALU.add)
                            else:
                                evac(ST[g][:, :], pst[:, :], e="dve")
                            P.copy(STb[g][:, :], ST[g][:, :], e="pool")
                        gate_norm_out(ti, 128, y, xs3, zs[:, j, :], xt)

                    for bi in range(16 // NB):
                        tiles_ = [bi * NB + j for j in range(NB)]
                        xts = []
                        for j, ti in enumerate(tiles_):
                            xt = x_r()
                            P.dma("sp", xt[:, :], xsrc(debug == "oddfirst", ti))
                            xb = xb_r()
                            P.copy(xb[:, :], xt[:, :], e="pool")
                            transposes(xT[:, :, j * 128:(j + 1) * 128], [xb[:, k * 128:(k + 1) * 128] for k in range(8)], 128, BF16)
                            xts.append(xt)
                        for blk in range(4):
                            Wb = load_w(blk * 512, 512)
                            for j in range(NB):
                                pz = pbank()
                                for k in range(8):
                                    P.mm(pz[:, :], xT[:, k, j * 128:(j + 1) * 128], Wb[:, k, :], start=(k == 0), stop=(k == 7))
                                P.act(zs[:, j, blk * 512:(blk + 1) * 512], pz[:, :], AF.Silu)
                        Wd = load_w(5120, 32)
                        for j in range(NB):
                            pdt = pbank()
                            for k in range(8):
                                P.mm(pdt[:, 0:32], xT[:, k, j * 128:(j + 1) * 128], Wd[:, k, 0:32], start=(k == 0), stop=(k == 7))
                            d_ = softplus_dt(pdt[:, 0:32], 128)
                            P.copy(dts[j][:, :], d_[:, :])
                        for blk in range(4, 10):
                            Wb = load_w(blk * 512, 512)
                            for sub in range(4):
                                ct = (blk - 4) * 4 + sub
                                px = pbank()
                                for k in range(8):
                                    P.mm(px[:, 0:TB], Wb[:, k, sub * 128:(sub + 1) * 128], xT[:, k, :], start=(k == 0), stop=(k == 7))
                                xbc = xbc_r()
                                P.copy(xbc[:, 0:3], halo[:, ct, :], e="pool")
                                evac(xbc[:, 3:3 + TB], px[:, 0:TB])
                                if bi == 16 // NB - 1:
                                    P.copy(raw3[:, ct, :], px[:, TB - 3:TB])
                                P.copy(halo[:, ct, :], xbc[:, TB:TB + 3], e="pool")
                                dg = make_dg(ct)
                                pc = pbank()
                                for tap in range(4):
                                    P.mm(pc[:, 0:TB], dg[:, tap, :], xbc[:, tap:tap + TB], start=(tap == 0), stop=(tap == 3))
                                P.act(xaT[:, ct, :], pc[:, 0:TB], AF.Silu, bias=cb[:, ct:ct + 1])
                        for j, ti in enumerate(tiles_):
                            ssd_tile(ti, j, xts[j])
                    osb_r = ring(s3, "o_osb", [128, 4, 128], F32, 2)
                    for g in range(4):
                        osb = osb_r()
                        transposes(osb[:, :, :], [ST[g][:, k * 128:(k + 1) * 128] for k in range(4)], 128, F32)
                        P.dma("sp", D["p_ssd"][li, g * 512:(g + 1) * 512, :].re("(k p) n -> p k n", p=128), osb[:, :, :], is_output=True)
                    for g in range(6):
                        osb = osb_r()
                        transposes(osb[:, :, :], [raw3[:, ct, :] for ct in range(4 * g, 4 * g + 4)], 128, F32)
                        P.dma("sp", D["p_conv"][li][:, g * 512:(g + 1) * 512].re("r (c q) -> r c q", q=128), osb[0:3, :, :], is_output=True)
                barrier()

                with ExitStack() as s4:
                    n = 16
                    ti = 16
                    xT = sbuf(s4, "os_xT", [128, 8, 16], BF16)
                    zs = sbuf(s4, "os_zs", [128, 2048], BF16)
                    xaT = sbuf(s4, "os_xaT", [128, 24, 16], BF16)
                    hist_r = ring(s4, "os_hist", [48, 512], F32, 2)
                    histT_r = ring(s4, "os_histT", [128, 4, 48], BF16, 2)
                    raw_r = ring(s4, "os_raw", [16, 512], F32, 2)
                    xbs_r = ring(s4, "os_xbs", [128, 16], BF16, 2)
                    xs = sbuf(s4, "os_xs", [16, 3072], BF16)
                    xsf = sbuf(s4, "os_xsf", [16, 3072])
                    dtx = sbuf(s4, "os_dtx", [16, 2048]); eexp = sbuf(s4, "os_eexp", [16, 2048])
                    eT = sbuf(s4, "os_eT", [128, 16, 16]); dtxT = sbuf(s4, "os_dtxT", [128, 16, 16])
                    yT = sbuf(s4, "os_yT", [128, 16, 16])
                    e16 = sbuf(s4, "os_e16", [16, 16, 128])
                    P.dma("sp", e16[:, :, :], D["c_e16"][:, :].re("p (b m) -> p b m", m=128))
                    S0_r = ring(s4, "os_S0", [128, 16, 128], F32, 2)
                    Sn_r = ring(s4, "os_Sn", [128, 16, 128], F32, 1)
                    tt_r = ring(s4, "os_tt", [128, 128], F32, 3)
                    jk_r = ring(s4, "os_jk", [128, 128], F32, 2)
                    xt = x_r()
                    P.dma("sp", xt[0:n, :], xsrc(debug == "oddfirst", ti))
                    xb = xb_r()
                    P.copy(xb[0:n, :], xt[0:n, :], e="pool")
                    transposes(xT[:, :, :], [xb[0:n, k * 128:(k + 1) * 128] for k in range(8)], n, BF16)
                    for blk in range(4):
                        Wb = load_w(blk * 512, 512)
                        pz = pbank()
                        for k in range(8):
                            P.mm(pz[0:n, :], xT[:, k, 0:n], Wb[:, k, :], start=(k == 0), stop=(k == 7))
                        P.act(zs[0:n, blk * 512:(blk + 1) * 512], pz[0:n, :], AF.Silu)
                    Wd = load_w(5120, 32)
                    pdt = pbank()
                    for k in range(8):
                        P.mm(pdt[0:n, 0:32], xT[:, k, 0:n], Wd[:, k, 0:32], start=(k == 0), stop=(k == 7))
                    dt = softplus_dt(pdt[0:n, 0:32], n)
                    P.dma("sp", D["s_conv"][li, :, 0:2, :], D["st_conv"][li, :, 1:3, :], is_output=True)
                    for blk in range(4, 10):
                        Wb = load_w(blk * 512, 512)
                        c0 = (blk - 4) * 512
                        praw = pbank()
                        for k in range(8):
                            P.mm(praw[0:n, :], xT[:, k, 0:n], Wb[:, k, :], start=(k == 0), stop=(k == 7))
                        raw = raw_r()
                        evac(raw[0:n, :], praw[0:n, :])
                        P.dma("sp", D["s_conv"][li, :, 2, c0:c0 + 512], raw[0:n, :], is_output=True)
                        hist = hist_r()
                        P.dma("sp", hist[:, :], D["st_conv"][li].re("b r c -> (b r) c")[:, c0:c0 + 512])
                        histT = histT_r()
                        transposes(histT[:, :, :], [hist[0:48, s_ * 128:(s_ + 1) * 128] for s_ in range(4)], 48, F32)
                        for sub in range(4):
                            ct = (blk - 4) * 4 + sub
                            px = pbank()
                            for k in range(8):
                                P.mm(px[:, 0:n], Wb[:, k, sub * 128:(sub + 1) * 128], xT[:, k, 0:n], start=(k == 0), stop=(k == 7))
                            xbs = xbs_r()
                            evac(xbs[:, :], px[:, 0:n])
                            dg = make_dg(ct)
                            pc = pbank()
                            for tap in range(3):
                                P.mm(pc[:, 0:n], dg[:, tap, :], histT[:, sub, tap:48:3], start=(tap == 0), stop=False)
                            P.mm(pc[:, 0:n], dg[:, 3, :], xbs[:, :], start=False, stop=True)
                            P.act(xaT[:, ct, :], pc[:, 0:n], AF.Silu, bias=cb[:, ct:ct + 1])
                    transposes(xs[:, :].re("t (c q) -> t c q", q=128), [xaT[:, ct, :] for ct in range(24)], 128, BF16)
                    P.copy(xsf[0:n, :], xs[0:n, :])
                    dA, ee = sm_r(), sm_r()
                    P.tt(dA[0:n, :], dt[0:n, :], aneg[0:n, :], ALU.mult)
                    P.act(ee[0:n, :], dA[0:n, :], AF.Exp)
                    xs3 = xsf[0:n, 0:2048].re("p (h q) -> p h q", q=64)
                    P.tt(dtx[0:n, :].re("p (h q) -> p h q", q=64), xs3, dt[0:n, :].re("p (h o) -> p h o", o=1).bc([n, 32, 64]), ALU.mult)
                    P.copy(eexp[0:n, :].re("p (h q) -> p h q", q=64), ee[0:n, :].re("p (h o) -> p h o", o=1).bc([n, 32, 64]), e="pool")
                    transposes(eT[:, :, :], [eexp[0:n, k * 128:(k + 1) * 128] for k in range(16)], n, F32)
                    transposes(dtxT[:, :, :], [dtx[0:n, k * 128:(k + 1) * 128] for k in range(16)], n, F32)
                    for b in range(16):
                        pB, pC = pbank(), pbank()
                        P.mm(pB[:, :], e16[:, b, :], xsf[0:n, 2048:2560])
                        P.mm(pC[:, :], e16[:, b, :], xsf[0:n, 2560:3072])
                        S0 = S0_r(); Sn = Sn_r()
                        P.dma("sp", S0[:, :, :], D["st_ssd"][li, b].re("(j q) n -> q j n", q=128))
                        for j in range(16):
                            g = j // 4
                            t1 = tt_r()
                            P.act(t1[:, :], S0[:, j, :], AF.Copy, scale=eT[:, j, b:b + 1])
                            P.stt(Sn[:, j, :], pB[:, g * 128:(g + 1) * 128], dtxT[:, j, b:b + 1], t1[:, :], ALU.mult, ALU.add)
                            jk = jk_r()
                            P.stt(jk[:, :], Sn[:, j, :], 1.0, pC[:, g * 128:(g + 1) * 128], ALU.mult, ALU.mult, accum=yT[:, j, b:b + 1])
                        P.dma("sp", D["s_ssd"][li, b].re("(j q) n -> q j n", q=128), Sn[:, :, :], is_output=True)
                    y = y_r()
                    transposes(y[:, :].re("t (c q) -> t c q", q=128), [yT[:, j, :] for j in range(16)], 128, F32)
                    gate_norm_out(ti, n, y, xs3, zs[0:n, :], xt)
            barrier()

        stop = debug
        try:
          if debug == "oddfirst":
              odd_phase(1)
              raise _Stop()
          if debug is not None and debug.startswith("peerfirst"):
              peer_phase_dense(0)
              raise _Stop()
          for layer in range(n_layers):
            if layer % 2 == 0:
                if even_phase(layer):
                    break
            else:
                odd_phase(layer)
            ck(f"mix{layer}")
            peer_phase_dense(layer)
            ck(f"peer{layer}")
        except _Stop:
            pass
        P.finish()
        print("program: instr", P.ninstr, "sems", P.nsem, {e: P.cnt[e] for e in P.ENG})
    return nc

def _consts():
    c = {}
    c["c_ident"] = np.eye(128, dtype=np.float32)
    s = np.arange(128)
    c["c_tri"] = (s[:, None] <= s[None, :]).astype(np.float32)
    e = np.zeros((128, 128), np.float32); e[127, :] = 1.0
    c["c_e127"] = e
    c["c_maskneg"] = np.where(s[:, None] <= s[None, :], 0.0, NEG).astype(np.float32)
    b = np.zeros((128, 255), np.float32); b[:, 127] = 1.0
    c["c_b255"] = b
    c["c_kk"] = (s + 1).astype(np.float32).reshape(128, 1)
    slopes = 2.0 ** (-8.0 * np.arange(1, 9, dtype=np.float32) / 8)
    col = np.arange(256)
    dist = 128 + s[:, None] - col[None, :]
    bp = np.zeros((128, 8, 256), np.float32)
    for j8 in range(8):
        bp[:, j8, :] = np.where((dist >= 0) & (dist < 128), -slopes[HP[j8]] * dist, NEG)
    c["c_bias_p"] = bp.reshape(128, 2048)
    bs = np.zeros((128, 128), np.float32)
    for p in range(128):
        bs[p, :] = -slopes[HP[p % 8]] * (127 - np.arange(128))
    c["c_bias_s"] = bs
    e16 = np.zeros((16, 16, 128), np.float32)
    for b_ in range(16):
        e16[b_, b_, :] = 1.0
    c["c_e16"] = e16.reshape(16, 2048)
    c["c_iota16"] = np.broadcast_to(np.arange(16, dtype=np.float32), (128, 16)).copy()
    c["c_iota128"] = np.broadcast_to(np.arange(128, dtype=np.float32), (128, 128)).copy()
    return c


IN_SHAPES = [
    ("x_p", [2048, 1024]), ("x_s", [16, 1024]), ("st_s5_re", [2, 16, 2048]), ("st_s5_im", [2, 16, 2048]),
    ("cache_k", [2, 16, 128, 128]), ("cache_v", [2, 16, 128, 128]), ("st_ssd", [2, 16, 2048, 128]), ("st_conv", [2, 16, 3, 3072]),
    ("w_in_e", [2, 1024, 1280]), ("lam_re", [2, 2048]), ("lam_im", [2, 2048]), ("log_dt", [2, 2048]),
    ("bblk_re", [2, 128, 2048]), ("bblk_im", [2, 128, 2048]), ("cblk_re", [2, 128, 512]), ("cblk_im", [2, 128, 512]),
    ("s5_d", [2, 512]), ("w_glu", [2, 512, 512]), ("sinks_p", [2, 8]), ("sinks_s", [2, 128]), ("w_out_e", [2, 1024, 1024]),
    ("w_in_o", [2, 1024, 5152]), ("conv_w", [2, 128, 96]), ("conv_b", [2, 128, 24]), ("dt_bias", [2, 32]), ("a_log", [2, 32]),
    ("ssd_d", [2, 32]), ("norm_w", [2, 2048]), ("w_out_o", [2, 2048, 1024]),
    ("ln1_g", [4, 1024]), ("ln1_b", [4, 1024]), ("ln2_g", [4, 1024]), ("ln2_b", [4, 1024]),
    ("w_q", [4, 1024, 2048]), ("k1t", [4, 128, 1024]), ("k2t", [4, 128, 1024]),
    ("peer_ut", [4, 1024, 16384]), ("peer_v", [4, 16384, 1024]),
    ("c_ident", [128, 128]), ("c_tri", [128, 128]), ("c_e127", [128, 128]), ("c_maskneg", [128, 128]), ("c_b255", [128, 255]),
    ("c_kk", [128, 1]), ("c_bias_p", [128, 2048]), ("c_bias_s", [128, 128]), ("c_e16", [16, 2048]), ("c_iota16", [128, 16]), ("c_iota128", [128, 128]),
]
OUT_SHAPES = [
    ("y_p", [2048, 1024]), ("y_s", [16, 1024]), ("p_s5_re", [2, 2048]), ("p_s5_im", [2, 2048]),
    ("p_swa_k", [2, 128, 128]), ("p_swa_v", [2, 128, 128]), ("p_ssd", [2, 2048, 128]), ("p_conv", [2, 3, 3072]),
    ("s_s5_re", [2, 16, 2048]), ("s_s5_im", [2, 16, 2048]), ("s_swa_k", [2, 16, 128, 128]), ("s_swa_v", [2, 16, 128, 128]),
    ("s_ssd", [2, 16, 2048, 128]), ("s_conv", [2, 16, 3, 3072]),
]


def _shared_inputs(inp):
    f = lambda a: np.ascontiguousarray(np.asarray(a, dtype=np.float32))
    sh = {}
    w_in_e = f(inp["w_in_even"])
    perm_q = np.concatenate([np.arange(512)] + [512 + HP[j] * 64 + np.arange(64) for j in range(8)] + [np.arange(1024, 1280)])
    sh["w_in_e"] = np.ascontiguousarray(w_in_e[:, :, perm_q])
    sh["lam_re"] = f(inp["s5_lambda_re"]).reshape(2, 2048)
    sh["lam_im"] = f(inp["s5_lambda_im"]).reshape(2, 2048)
    sh["log_dt"] = np.ascontiguousarray(np.repeat(f(inp["s5_log_dt"]), 64, axis=1))
    for nm, src in (("bblk_re", "s5_b_re"), ("bblk_im", "s5_b_im")):
        b = f(inp[src])
        blk = np.zeros((2, 128, 4, 8, 64), np.float32)
        for ch in range(4):
            for gl in range(8):
                g = ch * 8 + gl
                blk[:, gl * 16:(gl + 1) * 16, ch, gl, :] = np.transpose(b[:, g], (0, 2, 1))
        sh[nm] = blk.reshape(2, 128, 2048)
    for nm, src in (("cblk_re", "s5_c_re"), ("cblk_im", "s5_c_im")):
        cc = f(inp[src])
        blk = np.zeros((2, 128, 16, 2, 16), np.float32)
        for kc in range(16):
            for gl in range(2):
                g = kc * 2 + gl
                blk[:, gl * 64:(gl + 1) * 64, kc, gl, :] = np.transpose(cc[:, g], (0, 2, 1))
        sh[nm] = blk.reshape(2, 128, 512)
    sh["s5_d"] = f(inp["s5_d"])
    sh["w_glu"] = f(inp["s5_w_glu"])
    sk = f(inp["swa_sinks"])[:, HP]
    sh["sinks_p"] = np.ascontiguousarray(sk)
    sh["sinks_s"] = np.ascontiguousarray(np.tile(sk, (1, 16)))
    w_out_e = f(inp["w_out_even"])
    perm_o = np.concatenate([np.arange(512)] + [512 + HP[j] * 64 + np.arange(64) for j in range(8)])
    sh["w_out_e"] = np.ascontiguousarray(w_out_e[:, perm_o, :])
    sh["w_in_o"] = f(inp["w_in_odd"])
    cw = f(inp["ssd_conv_w"])
    sh["conv_w"] = np.ascontiguousarray(cw.reshape(2, 4, 24, 128).transpose(0, 3, 2, 1).reshape(2, 128, 96))
    sh["conv_b"] = np.ascontiguousarray(f(inp["ssd_conv_b"]).reshape(2, 24, 128).transpose(0, 2, 1))
    sh["dt_bias"] = f(inp["ssd_dt_bias"]); sh["a_log"] = f(inp["ssd_a_log"]); sh["ssd_d"] = f(inp["ssd_d"])
    sh["norm_w"] = f(inp["ssd_norm_w"]); sh["w_out_o"] = f(inp["w_out_odd"])
    for k in ("ln1_g", "ln1_b", "ln2_g", "ln2_b"):
        sh[k] = f(inp[k])
    sh["w_q"] = f(inp["peer_w_q"])
    sh["k1t"] = np.ascontiguousarray(f(inp["peer_k1"]).transpose(0, 3, 1, 2).reshape(4, 128, 1024))
    sh["k2t"] = np.ascontiguousarray(f(inp["peer_k2"]).transpose(0, 3, 1, 2).reshape(4, 128, 1024))
    sh["peer_ut"] = np.ascontiguousarray(f(inp["peer_u"]).transpose(0, 2, 1)); sh["peer_v"] = f(inp["peer_v"])
    sh.update(_consts())
    if globals().get("_TINY"):
        for nm in ("w_in_o", "w_q", "w_out_o"):
            sh[nm] = np.ascontiguousarray(sh[nm][:, :128])
    return sh


def _core_inputs(inp, sh, c):
    f = lambda a: np.ascontiguousarray(np.asarray(a, dtype=np.float32))
    b0 = c * 16
    m = dict(sh)
    m["x_p"] = f(inp["x_prompt"][c])
    m["x_s"] = f(inp["x_sample"][b0:b0 + 16, 0])
    m["st_s5_re"] = f(inp["state_s5_re"][:, b0:b0 + 16]).reshape(2, 16, 2048)
    m["st_s5_im"] = f(inp["state_s5_im"][:, b0:b0 + 16]).reshape(2, 16, 2048)
    m["cache_k"] = f(inp["cache_swa_k"][:, b0:b0 + 16]).reshape(2, 16, 128, 128)
    m["cache_v"] = f(inp["cache_swa_v"][:, b0:b0 + 16]).reshape(2, 16, 128, 128)
    m["st_ssd"] = f(inp["state_ssd"][:, b0:b0 + 16]).reshape(2, 16, 2048, 128)
    m["st_conv"] = f(inp["state_conv"][:, b0:b0 + 16])
    if globals().get("_TINY"):
        m["st_ssd"] = np.ascontiguousarray(m["st_ssd"][:, :, :128])
    return m


_NC_CACHE = {}
_LAST = None


def kernel(**inp):
    key = "full"
    if key not in _NC_CACHE:
        _NC_CACHE[key] = build_nc()
    nc = _NC_CACHE[key]
    sh = _shared_inputs(inp)
    in_maps = [_core_inputs(inp, sh, c) for c in range(8)]
    res = run_bass_kernel_spmd(nc, in_maps, core_ids=list(range(8)))
    R = res.results
    global _LAST
    _LAST = R
    g = lambda name: [np.asarray(R[c][name], dtype=np.float32) for c in range(8)]
    y_p = np.stack(g("y_p"), 0)
    y_s = np.concatenate(g("y_s"), 0).reshape(128, 1, 1024)
    p_s5_re = np.stack(g("p_s5_re"), 1).reshape(2, 8, 32, 64)
    p_s5_im = np.stack(g("p_s5_im"), 1).reshape(2, 8, 32, 64)
    p_swa_k = np.stack(g("p_swa_k"), 1).reshape(2, 8, 128, 2, 64)
    p_swa_v = np.stack(g("p_swa_v"), 1).reshape(2, 8, 128, 2, 64)
    p_ssd = np.stack(g("p_ssd"), 1).reshape(2, 8, 32, 64, 128)
    p_conv = np.stack(g("p_conv"), 1).reshape(2, 8, 3, 3072)
    s_s5_re = np.concatenate(g("s_s5_re"), 1).reshape(2, 128, 32, 64)
    s_s5_im = np.concatenate(g("s_s5_im"), 1).reshape(2, 128, 32, 64)
    s_swa_k = np.concatenate(g("s_swa_k"), 1).reshape(2, 128, 128, 2, 64)
    s_swa_v = np.concatenate(g("s_swa_v"), 1).reshape(2, 128, 128, 2, 64)
    s_ssd = np.concatenate(g("s_ssd"), 1).reshape(2, 128, 32, 64, 128)
    s_conv = np.concatenate(g("s_conv"), 1).reshape(2, 128, 3, 3072)
    return (y_p, y_s, p_s5_re, p_s5_im, p_swa_k, p_swa_v, p_ssd, p_conv,
            s_s5_re, s_s5_im, s_swa_k, s_swa_v, s_ssd, s_conv)
```

```python
import numpy as np
from contextlib import ExitStack
import concourse.bass as bass
import concourse.mybir as mybir
from concourse.bass_utils import run_bass_kernel_spmd

F32 = mybir.dt.float32
BF16 = mybir.dt.bfloat16
I32 = mybir.dt.int32
U32 = mybir.dt.uint32
AF = mybir.ActivationFunctionType
ALU = mybir.AluOpType
AX = mybir.AxisListType

SEM_GEN = 20000
NDSEM = 24


class Buf:
    def __init__(self, t, name=""):
        self.t = t
        self.name = name
        self.w = None
        self.r = {}

    def __getitem__(self, idx):
        return V(self, self.t[idx])

    def ap(self):
        return V(self, self.t.ap() if hasattr(self.t, "ap") else self.t[:])


class V:
    def __init__(self, buf, ap):
        self.buf = buf
        self.ap = ap

    def __getitem__(self, idx):
        return V(self.buf, self.ap[idx])

    def re(self, s, **kw):
        return V(self.buf, self.ap.rearrange(s, **kw))

    def bc(self, shape):
        return V(self.buf, self.ap.to_broadcast(shape))

    def bitcast(self, dt):
        return V(self.buf, self.ap.bitcast(dt))


class Prog:
    ENG = ("pe", "dve", "act", "pool", "sp")

    def __init__(self, nc, es):
        self.nc = nc
        self.es = es
        self.eng = {"pe": nc.tensor, "dve": nc.vector, "act": nc.scalar, "pool": nc.gpsimd, "sp": nc.sync}
        self.ops = {e: [] for e in self.ENG}
        self.cnt = {e: 0 for e in self.ENG}
        self.sems = {e: [] for e in self.ENG}
        self.seen = {e: {} for e in self.ENG}
        self.dsem = {}
        self.dcur = {e: 0 for e in self.ENG}
        self.nsem = 0
        self.out_tokens = []
        self.ninstr = 0

    def sb(self, name, shape, dt=F32):
        return Buf(self.es.enter_context(self.nc.sbuf_tensor(name, list(shape), dt)), name)

    def ps(self, name, shape, dt=F32):
        b = Buf(self.es.enter_context(self.nc.psum_tensor(name, list(shape), dt)), name)
        b.excl = True
        return b

    def dram(self, name, shape, dt=F32, kind="Internal"):
        return Buf(self.nc.dram_tensor(name, list(shape), dt, kind=kind), name)

    def _newsem(self, name):
        self.nsem += 1
        return self.es.enter_context(self.nc.semaphore(name))

    def _esem(self, e, gen):
        while len(self.sems[e]) <= gen:
            self.sems[e].append(self._newsem(f"s_{e}_{len(self.sems[e])}"))
        return self.sems[e][gen]

    def _need(self, e, tok):
        if tok is None:
            return
        sem, val, key, src = tok
        if self.seen[e].get(key, 0) >= val:
            return
        self.seen[e][key] = val
        self.eng[e].wait_ge(sem, val)

    def _deps(self, e, ins, outs, pe_accum=False):
        for v in ins:
            self._need(e, v.buf.w)
            if getattr(v.buf, "excl", False):
                for t in v.buf.r.values():
                    if t[3] != e:
                        self._need(e, t)
        for v in outs:
            b = v.buf
            if not (pe_accum and b.w is not None and b.w[3] == "pe" and e == "pe"):
                self._need(e, b.w)
            for t in b.r.values():
                self._need(e, t)

    def _commit(self, tok, ins, outs):
        for v in ins:
            if v.buf not in [o.buf for o in outs]:
                v.buf.r[tok[2]] = tok
        for v in outs:
            v.buf.w = tok
            v.buf.r = {}

    def op(self, e, fn, ins=(), outs=(), pe_accum=False):
        if getattr(self, "mute", None) and self.mute["on"]:
            return None
        ins = [v for v in ins if isinstance(v, V)]
        outs = [v for v in outs if isinstance(v, V)]
        self._deps(e, ins, outs, pe_accum)
        gen = self.cnt[e] // SEM_GEN
        sem = self._esem(e, gen)
        self.cnt[e] += 1
        val = self.cnt[e] - gen * SEM_GEN
        fn(self.eng[e]).then_inc(sem, 1)
        tok = (sem, val, (e, gen), e)
        self._commit(tok, ins, outs)
        self.ninstr += 1
        return tok

    def dma(self, q, out, in_, fn=None, is_output=False, **kw):
        if getattr(self, "mute", None) and self.mute["on"]:
            return None
        ins = [in_] + list(kw.pop("extra_ins", []))
        outs = [out]
        self._deps(q, ins, outs)
        ring = self.dsem.setdefault(q, [])
        if len(ring) < NDSEM:
            ring.append([self._newsem(f"d_{q}_{len(ring)}"), 0])
            slot = len(ring) - 1
        else:
            slot = self.dcur[q] % NDSEM
        self.dcur[q] += 1
        ent = ring[slot]
        sem, uses = ent
        key = ("d", q, slot)
        if uses > 0:
            self._need(q, (sem, 16 * uses, key, "dma"))
        ent[1] = uses + 1
        if fn is None:
            fn = lambda eng, o=out.ap, i=in_.ap, kw=kw: eng.dma_start(out=o, in_=i, **kw)
        fn(self.eng[q]).then_inc(sem, 16)
        tok = (sem, 16 * (uses + 1), key, "dma")
        self._commit(tok, ins, outs)
        if is_output:
            self.out_tokens.append(tok)
        self.ninstr += 1
        return tok

    def finish(self):
        for tok in self.out_tokens:
            self._need("sp", tok)

    def mm(self, out, lhsT, rhs, start=True, stop=True):
        return self.op("pe", lambda g: g.matmul(out.ap, lhsT.ap, rhs.ap, start=start, stop=stop),
                       ins=[lhsT, rhs], outs=[out], pe_accum=not start)

    def tr(self, out, in_, ident):
        return self.op("pe", lambda g: g.transpose(out.ap, in_.ap, ident.ap), ins=[in_, ident], outs=[out])

    def act(self, out, in_, func, bias=None, scale=None, accum=None, e="act"):
        kw = {}
        ins = [in_]
        outs = [out]
        if bias is not None:
            kw["bias"] = bias.ap if isinstance(bias, V) else bias
            ins.append(bias)
        if scale is not None:
            kw["scale"] = scale.ap if isinstance(scale, V) else scale
            ins.append(scale)
        if accum is not None:
            kw["accum_out"] = accum.ap
            outs.append(accum)
        return self.op("act", lambda g: g.activation(out.ap, in_.ap, func, **kw), ins=ins, outs=outs)

    def tt(self, out, a, b, op, e="dve"):
        return self.op(e, lambda g: g.tensor_tensor(out.ap, a.ap, b.ap, op), ins=[a, b], outs=[out])

    def ts(self, out, a, s1, s2, op0, op1=None, accum=None, e="dve"):
        ins = [a, s1, s2]
        outs = [out] + ([accum] if accum is not None else [])
        x1 = s1.ap if isinstance(s1, V) else s1
        x2 = s2.ap if isinstance(s2, V) else s2
        kw = {}
        if op1 is not None:
            kw["op1"] = op1
        if accum is not None:
            kw["accum_out"] = accum.ap
        return self.op(e, lambda g: g.tensor_scalar(out.ap, a.ap, x1, x2, op0, **kw), ins=ins, outs=outs)

    def stt(self, out, a, s, b, op0, op1, accum=None):
        x = s.ap if isinstance(s, V) else s
        kw = {"accum_out": accum.ap} if accum is not None else {}
        outs = [out] + ([accum] if accum is not None else [])
        return self.op("dve", lambda g: g.scalar_tensor_tensor(out.ap, a.ap, x, b.ap, op0, op1, **kw),
                       ins=[a, s, b], outs=outs)

    def ttr(self, out, a, b, op0, op1, accum, scale=1.0, scalar=0.0):
        return self.op("dve", lambda g: g.tensor_tensor_reduce(out.ap, a.ap, b.ap, scale, scalar, op0, op1, accum.ap),
                       ins=[a, b], outs=[out, accum])

    def copy(self, out, in_, e="dve"):
        if e == "act":
            return self.op("act", lambda g: g.copy(out.ap, in_.ap), ins=[in_], outs=[out])
        return self.op(e, lambda g: g.tensor_copy(out.ap, in_.ap), ins=[in_], outs=[out])

    def memset(self, out, val, e="dve"):
        return self.op(e, lambda g: g.memset(out.ap, val), outs=[out])

    def red(self, out, in_, op, axis=AX.X, e="dve"):
        return self.op(e, lambda g: g.tensor_reduce(out.ap, in_.ap, axis, op), ins=[in_], outs=[out])

    def recip(self, out, in_):
        return self.op("dve", lambda g: g.reciprocal(out.ap, in_.ap), ins=[in_], outs=[out])

HP = [0, 4, 1, 5, 2, 6, 3, 7]
ALPHA = float(8 ** 0.25)
NTILES = 17
NEG = -1.0e30
TWO_PI = 6.283185307179586


class _Stop(Exception):
    pass


def build_nc(n_layers=4, debug=None, small=False):
    mute = {"on": False}

    def ck(tag):
        if debug == tag:
            mute["on"] = True

    nc = bass.Bass("TRN2", target_bir_lowering=False)
    D = {}

    def din(name, shape, dt=F32):
        D[name] = Buf(nc.dram_tensor(name, list(shape), dt, kind="ExternalInput"), name)

    def dout(name, shape):
        D[name] = Buf(nc.dram_tensor(name, list(shape), F32, kind="ExternalOutput"), name)

    for name, shape in IN_SHAPES:
        if small and name == "peer_v":
            shape = [4, 128, 1024]
        if small and name == "peer_ut":
            shape = [4, 1024, 128]
        if small == 2 and name in ("w_in_o", "w_q", "st_ssd", "w_out_o"):
            shape = [shape[0], 128] + list(shape[2:]) if name != "st_ssd" else [2, 16, 128, 128]
        din(name, shape)
    for name, shape in OUT_SHAPES:
        dout(name, shape)

    with ExitStack() as es:
        P = Prog(nc, es)
        P.mute = mute
        xdram = Buf(nc.dram_tensor("xscratch", [NTILES, 128, 1024], F32, kind=("ExternalOutput" if debug else "Internal")), "xscratch")
        xd = [Buf(xdram.t, f"xd{i}")[i] for i in range(NTILES)]

        def bcast_rows(src_buf, row, ncols, nparts=128):
            return V(src_buf, src_buf.t[row:row + 1, 0:ncols].to_broadcast([nparts, ncols]))

        identf = P.sb("identf", [128, 128]); P.dma("sp", identf[:, :], D["c_ident"][:, :])
        identb = P.sb("identb", [128, 128], BF16); P.copy(identb[:, :], identf[:, :])
        triT = P.sb("triT", [128, 128]); P.dma("sp", triT[:, :], D["c_tri"][:, :])
        e127 = P.sb("e127", [128, 128]); P.dma("sp", e127[:, :], D["c_e127"][:, :])
        b255f = P.sb("b255f", [128, 255]); P.dma("sp", b255f[:, :], D["c_b255"][:, :])
        b255 = P.sb("b255", [128, 255], BF16); P.copy(b255[:, :], b255f[:, :])
        kkp = P.sb("kkp", [128, 1]); P.dma("sp", kkp[:, :], D["c_kk"][:, :])
        kkn = P.sb("kkn", [128, 1]); P.ts(kkn[:, :], kkp[:, :], -1.0, None, ALU.mult)
        one1 = P.sb("one1", [128, 1]); P.memset(one1[:, :], 1.0)
        mone1 = P.sb("mone1", [128, 1]); P.memset(mone1[:, :], -1.0)

        banks = [P.ps(f"bank{i}", [128, 512]) for i in range(8)]
        rot = {"i": 0}

        def pbank():
            b = banks[rot["i"] % 6]
            rot["i"] += 1
            return b

        ev = {"i": 0}

        def evac(out, in_, e=None):
            if e is None and ev.get("force"):
                e = ev["force"]
            if e is None:
                e = "act" if ev["i"] % 2 == 0 else "dve"
                ev["i"] += 1
            return P.copy(out, in_, e=e)

        uid = {"i": 0}

        def ring(sc, name, shape, dt, n):
            uid["i"] += 1
            name = f"{name}_{uid['i']}_"
            bufs = [Buf(sc.enter_context(nc.sbuf_tensor(f"{name}{i}", list(shape), dt)), f"{name}{i}") for i in range(n)]
            st = {"i": 0}

            def nxt():
                b = bufs[st["i"] % n]
                st["i"] += 1
                return b
            return nxt

        def sbuf(sc, name, shape, dt=F32):
            uid["i"] += 1
            name = f"{name}_{uid['i']}"
            return Buf(sc.enter_context(nc.sbuf_tensor(name, list(shape), dt)), name)

        def barrier():
            if mute["on"]:
                return
            toks = []
            for e in P.ENG:
                if P.cnt[e] > 0:
                    gen = (P.cnt[e] - 1) // SEM_GEN
                    toks.append((P.sems[e][gen], P.cnt[e] - gen * SEM_GEN, (e, gen), e))
            for q, rg in P.dsem.items():
                for slot, (sem, uses) in enumerate(rg):
                    if uses > 0:
                        toks.append((sem, 16 * uses, ("d", q, slot), "dma"))
            for e in P.ENG:
                for t in toks:
                    if t[3] != e or t[3] == "dma":
                        P._need(e, t)

        def transposes(dst, srcs, n, dt):
            per = 8 if dt == BF16 else 4
            ident = identb if dt == BF16 else identf
            i = 0
            while i < len(srcs):
                grp = srcs[i:i + per]
                bk = pbank()
                bv = bk[:, :].bitcast(BF16) if dt == BF16 else bk[:, :]
                for j, s in enumerate(grp):
                    w = s.ap.shape[1]
                    P.tr(bv[0:w, j * 128:j * 128 + n], s, ident[0:n, 0:n])
                wmax = max(s.ap.shape[1] for s in grp)
                evac(dst[0:wmax, i:i + len(grp), 0:n],
                     bv[0:wmax, 0:len(grp) * 128].re("p (j t) -> p j t", t=128)[:, :, 0:n])
                i += per

        def make_xT(sc_ring_xb, xT, xt, n):
            xb = sc_ring_xb()
            P.copy(xb[0:n, :], xt[0:n, :], e="pool")
            transposes(xT[:, :, :], [xb[0:n, k * 128:(k + 1) * 128] for k in range(8)], n, BF16)

        def xsrc(layer_first, ti):
            if layer_first:
                if ti < 16:
                    return D["x_p"][ti * 128:(ti + 1) * 128, :]
                return D["x_s"][:, :]
            return xd[ti][0:(128 if ti < 16 else 16), :]

        def layernorm(sc_tmp, x_old, mix_views, n, g_bc, b_bc, out_buf):
            s = sc_tmp["ln_s"]()
            for hh in range(2):
                P.stt(s[0:n, hh * 512:(hh + 1) * 512], x_old[0:n, hh * 512:(hh + 1) * 512], ALPHA, mix_views[hh],
                      ALU.mult, ALU.add)
            st = sc_tmp["ln_st"]()
            for hh in range(2):
                P.op("dve", lambda g, hh=hh: g.bn_stats(st.t[0:n, hh * 6:(hh + 1) * 6], s.t[0:n, hh * 512:(hh + 1) * 512]),
                     ins=[s[:, :]], outs=[st[:, :]])
            mv = sc_tmp["ln_mv"]()
            P.op("dve", lambda g: g.bn_aggr(mv.t[0:n, 0:2], st.t[0:n, 0:12]), ins=[st[:, :]], outs=[mv[:, :]])
            P.ts(mv[0:n, 2:3], mv[0:n, 1:2], 1e-5, None, ALU.add)
            P.act(mv[0:n, 3:4], mv[0:n, 2:3], AF.Sqrt)
            P.recip(mv[0:n, 4:5], mv[0:n, 3:4])
            P.ts(s[0:n, :], s[0:n, :], mv[0:n, 0:1], mv[0:n, 4:5], ALU.subtract, ALU.mult)
            P.tt(s[0:n, :], s[0:n, :], g_bc[0:n, :], ALU.mult, e="pool")
            P.tt(s[0:n, :], s[0:n, :], b_bc[0:n, :], ALU.add, e="pool")
            return s

        def even_phase(layer):
            li = layer // 2
            first = layer == 0
            with ExitStack() as sc:
                w_in = sbuf(sc, "e_win", [128, 8, 1280], BF16)
                w_out = sbuf(sc, "e_wout", [128, 8, 1024], BF16)
                w_glu = sbuf(sc, "e_wglu", [128, 4, 512], BF16)
                c_re = sbuf(sc, "e_cre", [128, 512], BF16)
                c_im = sbuf(sc, "e_cim", [128, 512], BF16)
                bb_re = sbuf(sc, "e_bbre", [128, 4, 512], BF16)
                bb_im = sbuf(sc, "e_bbim", [128, 4, 512], BF16)
                d_bc = sbuf(sc, "e_dbc", [128, 512])
                sk_p = sbuf(sc, "e_skp", [128, 8])
                sk_s = sbuf(sc, "e_sks", [128, 1])
                lng = sbuf(sc, "e_lng", [128, 1024]); lnb = sbuf(sc, "e_lnb", [128, 1024])
                bias_p = sbuf(sc, "e_biasp", [128, 8, 256])
                bias_s = sbuf(sc, "e_biass", [128, 128])
                tabs = {k: sbuf(sc, f"e_tab_{k}", [128, 2048]) for k in ("pr", "pi")}
                P.dma("pool", w_in[:, :, :], D["w_in_e"][li].re("(k p) n -> p k n", p=128))
                P.dma("pool", w_out[:, :, :], D["w_out_e"][li].re("(k p) n -> p k n", p=128))
                P.dma("pool", w_glu[:, :, :], D["w_glu"][li].re("(k p) n -> p k n", p=128))
                P.dma("pool", c_re[:, :], D["cblk_re"][li])
                with ExitStack() as s0:
                    cimf = sbuf(s0, "e_cimf", [128, 512])
                    P.dma("sp", cimf[:, :], D["cblk_im"][li])
                    P.ts(c_im[:, :], cimf[:, :], -1.0, None, ALU.mult)
                    barrier()
                P.dma("sp", d_bc[:, :], bcast_rows(D["s5_d"], li, 512))
                P.dma("sp", sk_p[:, :], bcast_rows(D["sinks_p"], li, 8))
                P.dma("sp", sk_s[:, :], D["sinks_s"][li].re("(p o) -> p o", o=1))
                P.dma("sp", lng[:, :], bcast_rows(D["ln1_g"], layer, 1024))
                P.dma("sp", lnb[:, :], bcast_rows(D["ln1_b"], layer, 1024))
                P.dma("sp", bias_p[:, :, :], D["c_bias_p"][:, :].re("p (h c) -> p h c", h=8))
                P.dma("sp", bias_s[:, :], D["c_bias_s"][:, :])

                def build_tables(kk_pos, kk_neg, with_bb):
                    with ExitStack() as s2:
                        T_ = lambda nm: sbuf(s2, f"e_su_{nm}", [128, 256])
                        lr, lim, dt_, ldt, lid, ang, tq, sn, cs, mg, t1, t2, t3 = [T_(n_) for n_ in
                            ("lr", "li", "dt", "ldt", "lid", "ang", "tq", "sn", "cs", "mg", "t1", "t2", "t3")]
                        tqi = sbuf(s2, "e_su_tqi", [128, 256], I32)
                        bsr = sbuf(s2, "e_su_bsr", [128, 256]); bsi = sbuf(s2, "e_su_bsi", [128, 256])
                        bbr_flat = bb_re[:, :, :].re("p c n -> p (c n)")
                        bbi_flat = bb_im[:, :, :].re("p c n -> p (c n)")
                        for c in range(8):
                            cs_ = slice(c * 256, (c + 1) * 256)
                            P.dma("sp", lr[:, :], V(D["lam_re"], D["lam_re"].t[li:li + 1, cs_].to_broadcast([128, 256])))
                            P.dma("sp", lim[:, :], V(D["lam_im"], D["lam_im"].t[li:li + 1, cs_].to_broadcast([128, 256])))
                            P.dma("sp", dt_[:, :], V(D["log_dt"], D["log_dt"].t[li:li + 1, cs_].to_broadcast([128, 256])))
                            P.act(dt_[:, :], dt_[:, :], AF.Exp)
                            P.tt(ldt[:, :], lr[:, :], dt_[:, :], ALU.mult)
                            P.tt(lid[:, :], lim[:, :], dt_[:, :], ALU.mult)

                            def sincos(kk, shift, outt):
                                P.ts(ang[:, :], lid[:, :], kk[:, 0:1], shift, ALU.mult, ALU.add)
                                P.ts(tq[:, :], ang[:, :], 1.0 / TWO_PI, None, ALU.mult)
                                P.copy(tqi[:, :], tq[:, :])
                                P.copy(tq[:, :], tqi[:, :])
                                P.stt(ang[:, :], tq[:, :], -TWO_PI, ang[:, :], ALU.mult, ALU.add)
                                P.ts(ang[:, :], ang[:, :], 3.1415925, -3.1415925, ALU.min, ALU.max)
                                P.act(outt[:, :], ang[:, :], AF.Sin)

                            def cplx_tab(kk, tre, tim):
                                sincos(kk, 0.0, sn)
                                sincos(kk, 1.5707963267948966, cs)
                                P.act(mg[:, :], ldt[:, :], AF.Exp, scale=kk[:, 0:1])
                                P.tt(tre[:, cs_], mg[:, :], cs[:, :], ALU.mult)
                                P.tt(tim[:, cs_], mg[:, :], sn[:, :], ALU.mult)

                            cplx_tab(kk_pos, tabs["pr"], tabs["pi"])
                            if kk_neg is not None:
                                cplx_tab(kk_neg, tabs["nr"], tabs["ni"])
                            if with_bb:
                                sincos(one1, 0.0, sn)
                                sincos(one1, 1.5707963267948966, cs)
                                P.act(mg[:, :], ldt[:, :], AF.Exp)
                                P.tt(t1[:, :], mg[:, :], cs[:, :], ALU.mult)
                                P.tt(t2[:, :], mg[:, :], sn[:, :], ALU.mult)
                                P.ts(t1[:, :], t1[:, :], -1.0, None, ALU.add)
                                P.tt(t3[:, :], lr[:, :], lr[:, :], ALU.mult)
                                P.tt(tq[:, :], lim[:, :], lim[:, :], ALU.mult)
                                P.tt(t3[:, :], t3[:, :], tq[:, :], ALU.add)
                                P.recip(t3[:, :], t3[:, :])
                                P.tt(tq[:, :], t1[:, :], lr[:, :], ALU.mult)
                                P.tt(ang[:, :], t2[:, :], lim[:, :], ALU.mult)
                                P.tt(tq[:, :], tq[:, :], ang[:, :], ALU.add)
                                P.tt(cs[:, :], tq[:, :], t3[:, :], ALU.mult)
                                P.tt(tq[:, :], t2[:, :], lr[:, :], ALU.mult)
                                P.tt(ang[:, :], t1[:, :], lim[:, :], ALU.mult)
                                P.tt(tq[:, :], tq[:, :], ang[:, :], ALU.subtract)
                                P.tt(sn[:, :], tq[:, :], t3[:, :], ALU.mult)
                                P.dma("sp", bsr[:, :], D["bblk_re"][li, :, cs_])
                                P.dma("sp", bsi[:, :], D["bblk_im"][li, :, cs_])
                                P.tt(t1[:, :], cs[:, :], bsr[:, :], ALU.mult)
                                P.tt(t2[:, :], sn[:, :], bsi[:, :], ALU.mult)
                                P.tt(bbr_flat[:, cs_], t1[:, :], t2[:, :], ALU.subtract)
                                P.tt(t1[:, :], cs[:, :], bsi[:, :], ALU.mult)
                                P.tt(t2[:, :], sn[:, :], bsr[:, :], ALU.mult)
                                P.tt(bbi_flat[:, cs_], t1[:, :], t2[:, :], ALU.add)
                    barrier()

                x_r = ring(sc, "e_x", [128, 1024], F32, 2)
                xb_r = ring(sc, "e_xb", [128, 1024], BF16, 2)
                xT_r = ring(sc, "e_xT", [128, 8, 128], BF16, 2)
                u_r = ring(sc, "e_u", [128, 512], F32, 2)
                ub_r = ring(sc, "e_ub", [128, 512], BF16, 2)
                uT_r = ring(sc, "e_uT", [128, 4, 128], BF16, 2)
                tmp_r = ring(sc, "e_tmp", [128, 512], F32, 4)
                w_r = ring(sc, "e_w", [128, 512], F32, 2)
                hre = [sbuf(sc, f"e_hre{c}", [128, 512]) for c in range(4)]
                him = [sbuf(sc, f"e_him{c}", [128, 512]) for c in range(4)]
                hT_re = sbuf(sc, "e_hTre", [128, 16, 128], BF16)
                hT_im = sbuf(sc, "e_hTim", [128, 16, 128], BF16)
                y_r = ring(sc, "e_y", [128, 512], F32, 3)
                gb_r = ring(sc, "e_gb", [128, 512], BF16, 1)
                gT_r = ring(sc, "e_gT", [128, 4, 128], BF16, 1)
                mix_r = ring(sc, "e_mix", [128, 1024], BF16, 1)
                mixT_r = ring(sc, "e_mixT", [128, 8, 128], BF16, 1)
                qb_r = ring(sc, "e_qb", [128, 512], BF16, 2)
                kvf_r = ring(sc, "e_kvf", [128, 256], F32, 2)
                sm_r = ring(sc, "e_sm", [128, 8], F32, 8)
                lnr = {"ln_s": ring(sc, "e_lns", [128, 1024], F32, 1), "ln_st": ring(sc, "e_lnst", [128, 12], F32, 2),
                       "ln_mv": ring(sc, "e_lnmv", [128, 8], F32, 2)}
                smid = ExitStack()
                sc.callback(smid.close)
                for k in ("nr", "ni"):
                    tabs[k] = sbuf(smid, f"e_tab_{k}", [128, 2048])
                build_tables(kkp, kkn, True)
                ck("e_setup")

                with ExitStack() as s3:
                    qT_r = ring(s3, "e_qT", [128, 4, 128], BF16, 2)
                    kvb = [sbuf(s3, f"e_kvb{i}", [128, 256], BF16) for i in range(2)]
                    kT = [sbuf(s3, f"e_kT{i}", [128, 1, 128], BF16) for i in range(2)]
                    sb_r = ring(s3, "e_sb", [128, 256], F32, 4)
                    pb_r = ring(s3, "e_pb", [128, 256], BF16, 4)
                    pT_r = ring(s3, "e_pT", [128, 2, 128], BF16, 4)

                    def s5_tail_and_rest(ti, n, xt, xT, pj_q, pj_kv, u, sample):
                        transposes(hT_re[:, :, :], [hre[c][0:n, k * 128:(k + 1) * 128] for c in range(4) for k in range(4)], n, F32)
                        transposes(hT_im[:, :, :], [him[c][0:n, k * 128:(k + 1) * 128] for c in range(4) for k in range(4)], n, F32)
                        py = pbank()
                        for kc in range(16):
                            P.mm(py[0:n, kc * 32:(kc + 1) * 32], hT_re[:, kc, 0:n], c_re[:, kc * 32:(kc + 1) * 32], start=True, stop=False)
                            P.mm(py[0:n, kc * 32:(kc + 1) * 32], hT_im[:, kc, 0:n], c_im[:, kc * 32:(kc + 1) * 32], start=False, stop=True)
                        du = y_r()
                        P.tt(du[0:n, :], d_bc[0:n, :], u[0:n, :], ALU.mult, e="pool")
                        y2 = y_r()
                        P.tt(y2[0:n, :], py[0:n, :], du[0:n, :], ALU.add)
                        g = y_r()
                        P.act(g[0:n, :], y2[0:n, :], AF.Gelu_apprx_tanh)
                        gb = gb_r()
                        P.copy(gb[0:n, :], g[0:n, :], e="pool")
                        gT = gT_r()
                        transposes(gT[:, :, :], [gb[0:n, k * 128:(k + 1) * 128] for k in range(4)], n, BF16)
                        pz = pbank()
                        for k in range(4):
                            P.mm(pz[0:n, :], gT[:, k, 0:n], w_glu[:, k, :], start=(k == 0), stop=(k == 3))
                        sg = tmp_r()
                        P.act(sg[0:n, :], pz[0:n, :], AF.Sigmoid)
                        mix = mix_r()
                        P.tt(mix[0:n, 0:512], g[0:n, :], sg[0:n, :], ALU.mult)
                        return mix

                    def out_and_ln(ti, n, xt, mixT):
                        po = [banks[6], banks[7]]
                        for hh in range(2):
                            for k in range(8):
                                P.mm(po[hh][0:n, :], mixT[:, k, 0:n], w_out[:, k, hh * 512:(hh + 1) * 512], start=(k == 0), stop=(k == 7))
                        xo = layernorm(lnr, xt, [po[0][0:n, :], po[1][0:n, :]], n, lng, lnb, None)
                        P.dma("sp", xd[ti][0:n, :], xo[0:n, :])

                    for ti in range(16):
                        if debug and debug.startswith("e_t") and ti >= int(debug[3:]):
                            mute["on"] = True
                        n = 128
                        xt = x_r()
                        P.dma("sp", xt[0:n, :], xsrc(first, ti))
                        xT = xT_r()
                        make_xT(xb_r, xT, xt, n)
                        ck("ck_xT")
                        pj = [pbank(), pbank(), pbank()]
                        for j, (c0, cw) in enumerate(((0, 512), (512, 512), (1024, 256))):
                            for k in range(8):
                                P.mm(pj[j][0:n, 0:cw], xT[:, k, 0:n], w_in[:, k, c0:c0 + cw], start=(k == 0), stop=(k == 7))
                        ck("ck_mm")
                        u = u_r(); ub = ub_r()
                        if debug != "ck_ev_b":
                            evac(u[0:n, :], pj[0][0:n, :], e="act")
                        if debug != "ck_ev_a":
                            evac(ub[0:n, :], pj[0][0:n, :], e="dve")
                        ck("ck_ev"); ck("ck_ev_a"); ck("ck_ev_b")
                        qb = qb_r()
                        P.act(qb[0:n, :], pj[1][0:n, :], AF.Copy, scale=0.125)
                        kvf = kvf_r()
                        evac(kvf[0:n, :], pj[2][0:n, 0:256], e="act")
                        cur = ti % 2
                        evac(kvb[cur][0:n, :], pj[2][0:n, 0:256], e="dve")
                        uT = uT_r()
                        transposes(uT[:, :, :], [ub[0:n, k * 128:(k + 1) * 128] for k in range(4)], n, BF16)
                        ck("ck_proj")
                        for c in range(4):
                            cs_ = slice(c * 512, (c + 1) * 512)
                            pre, pim = pbank(), pbank()
                            P.mm(pre[0:n, :], uT[:, c, 0:n], bb_re[:, c, :])
                            P.mm(pim[0:n, :], uT[:, c, 0:n], bb_im[:, c, :])
                            t1, t2, wre, wim = tmp_r(), tmp_r(), w_r(), w_r()
                            P.tt(t1[:, :], tabs["nr"][:, cs_], pre[:, :], ALU.mult)
                            P.tt(t2[:, :], tabs["ni"][:, cs_], pim[:, :], ALU.mult)
                            P.tt(wre[:, :], t1[:, :], t2[:, :], ALU.subtract, e="pool")
                            t3, t4 = tmp_r(), tmp_r()
                            P.tt(t3[:, :], tabs["nr"][:, cs_], pim[:, :], ALU.mult)
                            P.tt(t4[:, :], tabs["ni"][:, cs_], pre[:, :], ALU.mult)
                            P.tt(wim[:, :], t3[:, :], t4[:, :], ALU.add, e="pool")
                            cre, cim = pbank(), pbank()
                            P.mm(cre[:, :], triT[:, :], wre[:, :], start=True, stop=(ti == 0))
                            P.mm(cim[:, :], triT[:, :], wim[:, :], start=True, stop=(ti == 0))
                            if ti > 0:
                                P.mm(cre[:, :], e127[:, :], hre[c][:, :], start=False, stop=True)
                                P.mm(cim[:, :], e127[:, :], him[c][:, :], start=False, stop=True)
                            t1, t2 = tmp_r(), tmp_r()
                            P.tt(t1[:, :], tabs["pr"][:, cs_], cre[:, :], ALU.mult)
                            P.tt(t2[:, :], tabs["pi"][:, cs_], cim[:, :], ALU.mult)
                            P.tt(hre[c][:, :], t1[:, :], t2[:, :], ALU.subtract, e="pool")
                            t3, t4 = tmp_r(), tmp_r()
                            P.tt(t3[:, :], tabs["pr"][:, cs_], cim[:, :], ALU.mult)
                            P.tt(t4[:, :], tabs["pi"][:, cs_], cre[:, :], ALU.mult)
                            P.tt(him[c][:, :], t3[:, :], t4[:, :], ALU.add, e="pool")
                        ck("ck_scan")
                        mix = s5_tail_and_rest(ti, n, xt, xT, None, None, u, False)
                        ck("ck_s5")
                        qT = qT_r()
                        transposes(qT[:, :, :], [qb[0:n, k * 128:(k + 1) * 128] for k in range(4)], n, BF16)
                        transposes(kT[cur][:, :, :], [kvb[cur][0:n, 0:128]], n, BF16)
                        for j8 in range(8):
                            jp, half = j8 // 2, j8 % 2
                            ps_ = slice(half * 64, (half + 1) * 64)
                            S = pbank()
                            c0 = 0 if ti > 0 else 128
                            if ti > 0:
                                P.mm(S[:, 0:128], qT[ps_, jp, :], kT[1 - cur][ps_, 0, :])
                            P.mm(S[:, 128:256], qT[ps_, jp, :], kT[cur][ps_, 0, :])
                            sbv = sb_r()
                            P.tt(sbv[:, c0:256], S[:, c0:256], bias_p[:, j8, c0:256], ALU.add)
                            sm = sm_r()
                            P.red(sm[:, 0:1], sbv[:, c0:256], ALU.max)
                            P.ts(sm[:, 1:2], sm[:, 0:1], sk_p[:, j8:j8 + 1], -1.0, ALU.max, ALU.mult)
                            pbv = pb_r()
                            P.act(pbv[:, c0:256], sbv[:, c0:256], AF.Exp, bias=sm[:, 1:2], accum=sm[:, 2:3])
                            P.act(sm[:, 3:4], sm[:, 1:2], AF.Exp, bias=sk_p[:, j8:j8 + 1])
                            P.tt(sm[:, 4:5], sm[:, 2:3], sm[:, 3:4], ALU.add)
                            P.recip(sm[:, 5:6], sm[:, 4:5])
                            pT = pT_r()
                            srcs = ([pbv[:, 0:128]] if ti > 0 else []) + [pbv[:, 128:256]]
                            transposes(pT[:, (0 if ti > 0 else 1):2, :], srcs, n, BF16)
                            po_ = pbank()
                            if ti > 0:
                                P.mm(po_[:, 0:64], pT[:, 0, :], kvb[1 - cur][:, 128 + half * 64:128 + (half + 1) * 64], start=True, stop=False)
                            P.mm(po_[:, 0:64], pT[:, 1, :], kvb[cur][:, 128 + half * 64:128 + (half + 1) * 64], start=(ti == 0), stop=True)
                            P.ts(mix[:, 512 + j8 * 64:512 + (j8 + 1) * 64], po_[:, 0:64], sm[:, 5:6], None, ALU.mult)
                        ck("ck_swa")
                        mixT = mixT_r()
                        transposes(mixT[:, :, :], [mix[0:n, k * 128:(k + 1) * 128] for k in range(8)], n, BF16)
                        out_and_ln(ti, n, xt, mixT)
                        ck("ck_ln")
                        if ti == 15:
                            for c in range(4):
                                P.dma("sp", D["p_s5_re"][li:li + 1, c * 512:(c + 1) * 512], hre[c][127:128, :], is_output=True)
                                P.dma("sp", D["p_s5_im"][li:li + 1, c * 512:(c + 1) * 512], him[c][127:128, :], is_output=True)
                            P.dma("sp", D["p_swa_k"][li], kvf[:, 0:128], is_output=True)
                            P.dma("sp", D["p_swa_v"][li], kvf[:, 128:256], is_output=True)

                    barrier()
                    s3.close()
                    smid.close()
                    build_tables(one1, None, False)
                    n = 16
                    ti = 16
                    xt = x_r()
                    P.dma("sp", xt[0:n, :], xsrc(first, ti))
                    xT = xT_r()
                    make_xT(xb_r, xT, xt, n)
                    pj = [pbank(), pbank(), pbank()]
                    for j, (c0, cw) in enumerate(((0, 512), (512, 512), (1024, 256))):
                        for k in range(8):
                            P.mm(pj[j][0:n, 0:cw], xT[:, k, 0:n], w_in[:, k, c0:c0 + cw], start=(k == 0), stop=(k == 7))
                    u = u_r(); ub = ub_r()
                    evac(u[0:n, :], pj[0][0:n, :], e="act")
                    evac(ub[0:n, :], pj[0][0:n, :], e="dve")
                    qb = qb_r()
                    P.act(qb[0:n, :], pj[1][0:n, :], AF.Copy, scale=0.125)
                    kvf = kvf_r()
                    evac(kvf[0:n, :], pj[2][0:n, 0:256], e="act")
                    uT = uT_r()
                    transposes(uT[:, :, :], [ub[0:n, k * 128:(k + 1) * 128] for k in range(4)], n, BF16)
                    for c in range(4):
                        cs_ = slice(c * 512, (c + 1) * 512)
                        pre, pim = pbank(), pbank()
                        P.mm(pre[0:n, :], uT[:, c, 0:n], bb_re[:, c, :])
                        P.mm(pim[0:n, :], uT[:, c, 0:n], bb_im[:, c, :])
                        h0r, h0i = w_r(), w_r()
                        P.dma("sp", h0r[0:n, :], D["st_s5_re"][li, :, cs_])
                        P.dma("sp", h0i[0:n, :], D["st_s5_im"][li, :, cs_])
                        t1, t2, t3, t4 = tmp_r(), tmp_r(), tmp_r(), tmp_r()
                        P.tt(t1[0:n, :], tabs["pr"][0:n, cs_], h0r[0:n, :], ALU.mult)
                        P.tt(t2[0:n, :], tabs["pi"][0:n, cs_], h0i[0:n, :], ALU.mult)
                        P.tt(t1[0:n, :], t1[0:n, :], t2[0:n, :], ALU.subtract)
                        P.tt(hre[c][0:n, :], t1[0:n, :], pre[0:n, :], ALU.add)
                        P.tt(t3[0:n, :], tabs["pr"][0:n, cs_], h0i[0:n, :], ALU.mult)
                        P.tt(t4[0:n, :], tabs["pi"][0:n, cs_], h0r[0:n, :], ALU.mult)
                        P.tt(t3[0:n, :], t3[0:n, :], t4[0:n, :], ALU.add)
                        P.tt(him[c][0:n, :], t3[0:n, :], pim[0:n, :], ALU.add)
                        P.dma("sp", D["s_s5_re"][li, :, cs_], hre[c][0:n, :], is_output=True)
                        P.dma("sp", D["s_s5_im"][li, :, cs_], him[c][0:n, :], is_output=True)
                    mix = s5_tail_and_rest(ti, n, xt, xT, None, None, u, True)
                    mixT = mixT_r()
                    transposes(mixT[:, 0:4, :], [mix[0:n, k * 128:(k + 1) * 128] for k in range(4)], n, BF16)
                    with ExitStack() as s4:
                        KN = sbuf(s4, "e_KN", [128, 16, 128]); VN = KN
                        KNb = sbuf(s4, "e_KNb", [128, 16, 128], BF16); VNb = sbuf(s4, "e_VNb", [128, 16, 128], BF16)
                        KTs = sbuf(s4, "e_KTs", [128, 16, 128], BF16)
                        qTs = sbuf(s4, "e_qTs", [128, 4, 16], BF16)
                        sst = sbuf(s4, "e_sst", [128, 128]); ssb = sbuf(s4, "e_ssb", [128, 128])
                        Pn = sbuf(s4, "e_Pn", [128, 128], BF16); PT = sbuf(s4, "e_PT", [128, 1, 128], BF16)
                        for (XN, cache, outn, c0) in ((KN, "cache_k", "s_swa_k", 0), (VN, "cache_v", "s_swa_v", 128)):
                            P.dma("sp", XN[0:127, :, :], D[cache][li].re("b r f -> r b f")[1:128])
                            P.dma("sp", D[outn][li].re("b r f -> r b f")[0:127], XN[0:127, :, :], is_output=True)
                            P.dma("sp", D[outn][li].re("b r f -> r b f")[127], kvf[0:16, c0:c0 + 128], is_output=True)
                            P.dma("sp", XN[127:128, :, :], D[outn][li].re("b r f -> r b f")[127:128])
                            P.copy((KNb if c0 == 0 else VNb)[:, :, :], XN[:, :, :], e="pool")
                        transposes(KTs[:, :, :], [KNb[:, b, :] for b in range(16)], 128, BF16)
                        transposes(qTs[:, :, :], [qb[0:n, k * 128:(k + 1) * 128] for k in range(4)], n, BF16)
                        pst = pbank()
                        for b in range(16):
                            for half in range(2):
                                ps_ = slice(half * 64, (half + 1) * 64)
                                P.mm(pst[:, b * 8 + half:b * 8 + 8:2], KTs[ps_, b, :], qTs[ps_, :, b])
                        evac(sst[:, :], pst[:, 0:128])
                        pS = pbank()
                        P.tr(pS[:, 0:128], sst[:, :], identf[:, :])
                        P.tt(ssb[:, :], pS[:, 0:128], bias_s[:, :], ALU.add)
                        sm = sm_r()
                        P.red(sm[:, 0:1], ssb[:, :], ALU.max)
                        P.ts(sm[:, 1:2], sm[:, 0:1], sk_s[:, 0:1], -1.0, ALU.max, ALU.mult)
                        P.act(ssb[:, :], ssb[:, :], AF.Exp, bias=sm[:, 1:2], accum=sm[:, 2:3])
                        P.act(sm[:, 3:4], sm[:, 1:2], AF.Exp, bias=sk_s[:, 0:1])
                        P.tt(sm[:, 4:5], sm[:, 2:3], sm[:, 3:4], ALU.add)
                        P.recip(sm[:, 5:6], sm[:, 4:5])
                        P.ts(Pn[:, :], ssb[:, :], sm[:, 5:6], None, ALU.mult)
                        transposes(PT[:, :, :], [Pn[:, :]], 128, BF16)
                        poT = pbank()
                        for b in range(16):
                            for half in range(2):
                                ps_ = slice(half * 64, (half + 1) * 64)
                                P.mm(poT[ps_, 0:64].re("p (j b) -> p j b", b=16)[:, :, b], VNb[:, b, ps_], PT[:, 0, b * 8 + half:b * 8 + 8:2])
                        evac(mixT[:, 4:8, 0:16], poT[:, 0:64].re("p (j b) -> p j b", b=16))
                    out_and_ln(ti, n, xt, mixT)
            barrier()

        def peer_phase(layer):
            last = layer == 3
            with ExitStack() as sc:
                wq = sbuf(sc, "p_wq", [128, 8, 2048], BF16)
                k1t = sbuf(sc, "p_k1t", [128, 8, 128], BF16); k2t = sbuf(sc, "p_k2t", [128, 8, 128], BF16)
                lng = sbuf(sc, "p_lng", [128, 1024]); lnb = sbuf(sc, "p_lnb", [128, 1024])
                io16 = sbuf(sc, "p_io16", [128, 16])
                P.dma("pool", wq[:, :, :], D["w_q"][layer].re("(k p) n -> p k n", p=128))
                P.dma("pool", k1t[:, :, :], D["k1t"][layer].re("d (h k) -> d h k", h=8))
                P.dma("pool", k2t[:, :, :], D["k2t"][layer].re("d (h k) -> d h k", h=8))
                P.dma("sp", lng[:, :], bcast_rows(D["ln2_g"], layer, 1024))
                P.dma("sp", lnb[:, :], bcast_rows(D["ln2_b"], layer, 1024))
                P.dma("sp", io16[:, :], D["c_iota16"][:, :])
                x_r = ring(sc, "p_x", [128, 1024], F32, 2)
                xb_r = ring(sc, "p_xb", [128, 1024], BF16, 2)
                xT_r = ring(sc, "p_xT", [128, 8, 128], BF16, 2)
                qT = sbuf(sc, "p_qT", [128, 16, 128], BF16)
                scs = sbuf(sc, "p_sc", [128, 16, 128]); sc2 = sbuf(sc, "p_sc2", [128, 128])
                v16 = sbuf(sc, "p_v16", [128, 16, 16]); i16 = sbuf(sc, "p_i16", [128, 16, 16], U32); i16f = sbuf(sc, "p_i16f", [128, 16, 16])
                cand = sbuf(sc, "p_cand", [128, 8, 256]); cand2 = sbuf(sc, "p_cand2", [128, 256])
                big1 = sbuf(sc, "p_big1", [128, 8, 256]); big2 = sbuf(sc, "p_big2", [128, 8, 256])
                sv = sbuf(sc, "p_sv", [128, 8, 16]); pos = sbuf(sc, "p_pos", [128, 8, 16], U32)
                posf = sbuf(sc, "p_posf", [128, 8, 16]); ikf = sbuf(sc, "p_ikf", [128, 8, 16]); jkf = sbuf(sc, "p_jkf", [128, 8, 16])
                iki = sbuf(sc, "p_iki", [128, 8, 16], I32)
                e1 = sbuf(sc, "p_e1", [128, 8, 16]); e2 = sbuf(sc, "p_e2", [128, 8, 16])
                gat = sbuf(sc, "p_gat", [128, 8, 16]); gsum = sbuf(sc, "p_gsum", [128, 8])
                idxT = sbuf(sc, "p_idxT", [128, 128], I32); gateT = sbuf(sc, "p_gateT", [128, 128])
                hTall = sbuf(sc, "p_hT", [128, 128]); actT = sbuf(sc, "p_actT", [128, 128])
                junk_r = ring(sc, "p_junk", [128, 1024], BF16, 2)
                lhs_r = ring(sc, "p_lhs", [128, 128], BF16, 4)
                U_r = ring(sc, "p_U", [128, 1024], BF16, 6)
                V_r = ring(sc, "p_V", [128, 1024], BF16, 6)
                lnr = {"ln_s": ring(sc, "p_lns", [128, 1024], F32, 2), "ln_st": ring(sc, "p_lnst", [128, 12], F32, 2),
                       "ln_mv": ring(sc, "p_lnmv", [128, 8], F32, 2)}
                utab = D["peer_u"].t.ap().rearrange("l e d -> (l e) d")
                vtab = D["peer_v"].t.ap().rearrange("l e d -> (l e) d")
                nexp = D["peer_u"].t.shape[1]

                for ti in range(NTILES):
                    if debug is not None and debug.startswith("peerfirst") and len(debug) > 9 and ti not in (0, 16):
                        continue
                    n = 128 if ti < 16 else 16
                    xt = x_r()
                    P.dma("sp", xt[0:n, :], xsrc(debug is not None and debug.startswith("peerfirst"), ti))
                    xT = xT_r()
                    xb = xb_r()
                    P.copy(xb[0:n, :], xt[0:n, :], e="pool")
                    transposes(xT[:, :, :], [xb[0:n, k * 128:(k + 1) * 128] for k in range(8)], n, BF16)
                    for f0 in range(0, 16, 4):
                        bk = pbank()
                        for f in range(f0, f0 + 4):
                            for k in range(8):
                                P.mm(bk[:, (f - f0) * 128:(f - f0) * 128 + n], wq[:, k, f * 128:(f + 1) * 128], xT[:, k, 0:n],
                                     start=(k == 0), stop=(k == 7))
                        evac(qT[:, f0:f0 + 4, 0:n], bk[:, :].re("p (j t) -> p j t", t=128)[:, :, 0:n])
                    for f0 in range(0, 16, 4):
                        bk = pbank()
                        for f in range(f0, f0 + 4):
                            kt = k1t if f % 2 == 0 else k2t
                            P.mm(bk[0:n, (f - f0) * 128:(f - f0 + 1) * 128], qT[:, f, 0:n], kt[:, f // 2, :])
                        evac(scs[0:n, f0:f0 + 4, :], bk[0:n, :].re("p (j t) -> p j t", t=128))
                    for f in range(16):
                        P.op("dve", lambda g, f=f: g.max(v16.t[0:n, f, 0:8], scs.t[0:n, f, :]), ins=[scs[:, :, :]], outs=[v16[:, :, :]])
                        P.op("dve", lambda g, f=f: g.max_index(i16.t[0:n, f, 0:8], v16.t[0:n, f, 0:8], scs.t[0:n, f, :]),
                             ins=[scs[:, :, :], v16[:, :, :]], outs=[i16[:, :, :]])
                        P.op("dve", lambda g, f=f: g.match_replace(sc2.t[0:n, :], v16.t[0:n, f, 0:8], scs.t[0:n, f, :], NEG),
                             ins=[scs[:, :, :], v16[:, :, :]], outs=[sc2[:, :]])
                        P.op("dve", lambda g, f=f: g.max(v16.t[0:n, f, 8:16], sc2.t[0:n, :]), ins=[sc2[:, :]], outs=[v16[:, :, :]])
                        P.op("dve", lambda g, f=f: g.max_index(i16.t[0:n, f, 8:16], v16.t[0:n, f, 8:16], sc2.t[0:n, :]),
                             ins=[sc2[:, :], v16[:, :, :]], outs=[i16[:, :, :]])
                    P.copy(i16f[0:n, :, :], i16[0:n, :, :])
                    v4 = v16[0:n, :, :].re("p (h two) k -> p h two k", two=2)
                    i4 = i16f[0:n, :, :].re("p (h two) k -> p h two k", two=2)
                    P.ts(i4[:, :, 0, :], i4[:, :, 0, :], 128.0, None, ALU.mult)
                    c4 = cand[0:n, :, :].re("p h (i j) -> p h i j", j=16)
                    P.tt(c4, v4[:, :, 0, :].re("p h (i o) -> p h i o", o=1).bc([n, 8, 16, 16]),
                         v4[:, :, 1, :].re("p h (o j) -> p h o j", o=1).bc([n, 8, 16, 16]), ALU.add)
                    for h in range(8):
                        P.op("dve", lambda g, h=h: g.max(sv.t[0:n, h, 0:8], cand.t[0:n, h, :]), ins=[cand[:, :, :]], outs=[sv[:, :, :]])
                        P.op("dve", lambda g, h=h: g.max_index(pos.t[0:n, h, 0:8], sv.t[0:n, h, 0:8], cand.t[0:n, h, :]),
                             ins=[cand[:, :, :], sv[:, :, :]], outs=[pos[:, :, :]])
                        P.op("dve", lambda g, h=h: g.match_replace(cand2.t[0:n, :], sv.t[0:n, h, 0:8], cand.t[0:n, h, :], NEG),
                             ins=[cand[:, :, :], sv[:, :, :]], outs=[cand2[:, :]])
                        P.op("dve", lambda g, h=h: g.max(sv.t[0:n, h, 8:16], cand2.t[0:n, :]), ins=[cand2[:, :]], outs=[sv[:, :, :]])
                        P.op("dve", lambda g, h=h: g.max_index(pos.t[0:n, h, 8:16], sv.t[0:n, h, 8:16], cand2.t[0:n, :]),
                             ins=[cand2[:, :], sv[:, :, :]], outs=[pos[:, :, :]])
                    P.copy(posf[0:n, :, :], pos[0:n, :, :])
                    P.ts(ikf[0:n, :, :], posf[0:n, :, :], -7.5, 1.0 / 16, ALU.add, ALU.mult)
                    P.copy(iki[0:n, :, :], ikf[0:n, :, :])
                    P.copy(ikf[0:n, :, :], iki[0:n, :, :])
                    P.stt(jkf[0:n, :, :], ikf[0:n, :, :], -16.0, posf[0:n, :, :], ALU.mult, ALU.add)
                    b1 = big1[0:n, :, :].re("p h (k i) -> p h k i", i=16)
                    b2 = big2[0:n, :, :].re("p h (k i) -> p h k i", i=16)
                    io4 = io16[0:n, :].re("p (a b i) -> p a b i", a=1, b=1).bc([n, 8, 16, 16])
                    for (kf, tab, eo) in ((ikf, 0, e1), (jkf, 1, e2)):
                        P.tt(b1, kf[0:n, :, :].re("p h (k o) -> p h k o", o=1).bc([n, 8, 16, 16]), io4, ALU.is_equal)
                        P.tt(b2, b1, i4[:, :, tab, :].re("p h (o i) -> p h o i", o=1).bc([n, 8, 16, 16]), ALU.mult, e="pool")
                        P.red(eo[0:n, :, :], b2, ALU.add)
                    P.stt(e1[0:n, :, :], e1[0:n, :, :], float(layer * nexp), e2[0:n, :, :], ALU.add, ALU.add)
                    P.tt(gat[0:n, :, :], sv[0:n, :, :], sv[0:n, :, 0:1].bc([n, 8, 16]), ALU.subtract)
                    P.act(gat[0:n, :, :], gat[0:n, :, :], AF.Exp)
                    P.red(gsum[0:n, :], gat[0:n, :, :], ALU.add)
                    P.recip(gsum[0:n, :], gsum[0:n, :])
                    P.tt(gat[0:n, :, :], gat[0:n, :, :], gsum[0:n, :].re("p (h o) -> p h o", o=1).bc([n, 8, 16]), ALU.mult)
                    bk = pbank()
                    P.tr(bk[:, 0:n], e1[0:n, :, :].re("p h k -> p (h k)"), identf[0:n, 0:n])
                    P.tr(bk[:, 128:128 + n], gat[0:n, :, :].re("p h k -> p (h k)"), identf[0:n, 0:n])
                    evac(idxT[:, 0:n], bk[:, 0:n], e="dve")
                    evac(gateT[:, 0:n], bk[:, 128:128 + n], e="act")
                    for t in range(n):
                        Us = U_r()
                        P.dma("pool", Us[:, :], D["peer_u"][layer], extra_ins=[idxT[:, :]],
                              fn=lambda g, Us=Us, t=t: g.indirect_dma_start(
                                  out=Us.t[:, :], out_offset=None, in_=utab,
                                  in_offset=bass.IndirectOffsetOnAxis(ap=idxT.t[:, t:t + 1], axis=0)))
                        px = [pbank(), pbank()]
                        for hh in range(2):
                            P.mm(px[hh][:, :], identb[0:n, t:t + 1].bc([n, 128]), xb[0:n, hh * 512:(hh + 1) * 512])
                        jk = junk_r()
                        P.stt(jk[:, 0:512], Us[:, 0:512], 1.0, px[0][:, :], ALU.mult, ALU.mult, accum=hTall[:, t:t + 1])
                        P.stt(jk[:, 512:1024], Us[:, 512:1024], 1.0, px[1][:, :], ALU.mult, ALU.mult, accum=actT[:, t:t + 1])
                    P.tt(hTall[:, 0:n], hTall[:, 0:n], actT[:, 0:n], ALU.add)
                    P.act(actT[:, 0:n], hTall[:, 0:n], AF.Gelu_apprx_tanh)
                    P.tt(actT[:, 0:n], actT[:, 0:n], gateT[:, 0:n], ALU.mult)
                    po = [banks[6], banks[7]]
                    for t in range(n):
                        Vs = V_r()
                        P.dma("pool", Vs[:, :], D["peer_v"][layer], extra_ins=[idxT[:, :]],
                              fn=lambda g, Vs=Vs, t=t: g.indirect_dma_start(
                                  out=Vs.t[:, :], out_offset=None, in_=vtab,
                                  in_offset=bass.IndirectOffsetOnAxis(ap=idxT.t[:, t:t + 1], axis=0)))
                        lh = lhs_r()
                        P.ts(lh[:, :], b255[:, 127 - t:255 - t], actT[:, t:t + 1], None, ALU.mult, e="pool")
                        for hh in range(2):
                            P.mm(po[hh][:, :], lh[:, :], Vs[:, hh * 512:(hh + 1) * 512], start=(t == 0), stop=(t == n - 1))
                    xo = layernorm(lnr, xt, [po[0][0:n, :], po[1][0:n, :]], n, lng, lnb, None)
                    if last:
                        if ti < 16:
                            P.dma("sp", D["y_p"][ti * 128:(ti + 1) * 128, :], xo[0:n, :], is_output=True)
                        else:
                            P.dma("sp", D["y_s"][:, :], xo[0:n, :], is_output=True)
                    else:
                        P.dma("sp", xd[ti][0:n, :], xo[0:n, :])
            barrier()

        def peer_phase_dense(layer):
            last = layer == 3
            TL = [0, 16] if (debug is not None and debug.startswith("peerfirst") and len(debug) > 9) else list(range(NTILES))
            NTL = len(TL)
            NCOL = NTL * 128
            with ExitStack() as sc:
                lng = sbuf(sc, "p_lng", [128, 1024]); lnb = sbuf(sc, "p_lnb", [128, 1024])
                P.dma("sp", lng[:, :], bcast_rows(D["ln2_g"], layer, 1024))
                P.dma("sp", lnb[:, :], bcast_rows(D["ln2_b"], layer, 1024))
                xT_all = sbuf(sc, "p_xTall", [128, 8, NCOL], BF16)
                P.memset(xT_all[:, :, (NTL - 1) * 128:NCOL], 0.0, e="pool")
                sa = ExitStack()
                sc.callback(sa.close)
                wq = sbuf(sa, "p_wq", [128, 8, 2048], BF16)
                k1t = sbuf(sa, "p_k1t", [128, 8, 128], BF16); k2t = sbuf(sa, "p_k2t", [128, 8, 128], BF16)
                io16 = sbuf(sa, "p_io16", [128, 16])
                io128 = sbuf(sa, "p_io128", [128, 128]); ioc128 = sbuf(sa, "p_ioc128", [128, 128])
                P.dma("pool", wq[:, :, :], D["w_q"][layer].re("(k p) n -> p k n", p=128))
                P.dma("pool", k1t[:, :, :], D["k1t"][layer].re("d (h k) -> d h k", h=8))
                P.dma("pool", k2t[:, :, :], D["k2t"][layer].re("d (h k) -> d h k", h=8))
                P.dma("sp", io16[:, :], D["c_iota16"][:, :])
                P.dma("sp", io128[:, :], D["c_iota128"][:, :])
                P.ts(ioc128[:, :], io128[:, :], 128.0, None, ALU.mult)
                x_r = ring(sa, "p_x", [128, 1024], F32, 2)
                xb_r = ring(sa, "p_xb", [128, 1024], BF16, 2)
                sc2 = sbuf(sa, "p_sc2", [128, 128])
                v16 = sbuf(sa, "p_v16", [128, 16, 16]); i16 = sbuf(sa, "p_i16", [128, 16, 16], U32); i16f = sbuf(sa, "p_i16f", [128, 16, 16])
                cand = sbuf(sa, "p_cand", [128, 8, 256]); cand2 = sbuf(sa, "p_cand2", [128, 256])
                big1 = sbuf(sa, "p_big1", [128, 8, 256]); big2 = sbuf(sa, "p_big2", [128, 8, 256])
                sv = sbuf(sa, "p_sv", [128, 8, 16]); pos = sbuf(sa, "p_pos", [128, 8, 16], U32)
                posf = sbuf(sa, "p_posf", [128, 8, 16]); ikf = sbuf(sa, "p_ikf", [128, 8, 16]); jkf = sbuf(sa, "p_jkf", [128, 8, 16])
                iki = sbuf(sa, "p_iki", [128, 8, 16], I32)
                e1 = sbuf(sa, "p_e1", [128, 8, 16]); e2 = sbuf(sa, "p_e2", [128, 8, 16])
                gat = sbuf(sa, "p_gat", [128, 8, 16]); gsum = sbuf(sa, "p_gsum", [128, 8])
                slotT = sbuf(sa, "p_slotT", [128, 3, 128])
                nslot = sbuf(sa, "p_nslot", [128, 128])
                ab_r = ring(sa, "p_ab", [128, 128], F32, 12)
                At_r = ring(sa, "p_At", [128, 128], BF16, 12)
                Bt_r = ring(sa, "p_Bt", [128, 128], BF16, 12)
                WGt_r = ring(sa, "p_WGt", [128, 128, 128], BF16, 1)
                WG = P.dram(f"wg_scratch_{layer}", [NTL, 128, 128, 128], BF16)

                qT_r2 = ring(sa, "p_qT2", [128, 16, 128], BF16, 2)
                scs_r2 = ring(sa, "p_sc2b", [128, 16, 128], F32, 2)

                def front(tix):
                    ti = TL[tix]
                    n = 128 if ti < 16 else 16
                    qT = qT_r2(); scs = scs_r2()
                    xt = x_r()
                    P.dma("sp", xt[0:n, :], xsrc(debug is not None and debug.startswith("peerfirst"), ti))
                    xb = xb_r()
                    P.copy(xb[0:n, :], xt[0:n, :], e="pool")
                    xT = xT_all[:, :, tix * 128:(tix + 1) * 128]
                    transposes(xT, [xb[0:n, k * 128:(k + 1) * 128] for k in range(8)], n, BF16)
                    for f0 in range(0, 16, 4):
                        bk = pbank()
                        for f in range(f0, f0 + 4):
                            for k in range(8):
                                P.mm(bk[:, (f - f0) * 128:(f - f0) * 128 + n], wq[:, k, f * 128:(f + 1) * 128], xT[:, k, 0:n],
                                     start=(k == 0), stop=(k == 7))
                        evac(qT[:, f0:f0 + 4, 0:n], bk[:, :].re("p (j t) -> p j t", t=128)[:, :, 0:n])
                    for f0 in range(0, 16, 4):
                        bk = pbank()
                        for f in range(f0, f0 + 4):
                            kt = k1t if f % 2 == 0 else k2t
                            P.mm(bk[0:n, (f - f0) * 128:(f - f0 + 1) * 128], qT[:, f, 0:n], kt[:, f // 2, :])
                        evac(scs[0:n, f0:f0 + 4, :], bk[0:n, :].re("p (j t) -> p j t", t=128))
                    return dict(n=n, scs=scs, tix=tix)

                def mid(cx):
                    n, scs, tix = cx["n"], cx["scs"], cx["tix"]
                    for f in range(16):
                        P.op("dve", lambda g, f=f: g.max(v16.t[0:n, f, 0:8], scs.t[0:n, f, :]), ins=[scs[:, :, :]], outs=[v16[:, :, :]])
                        P.op("dve", lambda g, f=f: g.max_index(i16.t[0:n, f, 0:8], v16.t[0:n, f, 0:8], scs.t[0:n, f, :]),
                             ins=[scs[:, :, :], v16[:, :, :]], outs=[i16[:, :, :]])
                        P.op("dve", lambda g, f=f: g.match_replace(sc2.t[0:n, :], v16.t[0:n, f, 0:8], scs.t[0:n, f, :], NEG),
                             ins=[scs[:, :, :], v16[:, :, :]], outs=[sc2[:, :]])
                        P.op("dve", lambda g, f=f: g.max(v16.t[0:n, f, 8:16], sc2.t[0:n, :]), ins=[sc2[:, :]], outs=[v16[:, :, :]])
                        P.op("dve", lambda g, f=f: g.max_index(i16.t[0:n, f, 8:16], v16.t[0:n, f, 8:16], sc2.t[0:n, :]),
                             ins=[sc2[:, :], v16[:, :, :]], outs=[i16[:, :, :]])
                    P.copy(i16f[0:n, :, :], i16[0:n, :, :])
                    v4 = v16[0:n, :, :].re("p (h two) k -> p h two k", two=2)
                    i4 = i16f[0:n, :, :].re("p (h two) k -> p h two k", two=2)
                    c4 = cand[0:n, :, :].re("p h (i j) -> p h i j", j=16)
                    P.tt(c4, v4[:, :, 0, :].re("p h (i o) -> p h i o", o=1).bc([n, 8, 16, 16]),
                         v4[:, :, 1, :].re("p h (o j) -> p h o j", o=1).bc([n, 8, 16, 16]), ALU.add)
                    for h in range(8):
                        P.op("dve", lambda g, h=h: g.max(sv.t[0:n, h, 0:8], cand.t[0:n, h, :]), ins=[cand[:, :, :]], outs=[sv[:, :, :]])
                        P.op("dve", lambda g, h=h: g.max_index(pos.t[0:n, h, 0:8], sv.t[0:n, h, 0:8], cand.t[0:n, h, :]),
                             ins=[cand[:, :, :], sv[:, :, :]], outs=[pos[:, :, :]])
                        P.op("dve", lambda g, h=h: g.match_replace(cand2.t[0:n, :], sv.t[0:n, h, 0:8], cand.t[0:n, h, :], NEG),
                             ins=[cand[:, :, :], sv[:, :, :]], outs=[cand2[:, :]])
                        P.op("dve", lambda g, h=h: g.max(sv.t[0:n, h, 8:16], cand2.t[0:n, :]), ins=[cand2[:, :]], outs=[sv[:, :, :]])
                        P.op("dve", lambda g, h=h: g.max_index(pos.t[0:n, h, 8:16], sv.t[0:n, h, 8:16], cand2.t[0:n, :]),
                             ins=[cand2[:, :], sv[:, :, :]], outs=[pos[:, :, :]])
                    P.copy(posf[0:n, :, :], pos[0:n, :, :])
                    P.ts(ikf[0:n, :, :], posf[0:n, :, :], -7.5, 1.0 / 16, ALU.add, ALU.mult)
                    P.copy(iki[0:n, :, :], ikf[0:n, :, :])
                    P.copy(ikf[0:n, :, :], iki[0:n, :, :])
                    P.stt(jkf[0:n, :, :], ikf[0:n, :, :], -16.0, posf[0:n, :, :], ALU.mult, ALU.add)
                    b1 = big1[0:n, :, :].re("p h (k i) -> p h k i", i=16)
                    b2 = big2[0:n, :, :].re("p h (k i) -> p h k i", i=16)
                    io4 = io16[0:n, :].re("p (a b i) -> p a b i", a=1, b=1).bc([n, 8, 16, 16])
                    for (kf, tab, eo) in ((ikf, 0, e1), (jkf, 1, e2)):
                        P.tt(b1, kf[0:n, :, :].re("p h (k o) -> p h k o", o=1).bc([n, 8, 16, 16]), io4, ALU.is_equal)
                        P.tt(b2, b1, i4[:, :, tab, :].re("p h (o i) -> p h o i", o=1).bc([n, 8, 16, 16]), ALU.mult, e="pool")
                        P.red(eo[0:n, :, :], b2, ALU.add)
                    P.tt(gat[0:n, :, :], sv[0:n, :, :], sv[0:n, :, 0:1].bc([n, 8, 16]), ALU.subtract)
                    P.act(gat[0:n, :, :], gat[0:n, :, :], AF.Exp)
                    P.red(gsum[0:n, :], gat[0:n, :, :], ALU.add)
                    P.recip(gsum[0:n, :], gsum[0:n, :])
                    P.tt(gat[0:n, :, :], gat[0:n, :, :], gsum[0:n, :].re("p (h o) -> p h o", o=1).bc([n, 8, 16]), ALU.mult)

                def back(cx):
                    n, tix = cx["n"], cx["tix"]
                    bk = pbank()
                    P.tr(bk[:, 0:n], e1[0:n, :, :].re("p h k -> p (h k)"), identf[0:n, 0:n])
                    P.tr(bk[:, 128:128 + n], e2[0:n, :, :].re("p h k -> p (h k)"), identf[0:n, 0:n])
                    P.tr(bk[:, 256:256 + n], gat[0:n, :, :].re("p h k -> p (h k)"), identf[0:n, 0:n])
                    evac(slotT[:, :, 0:n], bk[:, 0:384].re("p (j t) -> p j t", t=128)[:, :, 0:n], e="act")
                    P.ts(nslot[:, 0:n], slotT[:, 1, 0:n], -1.0, None, ALU.mult)
                    WGt = WGt_r()
                    if n < 128:
                        P.memset(WGt[:, :, :], 0.0, e="pool")
                    for t0 in range(0, n, 4):
                        pG = pbank()
                        for t in range(t0, t0 + 4):
                            At = At_r(); Bt = Bt_r()
                            P.ts(At[:, :], io128[:, :], slotT[:, 0, t:t + 1], slotT[:, 2, t:t + 1], ALU.is_equal, ALU.mult)
                            if t % 3 == 2:
                                P.ts(Bt[:, :], io128[:, :], slotT[:, 1, t:t + 1], None, ALU.is_equal)
                            else:
                                ab = ab_r()
                                P.act(ab[:, :], io128[:, :], AF.Abs, bias=nslot[:, t:t + 1])
                                P.act(Bt[:, :], ab[:, :], AF.Relu, scale=-1.0, bias=1.0)
                            P.mm(pG[:, (t - t0) * 128:(t - t0 + 1) * 128], Bt[:, :], At[:, :])
                        evac(WGt[:, :, t0:t0 + 4], pG[:, :].re("p (t c) -> p c t", c=128))
                    P.dma("sp", WG[tix], WGt[:, :, :])

                ev["force"] = "act"
                cxs = {0: front(0)}
                ev["force"] = None
                for tix in range(NTL):
                    if tix + 1 < NTL:
                        ev["force"] = "act"
                        cxs[tix + 1] = front(tix + 1)
                        ev["force"] = None
                    mid(cxs[tix])
                    back(cxs[tix])
                barrier()
                sa.close()

                G = 4
                acc = [sbuf(sc, f"p_acc{i}", [128, 1024]) for i in range(NTL)]
                UT_r = ring(sc, "p_UT", [128, 8, G * 128], BF16, 2)
                Vg_r = ring(sc, "p_Vg", [128, G, 1024], BF16, 2)
                WGg_r = ring(sc, "p_WGg", [128, NTL, G, 128], BF16, 1)
                gb_r = ring(sc, "p_gb", [128, NCOL], BF16, 2)
                Ag_r = ring(sc, "p_Ag", [128, NCOL], BF16, G + 1)
                x_r = ring(sc, "p_x2", [128, 1024], F32, 2)
                lnr = {"ln_s": ring(sc, "p_lns", [128, 1024], F32, 1), "ln_st": ring(sc, "p_lnst", [128, 12], F32, 2),
                       "ln_mv": ring(sc, "p_lnmv", [128, 8], F32, 2)}
                ut_l = D["peer_ut"][layer].re("(k p) e -> p k e", p=128)
                v_l = D["peer_v"][layer]
                nblk = [(c0, min(512, NCOL - c0)) for c0 in range(0, NCOL, 512)]
                tblocks = [list(range(i, min(i + 3, NTL))) for i in range(0, NTL, 3)]
                for gi in range(128 // G):
                    c0 = gi * G
                    UTg = UT_r(); Vg = Vg_r(); WGg = WGg_r()
                    P.dma("pool", UTg[:, :, :], ut_l[:, :, c0 * 128:(c0 + G) * 128])
                    P.dma("pool", Vg[:, :, :], v_l[c0 * 128:(c0 + G) * 128, :].re("(c e) d -> e c d", e=128))
                    P.dma("sp", WGg[:, :, :, :], WG[:, :, c0:c0 + G, :].re("i e c t -> e i c t"))
                    Ags = []
                    for ci in range(G):
                        gb = gb_r()
                        for bi, (n0, nw) in enumerate(nblk):
                            ph = banks[6 + bi % 2]
                            for k in range(8):
                                P.mm(ph[:, 0:nw], UTg[:, k, ci * 128:(ci + 1) * 128], xT_all[:, k, n0:n0 + nw], start=(k == 0), stop=(k == 7))
                            P.act(gb[:, n0:n0 + nw], ph[:, 0:nw], AF.Gelu_apprx_tanh)
                        Ag = Ag_r()
                        P.tt(Ag[:, :].re("p (i t) -> p i t", t=128), gb[:, :].re("p (i t) -> p i t", t=128), WGg[:, :, ci, :], ALU.mult, e="pool")
                        Ags.append(Ag)
                    for tb in tblocks:
                        for j, tix in enumerate(tb):
                            for hh in range(2):
                                for ci in range(G):
                                    P.mm(banks[j * 2 + hh][:, :], Ags[ci][:, tix * 128:(tix + 1) * 128], Vg[:, ci, hh * 512:(hh + 1) * 512],
                                         start=(ci == 0), stop=(ci == G - 1))
                        for j, tix in enumerate(tb):
                            for hh in range(2):
                                hs = slice(hh * 512, (hh + 1) * 512)
                                if gi == 0:
                                    evac(acc[tix][:, hs], banks[j * 2 + hh][:, :])
                                else:
                                    P.tt(acc[tix][:, hs], acc[tix][:, hs], banks[j * 2 + hh][:, :], ALU.add)
                for tix, ti in enumerate(TL):
                    n = 128 if ti < 16 else 16
                    xt = x_r()
                    P.dma("sp", xt[0:n, :], xsrc(debug is not None and debug.startswith("peerfirst"), ti))
                    xo = layernorm(lnr, xt, [acc[tix][0:n, 0:512], acc[tix][0:n, 512:1024]], n, lng, lnb, None)
                    if last:
                        if ti < 16:
                            P.dma("sp", D["y_p"][ti * 128:(ti + 1) * 128, :], xo[0:n, :], is_output=True)
                        else:
                            P.dma("sp", D["y_s"][:, :], xo[0:n, :], is_output=True)
                    else:
                        P.dma("sp", xd[ti][0:n, :], xo[0:n, :])
            barrier()

        def odd_phase(layer):
            li = layer // 2
            NB = 2
            TB = NB * 128
            win = D["w_in_o"][li]

            with ExitStack() as sc:
                w_out = sbuf(sc, "o_wout", [128, 16, 1024], BF16)
                cw = sbuf(sc, "o_cw", [128, 96]); cb = sbuf(sc, "o_cb", [128, 24])
                dtb = sbuf(sc, "o_dtb", [128, 32]); aneg = sbuf(sc, "o_aneg", [128, 32]); dsk = sbuf(sc, "o_dsk", [128, 32])
                normw = sbuf(sc, "o_normw", [128, 2048])
                lng = sbuf(sc, "o_lng", [128, 1024]); lnb = sbuf(sc, "o_lnb", [128, 1024])
                maskneg = sbuf(sc, "o_mask", [128, 128])
                P.dma("pool", w_out[:, :, :], D["w_out_o"][li].re("(k p) n -> p k n", p=128))
                P.dma("sp", cw[:, :], D["conv_w"][li]); P.dma("sp", cb[:, :], D["conv_b"][li])
                P.dma("sp", dtb[:, :], bcast_rows(D["dt_bias"], li, 32))
                P.dma("sp", aneg[:, :], bcast_rows(D["a_log"], li, 32))
                P.act(aneg[:, :], aneg[:, :], AF.Exp)
                P.ts(aneg[:, :], aneg[:, :], -1.0, None, ALU.mult)
                P.dma("sp", dsk[:, :], bcast_rows(D["ssd_d"], li, 32))
                P.dma("sp", normw[:, :], bcast_rows(D["norm_w"], li, 2048))
                P.dma("sp", lng[:, :], bcast_rows(D["ln1_g"], layer, 1024))
                P.dma("sp", lnb[:, :], bcast_rows(D["ln1_b"], layer, 1024))
                P.dma("sp", maskneg[:, :], D["c_maskneg"][:, :])
                Wb_r = ring(sc, "o_wb", [128, 8, 512], BF16, 2)
                dg_r = ring(sc, "o_dg", [128, 4, 128], BF16, 4)
                lnr = {"ln_s": ring(sc, "o_lns", [128, 1024], F32, 1), "ln_st": ring(sc, "o_lnst", [128, 12], F32, 2),
                       "ln_mv": ring(sc, "o_lnmv", [128, 8], F32, 2)}
                x_r = ring(sc, "o_x", [128, 1024], F32, 2)
                xb_r = ring(sc, "o_xb", [128, 1024], BF16, 2)
                sm_r = ring(sc, "o_sm", [128, 32], F32, 12)
                y_r = ring(sc, "o_y", [128, 2048], F32, 2)
                yb_r = ring(sc, "o_yb", [128, 2048], BF16, 1)
                ynT_r = ring(sc, "o_ynT", [128, 16, 128], BF16, 1)

                wbf = P.dram(f"wino_bf16_{layer}", [1024, 5152], BF16)
                with ExitStack() as s0:
                    stg_r = ring(s0, "o_stg", [128, 8, 512], F32, 2)
                    stb_r = ring(s0, "o_stb", [128, 8, 512], BF16, 2)
                    for blk in range(11):
                        c0 = blk * 512
                        cwid = min(512, 5152 - c0)
                        stg = stg_r(); stb = stb_r()
                        P.dma("sp", stg[:, :, 0:cwid], win[:, c0:c0 + cwid].re("(k p) n -> p k n", p=128))
                        P.copy(stb[:, :, 0:cwid], stg[:, :, 0:cwid], e=("act" if blk % 2 == 0 else "dve"))
                        P.dma("sp", wbf[:, c0:c0 + cwid].re("(k p) n -> p k n", p=128), stb[:, :, 0:cwid])
                    barrier()

                def load_w(c0, cwid):
                    Wb = Wb_r()
                    P.dma("sp", Wb[:, :, 0:cwid], wbf[:, c0:c0 + cwid].re("(k p) n -> p k n", p=128))
                    return Wb

                def softplus_dt(pdt, n):
                    a, b_, c_, d_ = sm_r(), sm_r(), sm_r(), sm_r()
                    P.tt(a[0:n, :], pdt, dtb[0:n, :], ALU.add)
                    P.stt(b_[0:n, :], a[0:n, :], -1.0, a[0:n, :], ALU.mult, ALU.max)
                    P.act(c_[0:n, :], b_[0:n, :], AF.Exp, scale=-1.0)
                    P.act(c_[0:n, :], c_[0:n, :], AF.Ln, bias=1.0)
                    P.stt(d_[0:n, :], a[0:n, :], 0.0, c_[0:n, :], ALU.max, ALU.add)
                    return d_

                def make_dg(ct):
                    dg = dg_r()
                    for tap in range(4):
                        P.ts(dg[:, tap, :], identb[:, :], cw[:, ct * 4 + tap:ct * 4 + tap + 1], None, ALU.mult)
                    return dg

                def gate_norm_out(ti, n, y, xs_view, zs_view, xt):
                    t = y_r()
                    P.tt(t[0:n, :].re("p (h q) -> p h q", q=64), xs_view, dsk[0:n, :].re("p (h o) -> p h o", o=1).bc([n, 32, 64]),
                         ALU.mult, e="pool")
                    P.tt(y[0:n, :], y[0:n, :], t[0:n, :], ALU.add, e="pool")
                    P.tt(y[0:n, :], y[0:n, :], zs_view, ALU.mult)
                    ss = sm_r()
                    P.act(t[0:n, :], y[0:n, :], AF.Square, accum=ss[0:n, 0:1])
                    P.ts(ss[0:n, 1:2], ss[0:n, 0:1], 1.0 / 2048, 1e-5, ALU.mult, ALU.add)
                    P.act(ss[0:n, 2:3], ss[0:n, 1:2], AF.Sqrt)
                    P.recip(ss[0:n, 3:4], ss[0:n, 2:3])
                    yb = yb_r()
                    P.stt(yb[0:n, :], y[0:n, :], ss[0:n, 3:4], normw[0:n, :], ALU.mult, ALU.mult)
                    ynT = ynT_r()
                    transposes(ynT[:, :, :], [yb[0:n, k * 128:(k + 1) * 128] for k in range(16)], n, BF16)
                    po = [banks[6], banks[7]]
                    for hh in range(2):
                        for k in range(16):
                            P.mm(po[hh][0:n, :], ynT[:, k, 0:n], w_out[:, k, hh * 512:(hh + 1) * 512], start=(k == 0), stop=(k == 15))
                    xo = layernorm(lnr, xt, [po[0][0:n, :], po[1][0:n, :]], n, lng, lnb, None)
                    P.dma("sp", xd[ti][0:n, :], xo[0:n, :])

                with ExitStack() as s3:
                    xT = sbuf(s3, "o_xT", [128, 8, TB], BF16)
                    xbc_r = ring(s3, "o_xbc", [128, 3 + TB], BF16, 4)
                    halo = sbuf(s3, "o_halo", [128, 24, 3], BF16)
                    P.memset(halo[:, :, :], 0.0)
                    xaT = sbuf(s3, "o_xaT", [128, 24, TB], BF16)
                    zs = sbuf(s3, "o_zs", [128, NB, 2048], BF16)
                    raw3 = sbuf(s3, "o_raw3", [128, 24, 3])
                    dts = [sbuf(s3, f"o_dt{j}", [128, 32]) for j in range(NB)]
                    ST = [sbuf(s3, f"o_ST{g}", [128, 512]) for g in range(4)]
                    STb = [sbuf(s3, f"o_STb{g}", [128, 512], BF16) for g in range(4)]
                    xs_r = ring(s3, "o_xs", [128, 2048], BF16, 2)
                    Bt_r = ring(s3, "o_Bt", [128, 512], BF16, 2)
                    dxb_r = ring(s3, "o_dxb", [128, 2048], BF16, 2)
                    dxs_r = ring(s3, "o_dxs", [128, 2048], BF16, 2)
                    cbT_r = ring(s3, "o_cbT", [128, 128], F32, 2)
                    dAb_r = ring(s3, "o_dAb", [128, 128], F32, 6)
                    seg_r = ring(s3, "o_seg", [128, 128], F32, 6)
                    L_r = ring(s3, "o_L", [128, 128], F32, 6)
                    MT_r = ring(s3, "o_MT", [128, 128], BF16, 6)
                    t1_r = ring(s3, "o_t1", [128, 512], F32, 2)

                    def ssd_tile(ti, j, xt):
                        tsl = slice(j * 128, (j + 1) * 128)
                        dt = dts[j]
                        xs = xs_r()
                        transposes(xs[:, :].re("t (c q) -> t c q", q=128), [xaT[:, ct, tsl] for ct in range(16)], 128, BF16)
                        Bt = Bt_r()
                        transposes(Bt[:, :].re("t (c q) -> t c q", q=128), [xaT[:, 16 + g, tsl] for g in range(4)], 128, BF16)
                        dA, acs, nacs, eacs, eal, decs = sm_r(), sm_r(), sm_r(), sm_r(), sm_r(), sm_r()
                        P.tt(dA[:, :], dt[:, :], aneg[:, :], ALU.mult)
                        pa = pbank()
                        P.mm(pa[:, 0:32], triT[:, :], dA[:, :])
                        evac(acs[:, :], pa[:, 0:32], e="dve")
                        P.ts(nacs[:, :], acs[:, :], -1.0, None, ALU.mult)
                        P.act(eacs[:, :], acs[:, :], AF.Exp)
                        pl = pbank()
                        P.mm(pl[:, 0:32], e127[:, :], acs[:, :])
                        P.act(eal[:, :], pl[:, 0:32], AF.Exp)
                        P.tt(decs[:, :], pl[:, 0:32], acs[:, :], ALU.subtract)
                        P.act(decs[:, :], decs[:, :], AF.Exp)
                        xs3 = xs[:, :].re("p (h q) -> p h q", q=64)
                        dxb = dxb_r(); dxs = dxs_r()
                        P.tt(dxb[:, :].re("p (h q) -> p h q", q=64), xs3, dt[:, :].re("p (h o) -> p h o", o=1).bc([128, 32, 64]), ALU.mult)
                        P.tt(dxs[:, :].re("p (h q) -> p h q", q=64), dxb[:, :].re("p (h q) -> p h q", q=64),
                             decs[:, :].re("p (h o) -> p h o", o=1).bc([128, 32, 64]), ALU.mult, e="pool")
                        y = y_r()
                        for g in range(4):
                            pcb = pbank()
                            P.mm(pcb[:, 0:128], xaT[:, 16 + g, tsl], xaT[:, 20 + g, tsl])
                            cbT = cbT_r()
                            evac(cbT[:, :], pcb[:, 0:128])
                            pyd, pyo = banks[6], banks[7]
                            for hl in range(8):
                                h = g * 8 + hl
                                dAb = dAb_r()
                                P.copy(dAb[:, :], dA[:, h:h + 1].bc([128, 128]), e="pool")
                                pbc = pbank()
                                P.mm(pbc[:, 0:128], dAb[:, :], triT[:, :])
                                seg = seg_r()
                                P.tt(seg[:, :], pbc[:, 0:128], maskneg[:, :], ALU.add)
                                L = L_r()
                                P.act(L[:, :], seg[:, :], AF.Exp, bias=nacs[:, h:h + 1])
                                MT = MT_r()
                                P.tt(MT[:, :], L[:, :], cbT[:, :], ALU.mult, e="pool")
                                P.mm(pyd[:, hl * 64:(hl + 1) * 64], MT[:, :], dxb[:, h * 64:(h + 1) * 64])
                            gs = slice(g * 512, (g + 1) * 512)
                            if ti > 0:
                                P.mm(pyo[:, :], xaT[:, 20 + g, tsl], STb[g][:, :])
                                t1 = t1_r()
                                P.tt(t1[:, :].re("p (h q) -> p h q", q=64), pyo[:, :].re("p (h q) -> p h q", q=64),
                                     eacs[:, g * 8:(g + 1) * 8].re("p (h o) -> p h o", o=1).bc([128, 8, 64]), ALU.mult)
                                P.tt(y[:, gs], t1[:, :], pyd[:, :], ALU.add)
                            else:
                                evac(y[:, gs], pyd[:, :])
                            pst = pbank()
                            P.mm(pst[:, :], Bt[:, g * 128:(g + 1) * 128], dxs[:, gs])
                            if ti > 0:
                                t2 = t1_r()
                                P.tt(t2[:, :].re("p (h q) -> p h q", q=64), ST[g][:, :].re("p (h q) -> p h q", q=64),
                                     eal[:, g * 8:(g + 1) * 8].re("p (h o) -> p h o", o=1).bc([128, 8, 64]), ALU.mult, e="pool")
                                P.tt(ST[g][:, :], t2[:, :], pst[:, :], ALU.add)
                            else:
                                evac(ST[g][:, :], pst[:, :], e="dve")
                            P.copy(STb[g][:, :], ST[g][:, :], e="pool")
                        gate_norm_out(ti, 128, y, xs3, zs[:, j, :], xt)

                    for bi in range(16 // NB):
                        tiles_ = [bi * NB + j for j in range(NB)]
                        xts = []
                        for j, ti in enumerate(tiles_):
                            xt = x_r()
                            P.dma("sp", xt[:, :], xsrc(debug == "oddfirst", ti))
                            xb = xb_r()
                            P.copy(xb[:, :], xt[:, :], e="pool")
                            transposes(xT[:, :, j * 128:(j + 1) * 128], [xb[:, k * 128:(k + 1) * 128] for k in range(8)], 128, BF16)
                            xts.append(xt)
                        for blk in range(4):
                            Wb = load_w(blk * 512, 512)
                            for j in range(NB):
                                pz = pbank()
                                for k in range(8):
                                    P.mm(pz[:, :], xT[:, k, j * 128:(j + 1) * 128], Wb[:, k, :], start=(k == 0), stop=(k == 7))
                                P.act(zs[:, j, blk * 512:(blk + 1) * 512], pz[:, :], AF.Silu)
                        Wd = load_w(5120, 32)
                        for j in range(NB):
                            pdt = pbank()
                            for k in range(8):
                                P.mm(pdt[:, 0:32], xT[:, k, j * 128:(j + 1) * 128], Wd[:, k, 0:32], start=(k == 0), stop=(k == 7))
                            d_ = softplus_dt(pdt[:, 0:32], 128)
                            P.copy(dts[j][:, :], d_[:, :])
                        for blk in range(4, 10):
                            Wb = load_w(blk * 512, 512)
                            for sub in range(4):
                                ct = (blk - 4) * 4 + sub
                                px = pbank()
                                for k in range(8):
                                    P.mm(px[:, 0:TB], Wb[:, k, sub * 128:(sub + 1) * 128], xT[:, k, :], start=(k == 0), stop=(k == 7))
                                xbc = xbc_r()
                                P.copy(xbc[:, 0:3], halo[:, ct, :], e="pool")
                                evac(xbc[:, 3:3 + TB], px[:, 0:TB])
                                if bi == 16 // NB - 1:
                                    P.copy(raw3[:, ct, :], px[:, TB - 3:TB])
                                P.copy(halo[:, ct, :], xbc[:, TB:TB + 3], e="pool")
                                dg = make_dg(ct)
                                pc = pbank()
                                for tap in range(4):
                                    P.mm(pc[:, 0:TB], dg[:, tap, :], xbc[:, tap:tap + TB], start=(tap == 0), stop=(tap == 3))
                                P.act(xaT[:, ct, :], pc[:, 0:TB], AF.Silu, bias=cb[:, ct:ct + 1])
                        for j, ti in enumerate(tiles_):
                            ssd_tile(ti, j, xts[j])
                    osb_r = ring(s3, "o_osb", [128, 4, 128], F32, 2)
                    for g in range(4):
                        osb = osb_r()
                        transposes(osb[:, :, :], [ST[g][:, k * 128:(k + 1) * 128] for k in range(4)], 128, F32)
                        P.dma("sp", D["p_ssd"][li, g * 512:(g + 1) * 512, :].re("(k p) n -> p k n", p=128), osb[:, :, :], is_output=True)
                    for g in range(6):
                        osb = osb_r()
                        transposes(osb[:, :, :], [raw3[:, ct, :] for ct in range(4 * g, 4 * g + 4)], 128, F32)
                        P.dma("sp", D["p_conv"][li][:, g * 512:(g + 1) * 512].re("r (c q) -> r c q", q=128), osb[0:3, :, :], is_output=True)
                barrier()

                with ExitStack() as s4:
                    n = 16
                    ti = 16
                    xT = sbuf(s4, "os_xT", [128, 8, 16], BF16)
                    zs = sbuf(s4, "os_zs", [128, 2048], BF16)
                    xaT = sbuf(s4, "os_xaT", [128, 24, 16], BF16)
                    hist_r = ring(s4, "os_hist", [48, 512], F32, 2)
                    histT_r = ring(s4, "os_histT", [128, 4, 48], BF16, 2)
                    raw_r = ring(s4, "os_raw", [16, 512], F32, 2)
                    xbs_r = ring(s4, "os_xbs", [128, 16], BF16, 2)
                    xs = sbuf(s4, "os_xs", [16, 3072], BF16)
                    xsf = sbuf(s4, "os_xsf", [16, 3072])
                    dtx = sbuf(s4, "os_dtx", [16, 2048]); eexp = sbuf(s4, "os_eexp", [16, 2048])
                    eT = sbuf(s4, "os_eT", [128, 16, 16]); dtxT = sbuf(s4, "os_dtxT", [128, 16, 16])
                    yT = sbuf(s4, "os_yT", [128, 16, 16])
                    e16 = sbuf(s4, "os_e16", [16, 16, 128])
                    P.dma("sp", e16[:, :, :], D["c_e16"][:, :].re("p (b m) -> p b m", m=128))
                    S0_r = ring(s4, "os_S0", [128, 16, 128], F32, 2)
                    Sn_r = ring(s4, "os_Sn", [128, 16, 128], F32, 1)
                    tt_r = ring(s4, "os_tt", [128, 128], F32, 3)
                    jk_r = ring(s4, "os_jk", [128, 128], F32, 2)
                    xt = x_r()
                    P.dma("sp", xt[0:n, :], xsrc(debug == "oddfirst", ti))
                    xb = xb_r()
                    P.copy(xb[0:n, :], xt[0:n, :], e="pool")
                    transposes(xT[:, :, :], [xb[0:n, k * 128:(k + 1) * 128] for k in range(8)], n, BF16)
                    for blk in range(4):
                        Wb = load_w(blk * 512, 512)
                        pz = pbank()
                        for k in range(8):
                            P.mm(pz[0:n, :], xT[:, k, 0:n], Wb[:, k, :], start=(k == 0), stop=(k == 7))
                        P.act(zs[0:n, blk * 512:(blk + 1) * 512], pz[0:n, :], AF.Silu)
                    Wd = load_w(5120, 32)
                    pdt = pbank()
                    for k in range(8):
                        P.mm(pdt[0:n, 0:32], xT[:, k, 0:n], Wd[:, k, 0:32], start=(k == 0), stop=(k == 7))
                    dt = softplus_dt(pdt[0:n, 0:32], n)
                    P.dma("sp", D["s_conv"][li, :, 0:2, :], D["st_conv"][li, :, 1:3, :], is_output=True)
                    for blk in range(4, 10):
                        Wb = load_w(blk * 512, 512)
                        c0 = (blk - 4) * 512
                        praw = pbank()
                        for k in range(8):
                            P.mm(praw[0:n, :], xT[:, k, 0:n], Wb[:, k, :], start=(k == 0), stop=(k == 7))
                        raw = raw_r()
                        evac(raw[0:n, :], praw[0:n, :])
                        P.dma("sp", D["s_conv"][li, :, 2, c0:c0 + 512], raw[0:n, :], is_output=True)
                        hist = hist_r()
                        P.dma("sp", hist[:, :], D["st_conv"][li].re("b r c -> (b r) c")[:, c0:c0 + 512])
                        histT = histT_r()
                        transposes(histT[:, :, :], [hist[0:48, s_ * 128:(s_ + 1) * 128] for s_ in range(4)], 48, F32)
                        for sub in range(4):
                            ct = (blk - 4) * 4 + sub
                            px = pbank()
                            for k in range(8):
                                P.mm(px[:, 0:n], Wb[:, k, sub * 128:(sub + 1) * 128], xT[:, k, 0:n], start=(k == 0), stop=(k == 7))
                            xbs = xbs_r()
                            evac(xbs[:, :], px[:, 0:n])
                            dg = make_dg(ct)
                            pc = pbank()
                            for tap in range(3):
                                P.mm(pc[:, 0:n], dg[:, tap, :], histT[:, sub, tap:48:3], start=(tap == 0), stop=False)
                            P.mm(pc[:, 0:n], dg[:, 3, :], xbs[:, :], start=False, stop=True)
                            P.act(xaT[:, ct, :], pc[:, 0:n], AF.Silu, bias=cb[:, ct:ct + 1])
                    transposes(xs[:, :].re("t (c q) -> t c q", q=128), [xaT[:, ct, :] for ct in range(24)], 128, BF16)
                    P.copy(xsf[0:n, :], xs[0:n, :])
                    dA, ee = sm_r(), sm_r()
                    P.tt(dA[0:n, :], dt[0:n, :], aneg[0:n, :], ALU.mult)
                    P.act(ee[0:n, :], dA[0:n, :], AF.Exp)
                    xs3 = xsf[0:n, 0:2048].re("p (h q) -> p h q", q=64)
                    P.tt(dtx[0:n, :].re("p (h q) -> p h q", q=64), xs3, dt[0:n, :].re("p (h o) -> p h o", o=1).bc([n, 32, 64]), ALU.mult)
                    P.copy(eexp[0:n, :].re("p (h q) -> p h q", q=64), ee[0:n, :].re("p (h o) -> p h o", o=1).bc([n, 32, 64]), e="pool")
                    transposes(eT[:, :, :], [eexp[0:n, k * 128:(k + 1) * 128] for k in range(16)], n, F32)
                    transposes(dtxT[:, :, :], [dtx[0:n, k * 128:(k + 1) * 128] for k in range(16)], n, F32)
                    for b in range(16):
                        pB, pC = pbank(), pbank()
                        P.mm(pB[:, :], e16[:, b, :], xsf[0:n, 2048:2560])
                        P.mm(pC[:, :], e16[:, b, :], xsf[0:n, 2560:3072])
                        S0 = S0_r(); Sn = Sn_r()
                        P.dma("sp", S0[:, :, :], D["st_ssd"][li, b].re("(j q) n -> q j n", q=128))
                        for j in range(16):
                            g = j // 4
                            t1 = tt_r()
                            P.act(t1[:, :], S0[:, j, :], AF.Copy, scale=eT[:, j, b:b + 1])
                            P.stt(Sn[:, j, :], pB[:, g * 128:(g + 1) * 128], dtxT[:, j, b:b + 1], t1[:, :], ALU.mult, ALU.add)
                            jk = jk_r()
                            P.stt(jk[:, :], Sn[:, j, :], 1.0, pC[:, g * 128:(g + 1) * 128], ALU.mult, ALU.mult, accum=yT[:, j, b:b + 1])
                        P.dma("sp", D["s_ssd"][li, b].re("(j q) n -> q j n", q=128), Sn[:, :, :], is_output=True)
                    y = y_r()
                    transposes(y[:, :].re("t (c q) -> t c q", q=128), [yT[:, j, :] for j in range(16)], 128, F32)
                    gate_norm_out(ti, n, y, xs3, zs[0:n, :], xt)
            barrier()

        stop = debug
        try:
          if debug == "oddfirst":
              odd_phase(1)
              raise _Stop()
          if debug is not None and debug.startswith("peerfirst"):
              peer_phase_dense(0)
              raise _Stop()
          for layer in range(n_layers):
            if layer % 2 == 0:
                if even_phase(layer):
                    break
            else:
                odd_phase(layer)
            ck(f"mix{layer}")
            peer_phase_dense(layer)
            ck(f"peer{layer}")
        except _Stop:
            pass
        P.finish()
        print("program: instr", P.ninstr, "sems", P.nsem, {e: P.cnt[e] for e in P.ENG})
    return nc

def _consts():
    c = {}
    c["c_ident"] = np.eye(128, dtype=np.float32)
    s = np.arange(128)
    c["c_tri"] = (s[:, None] <= s[None, :]).astype(np.float32)
    e = np.zeros((128, 128), np.float32); e[127, :] = 1.0
    c["c_e127"] = e
    c["c_maskneg"] = np.where(s[:, None] <= s[None, :], 0.0, NEG).astype(np.float32)
    b = np.zeros((128, 255), np.float32); b[:, 127] = 1.0
    c["c_b255"] = b
    c["c_kk"] = (s + 1).astype(np.float32).reshape(128, 1)
    slopes = 2.0 ** (-8.0 * np.arange(1, 9, dtype=np.float32) / 8)
    col = np.arange(256)
    dist = 128 + s[:, None] - col[None, :]
    bp = np.zeros((128, 8, 256), np.float32)
    for j8 in range(8):
        bp[:, j8, :] = np.where((dist >= 0) & (dist < 128), -slopes[HP[j8]] * dist, NEG)
    c["c_bias_p"] = bp.reshape(128, 2048)
    bs = np.zeros((128, 128), np.float32)
    for p in range(128):
        bs[p, :] = -slopes[HP[p % 8]] * (127 - np.arange(128))
    c["c_bias_s"] = bs
    e16 = np.zeros((16, 16, 128), np.float32)
    for b_ in range(16):
        e16[b_, b_, :] = 1.0
    c["c_e16"] = e16.reshape(16, 2048)
    c["c_iota16"] = np.broadcast_to(np.arange(16, dtype=np.float32), (128, 16)).copy()
    c["c_iota128"] = np.broadcast_to(np.arange(128, dtype=np.float32), (128, 128)).copy()
    return c


IN_SHAPES = [
    ("x_p", [2048, 1024]), ("x_s", [16, 1024]), ("st_s5_re", [2, 16, 2048]), ("st_s5_im", [2, 16, 2048]),
    ("cache_k", [2, 16, 128, 128]), ("cache_v", [2, 16, 128, 128]), ("st_ssd", [2, 16, 2048, 128]), ("st_conv", [2, 16, 3, 3072]),
    ("w_in_e", [2, 1024, 1280]), ("lam_re", [2, 2048]), ("lam_im", [2, 2048]), ("log_dt", [2, 2048]),
    ("bblk_re", [2, 128, 2048]), ("bblk_im", [2, 128, 2048]), ("cblk_re", [2, 128, 512]), ("cblk_im", [2, 128, 512]),
    ("s5_d", [2, 512]), ("w_glu", [2, 512, 512]), ("sinks_p", [2, 8]), ("sinks_s", [2, 128]), ("w_out_e", [2, 1024, 1024]),
    ("w_in_o", [2, 1024, 5152]), ("conv_w", [2, 128, 96]), ("conv_b", [2, 128, 24]), ("dt_bias", [2, 32]), ("a_log", [2, 32]),
    ("ssd_d", [2, 32]), ("norm_w", [2, 2048]), ("w_out_o", [2, 2048, 1024]),
    ("ln1_g", [4, 1024]), ("ln1_b", [4, 1024]), ("ln2_g", [4, 1024]), ("ln2_b", [4, 1024]),
    ("w_q", [4, 1024, 2048]), ("k1t", [4, 128, 1024]), ("k2t", [4, 128, 1024]),
    ("peer_ut", [4, 1024, 16384]), ("peer_v", [4, 16384, 1024]),
    ("c_ident", [128, 128]), ("c_tri", [128, 128]), ("c_e127", [128, 128]), ("c_maskneg", [128, 128]), ("c_b255", [128, 255]),
    ("c_kk", [128, 1]), ("c_bias_p", [128, 2048]), ("c_bias_s", [128, 128]), ("c_e16", [16, 2048]), ("c_iota16", [128, 16]), ("c_iota128", [128, 128]),
]
OUT_SHAPES = [
    ("y_p", [2048, 1024]), ("y_s", [16, 1024]), ("p_s5_re", [2, 2048]), ("p_s5_im", [2, 2048]),
    ("p_swa_k", [2, 128, 128]), ("p_swa_v", [2, 128, 128]), ("p_ssd", [2, 2048, 128]), ("p_conv", [2, 3, 3072]),
    ("s_s5_re", [2, 16, 2048]), ("s_s5_im", [2, 16, 2048]), ("s_swa_k", [2, 16, 128, 128]), ("s_swa_v", [2, 16, 128, 128]),
    ("s_ssd", [2, 16, 2048, 128]), ("s_conv", [2, 16, 3, 3072]),
]


def _shared_inputs(inp):
    f = lambda a: np.ascontiguousarray(np.asarray(a, dtype=np.float32))
    sh = {}
    w_in_e = f(inp["w_in_even"])
    perm_q = np.concatenate([np.arange(512)] + [512 + HP[j] * 64 + np.arange(64) for j in range(8)] + [np.arange(1024, 1280)])
    sh["w_in_e"] = np.ascontiguousarray(w_in_e[:, :, perm_q])
    sh["lam_re"] = f(inp["s5_lambda_re"]).reshape(2, 2048)
    sh["lam_im"] = f(inp["s5_lambda_im"]).reshape(2, 2048)
    sh["log_dt"] = np.ascontiguousarray(np.repeat(f(inp["s5_log_dt"]), 64, axis=1))
    for nm, src in (("bblk_re", "s5_b_re"), ("bblk_im", "s5_b_im")):
        b = f(inp[src])
        blk = np.zeros((2, 128, 4, 8, 64), np.float32)
        for ch in range(4):
            for gl in range(8):
                g = ch * 8 + gl
                blk[:, gl * 16:(gl + 1) * 16, ch, gl, :] = np.transpose(b[:, g], (0, 2, 1))
        sh[nm] = blk.reshape(2, 128, 2048)
    for nm, src in (("cblk_re", "s5_c_re"), ("cblk_im", "s5_c_im")):
        cc = f(inp[src])
        blk = np.zeros((2, 128, 16, 2, 16), np.float32)
        for kc in range(16):
            for gl in range(2):
                g = kc * 2 + gl
                blk[:, gl * 64:(gl + 1) * 64, kc, gl, :] = np.transpose(cc[:, g], (0, 2, 1))
        sh[nm] = blk.reshape(2, 128, 512)
    sh["s5_d"] = f(inp["s5_d"])
    sh["w_glu"] = f(inp["s5_w_glu"])
    sk = f(inp["swa_sinks"])[:, HP]
    sh["sinks_p"] = np.ascontiguousarray(sk)
    sh["sinks_s"] = np.ascontiguousarray(np.tile(sk, (1, 16)))
    w_out_e = f(inp["w_out_even"])
    perm_o = np.concatenate([np.arange(512)] + [512 + HP[j] * 64 + np.arange(64) for j in range(8)])
    sh["w_out_e"] = np.ascontiguousarray(w_out_e[:, perm_o, :])
    sh["w_in_o"] = f(inp["w_in_odd"])
    cw = f(inp["ssd_conv_w"])
    sh["conv_w"] = np.ascontiguousarray(cw.reshape(2, 4, 24, 128).transpose(0, 3, 2, 1).reshape(2, 128, 96))
    sh["conv_b"] = np.ascontiguousarray(f(inp["ssd_conv_b"]).reshape(2, 24, 128).transpose(0, 2, 1))
    sh["dt_bias"] = f(inp["ssd_dt_bias"]); sh["a_log"] = f(inp["ssd_a_log"]); sh["ssd_d"] = f(inp["ssd_d"])
    sh["norm_w"] = f(inp["ssd_norm_w"]); sh["w_out_o"] = f(inp["w_out_odd"])
    for k in ("ln1_g", "ln1_b", "ln2_g", "ln2_b"):
        sh[k] = f(inp[k])
    sh["w_q"] = f(inp["peer_w_q"])
    sh["k1t"] = np.ascontiguousarray(f(inp["peer_k1"]).transpose(0, 3, 1, 2).reshape(4, 128, 1024))
    sh["k2t"] = np.ascontiguousarray(f(inp["peer_k2"]).transpose(0, 3, 1, 2).reshape(4, 128, 1024))
    sh["peer_ut"] = np.ascontiguousarray(f(inp["peer_u"]).transpose(0, 2, 1)); sh["peer_v"] = f(inp["peer_v"])
    sh.update(_consts())
    if globals().get("_TINY"):
        for nm in ("w_in_o", "w_q", "w_out_o"):
            sh[nm] = np.ascontiguousarray(sh[nm][:, :128])
    return sh


def _core_inputs(inp, sh, c):
    f = lambda a: np.ascontiguousarray(np.asarray(a, dtype=np.float32))
    b0 = c * 16
    m = dict(sh)
    m["x_p"] = f(inp["x_prompt"][c])
    m["x_s"] = f(inp["x_sample"][b0:b0 + 16, 0])
    m["st_s5_re"] = f(inp["state_s5_re"][:, b0:b0 + 16]).reshape(2, 16, 2048)
    m["st_s5_im"] = f(inp["state_s5_im"][:, b0:b0 + 16]).reshape(2, 16, 2048)
    m["cache_k"] = f(inp["cache_swa_k"][:, b0:b0 + 16]).reshape(2, 16, 128, 128)
    m["cache_v"] = f(inp["cache_swa_v"][:, b0:b0 + 16]).reshape(2, 16, 128, 128)
    m["st_ssd"] = f(inp["state_ssd"][:, b0:b0 + 16]).reshape(2, 16, 2048, 128)
    m["st_conv"] = f(inp["state_conv"][:, b0:b0 + 16])
    if globals().get("_TINY"):
        m["st_ssd"] = np.ascontiguousarray(m["st_ssd"][:, :, :128])
    return m


_NC_CACHE = {}
_LAST = None


def kernel(**inp):
    key = "full"
    if key not in _NC_CACHE:
        _NC_CACHE[key] = build_nc()
    nc = _NC_CACHE[key]
    sh = _shared_inputs(inp)
    in_maps = [_core_inputs(inp, sh, c) for c in range(8)]
    res = run_bass_kernel_spmd(nc, in_maps, core_ids=list(range(8)))
    R = res.results
    global _LAST
    _LAST = R
    g = lambda name: [np.asarray(R[c][name], dtype=np.float32) for c in range(8)]
    y_p = np.stack(g("y_p"), 0)
    y_s = np.concatenate(g("y_s"), 0).reshape(128, 1, 1024)
    p_s5_re = np.stack(g("p_s5_re"), 1).reshape(2, 8, 32, 64)
    p_s5_im = np.stack(g("p_s5_im"), 1).reshape(2, 8, 32, 64)
    p_swa_k = np.stack(g("p_swa_k"), 1).reshape(2, 8, 128, 2, 64)
    p_swa_v = np.stack(g("p_swa_v"), 1).reshape(2, 8, 128, 2, 64)
    p_ssd = np.stack(g("p_ssd"), 1).reshape(2, 8, 32, 64, 128)
    p_conv = np.stack(g("p_conv"), 1).reshape(2, 8, 3, 3072)
    s_s5_re = np.concatenate(g("s_s5_re"), 1).reshape(2, 128, 32, 64)
    s_s5_im = np.concatenate(g("s_s5_im"), 1).reshape(2, 128, 32, 64)
    s_swa_k = np.concatenate(g("s_swa_k"), 1).reshape(2, 128, 128, 2, 64)
    s_swa_v = np.concatenate(g("s_swa_v"), 1).reshape(2, 128, 128, 2, 64)
    s_ssd = np.concatenate(g("s_ssd"), 1).reshape(2, 128, 32, 64, 128)
    s_conv = np.concatenate(g("s_conv"), 1).reshape(2, 128, 3, 3072)
    return (y_p, y_s, p_s5_re, p_s5_im, p_swa_k, p_swa_v, p_ssd, p_conv,
            s_s5_re, s_s5_im, s_swa_k, s_swa_v, s_ssd, s_conv)
```

```python
import numpy as np
from contextlib import ExitStack
import concourse.bass as bass
import concourse.mybir as mybir
from concourse.bass_utils import run_bass_kernel_spmd

F32 = mybir.dt.float32
BF16 = mybir.dt.bfloat16
I32 = mybir.dt.int32
U32 = mybir.dt.uint32
AF = mybir.ActivationFunctionType
ALU = mybir.AluOpType
AX = mybir.AxisListType

SEM_GEN = 20000
NDSEM = 24


class Buf:
    def __init__(self, t, name=""):
        self.t = t
        self.name = name
        self.w = None
        self.r = {}

    def __getitem__(self, idx):
        return V(self, self.t[idx])

    def ap(self):
        return V(self, self.t.ap() if hasattr(self.t, "ap") else self.t[:])


class V:
    def __init__(self, buf, ap):
        self.buf = buf
        self.ap = ap

    def __getitem__(self, idx):
        return V(self.buf, self.ap[idx])

    def re(self, s, **kw):
        return V(self.buf, self.ap.rearrange(s, **kw))

    def bc(self, shape):
        return V(self.buf, self.ap.to_broadcast(shape))

    def bitcast(self, dt):
        return V(self.buf, self.ap.bitcast(dt))


class Prog:
    ENG = ("pe", "dve", "act", "pool", "sp")

    def __init__(self, nc, es):
        self.nc = nc
        self.es = es
        self.eng = {"pe": nc.tensor, "dve": nc.vector, "act": nc.scalar, "pool": nc.gpsimd, "sp": nc.sync}
        self.ops = {e: [] for e in self.ENG}
        self.cnt = {e: 0 for e in self.ENG}
        self.sems = {e: [] for e in self.ENG}
        self.seen = {e: {} for e in self.ENG}
        self.dsem = {}
        self.dcur = {e: 0 for e in self.ENG}
        self.nsem = 0
        self.out_tokens = []
        self.ninstr = 0

    def sb(self, name, shape, dt=F32):
        return Buf(self.es.enter_context(self.nc.sbuf_tensor(name, list(shape), dt)), name)

    def ps(self, name, shape, dt=F32):
        b = Buf(self.es.enter_context(self.nc.psum_tensor(name, list(shape), dt)), name)
        b.excl = True
        return b

    def dram(self, name, shape, dt=F32, kind="Internal"):
        return Buf(self.nc.dram_tensor(name, list(shape), dt, kind=kind), name)

    def _newsem(self, name):
        self.nsem += 1
        return self.es.enter_context(self.nc.semaphore(name))

    def _esem(self, e, gen):
        while len(self.sems[e]) <= gen:
            self.sems[e].append(self._newsem(f"s_{e}_{len(self.sems[e])}"))
        return self.sems[e][gen]

    def _need(self, e, tok):
        if tok is None:
            return
        sem, val, key, src = tok
        if self.seen[e].get(key, 0) >= val:
            return
        self.seen[e][key] = val
        self.eng[e].wait_ge(sem, val)

    def _deps(self, e, ins, outs, pe_accum=False):
        for v in ins:
            self._need(e, v.buf.w)
            if getattr(v.buf, "excl", False):
                for t in v.buf.r.values():
                    if t[3] != e:
                        self._need(e, t)
        for v in outs:
            b = v.buf
            if not (pe_accum and b.w is not None and b.w[3] == "pe" and e == "pe"):
                self._need(e, b.w)
            for t in b.r.values():
                self._need(e, t)

    def _commit(self, tok, ins, outs):
        for v in ins:
            if v.buf not in [o.buf for o in outs]:
                v.buf.r[tok[2]] = tok
        for v in outs:
            v.buf.w = tok
            v.buf.r = {}

    def op(self, e, fn, ins=(), outs=(), pe_accum=False):
        if getattr(self, "mute", None) and self.mute["on"]:
            return None
        ins = [v for v in ins if isinstance(v, V)]
        outs = [v for v in outs if isinstance(v, V)]
        self._deps(e, ins, outs, pe_accum)
        gen = self.cnt[e] // SEM_GEN
        sem = self._esem(e, gen)
        self.cnt[e] += 1
        val = self.cnt[e] - gen * SEM_GEN
        fn(self.eng[e]).then_inc(sem, 1)
        tok = (sem, val, (e, gen), e)
        self._commit(tok, ins, outs)
        self.ninstr += 1
        return tok

    def dma(self, q, out, in_, fn=None, is_output=False, **kw):
        if getattr(self, "mute", None) and self.mute["on"]:
            return None
        ins = [in_] + list(kw.pop("extra_ins", []))
        outs = [out]
        self._deps(q, ins, outs)
        ring = self.dsem.setdefault(q, [])
        if len(ring) < NDSEM:
            ring.append([self._newsem(f"d_{q}_{len(ring)}"), 0])
            slot = len(ring) - 1
        else:
            slot = self.dcur[q] % NDSEM
        self.dcur[q] += 1
        ent = ring[slot]
        sem, uses = ent
        key = ("d", q, slot)
        if uses > 0:
            self._need(q, (sem, 16 * uses, key, "dma"))
        ent[1] = uses + 1
        if fn is None:
            fn = lambda eng, o=out.ap, i=in_.ap, kw=kw: eng.dma_start(out=o, in_=i, **kw)
        fn(self.eng[q]).then_inc(sem, 16)
        tok = (sem, 16 * (uses + 1), key, "dma")
        self._commit(tok, ins, outs)
        if is_output:
            self.out_tokens.append(tok)
        self.ninstr += 1
        return tok

    def finish(self):
        for tok in self.out_tokens:
            self._need("sp", tok)

    def mm(self, out, lhsT, rhs, start=True, stop=True):
        return self.op("pe", lambda g: g.matmul(out.ap, lhsT.ap, rhs.ap, start=start, stop=stop),
                       ins=[lhsT, rhs], outs=[out], pe_accum=not start)

    def tr(self, out, in_, ident):
        return self.op("pe", lambda g: g.transpose(out.ap, in_.ap, ident.ap), ins=[in_, ident], outs=[out])

    def act(self, out, in_, func, bias=None, scale=None, accum=None, e="act"):
        kw = {}
        ins = [in_]
        outs = [out]
        if bias is not None:
            kw["bias"] = bias.ap if isinstance(bias, V) else bias
            ins.append(bias)
        if scale is not None:
            kw["scale"] = scale.ap if isinstance(scale, V) else scale
            ins.append(scale)
        if accum is not None:
            kw["accum_out"] = accum.ap
            outs.append(accum)
        return self.op("act", lambda g: g.activation(out.ap, in_.ap, func, **kw), ins=ins, outs=outs)

    def tt(self, out, a, b, op, e="dve"):
        return self.op(e, lambda g: g.tensor_tensor(out.ap, a.ap, b.ap, op), ins=[a, b], outs=[out])

    def ts(self, out, a, s1, s2, op0, op1=None, accum=None, e="dve"):
        ins = [a, s1, s2]
        outs = [out] + ([accum] if accum is not None else [])
        x1 = s1.ap if isinstance(s1, V) else s1
        x2 = s2.ap if isinstance(s2, V) else s2
        kw = {}
        if op1 is not None:
            kw["op1"] = op1
        if accum is not None:
            kw["accum_out"] = accum.ap
        return self.op(e, lambda g: g.tensor_scalar(out.ap, a.ap, x1, x2, op0, **kw), ins=ins, outs=outs)

    def stt(self, out, a, s, b, op0, op1, accum=None):
        x = s.ap if isinstance(s, V) else s
        kw = {"accum_out": accum.ap} if accum is not None else {}
        outs = [out] + ([accum] if accum is not None else [])
        return self.op("dve", lambda g: g.scalar_tensor_tensor(out.ap, a.ap, x, b.ap, op0, op1, **kw),
                       ins=[a, s, b], outs=outs)

    def ttr(self, out, a, b, op0, op1, accum, scale=1.0, scalar=0.0):
        return self.op("dve", lambda g: g.tensor_tensor_reduce(out.ap, a.ap, b.ap, scale, scalar, op0, op1, accum.ap),
                       ins=[a, b], outs=[out, accum])

    def copy(self, out, in_, e="dve"):
        if e == "act":
            return self.op("act", lambda g: g.copy(out.ap, in_.ap), ins=[in_], outs=[out])
        return self.op(e, lambda g: g.tensor_copy(out.ap, in_.ap), ins=[in_], outs=[out])

    def memset(self, out, val, e="dve"):
        return self.op(e, lambda g: g.memset(out.ap, val), outs=[out])

    def red(self, out, in_, op, axis=AX.X, e="dve"):
        return self.op(e, lambda g: g.tensor_reduce(out.ap, in_.ap, axis, op), ins=[in_], outs=[out])

    def recip(self, out, in_):
        return self.op("dve", lambda g: g.reciprocal(out.ap, in_.ap), ins=[in_], outs=[out])

HP = [0, 4, 1, 5, 2, 6, 3, 7]
ALPHA = float(8 ** 0.25)
NTILES = 17
NEG = -1.0e30
TWO_PI = 6.283185307179586


class _Stop(Exception):
    pass


def build_nc(n_layers=4, debug=None, small=False):
    mute = {"on": False}

    def ck(tag):
        if debug == tag:
            mute["on"] = True

    nc = bass.Bass("TRN2", target_bir_lowering=False)
    D = {}

    def din(name, shape, dt=F32):
        D[name] = Buf(nc.dram_tensor(name, list(shape), dt, kind="ExternalInput"), name)

    def dout(name, shape):
        D[name] = Buf(nc.dram_tensor(name, list(shape), F32, kind="ExternalOutput"), name)

    for name, shape in IN_SHAPES:
        if small and name == "peer_v":
            shape = [4, 128, 1024]
        if small and name == "peer_ut":
            shape = [4, 1024, 128]
        if small == 2 and name in ("w_in_o", "w_q", "st_ssd", "w_out_o"):
            shape = [shape[0], 128] + list(shape[2:]) if name != "st_ssd" else [2, 16, 128, 128]
        din(name, shape)
    for name, shape in OUT_SHAPES:
        dout(name, shape)

    with ExitStack() as es:
        P = Prog(nc, es)
        P.mute = mute
        xdram = Buf(nc.dram_tensor("xscratch", [NTILES, 128, 1024], F32, kind=("ExternalOutput" if debug else "Internal")), "xscratch")
        xd = [Buf(xdram.t, f"xd{i}")[i] for i in range(NTILES)]

        def bcast_rows(src_buf, row, ncols, nparts=128):
            return V(src_buf, src_buf.t[row:row + 1, 0:ncols].to_broadcast([nparts, ncols]))

        identf = P.sb("identf", [128, 128]); P.dma("sp", identf[:, :], D["c_ident"][:, :])
        identb = P.sb("identb", [128, 128], BF16); P.copy(identb[:, :], identf[:, :])
        triT = P.sb("triT", [128, 128]); P.dma("sp", triT[:, :], D["c_tri"][:, :])
        e127 = P.sb("e127", [128, 128]); P.dma("sp", e127[:, :], D["c_e127"][:, :])
        b255f = P.sb("b255f", [128, 255]); P.dma("sp", b255f[:, :], D["c_b255"][:, :])
        b255 = P.sb("b255", [128, 255], BF16); P.copy(b255[:, :], b255f[:, :])
        kkp = P.sb("kkp", [128, 1]); P.dma("sp", kkp[:, :], D["c_kk"][:, :])
        kkn = P.sb("kkn", [128, 1]); P.ts(kkn[:, :], kkp[:, :], -1.0, None, ALU.mult)
        one1 = P.sb("one1", [128, 1]); P.memset(one1[:, :], 1.0)
        mone1 = P.sb("mone1", [128, 1]); P.memset(mone1[:, :], -1.0)

        banks = [P.ps(f"bank{i}", [128, 512]) for i in range(8)]
        rot = {"i": 0}

        def pbank():
            b = banks[rot["i"] % 6]
            rot["i"] += 1
            return b

        ev = {"i": 0}

        def evac(out, in_, e=None):
            if e is None and ev.get("force"):
                e = ev["force"]
            if e is None:
                e = "act" if ev["i"] % 2 == 0 else "dve"
                ev["i"] += 1
            return P.copy(out, in_, e=e)

        uid = {"i": 0}

        def ring(sc, name, shape, dt, n):
            uid["i"] += 1
            name = f"{name}_{uid['i']}_"
            bufs = [Buf(sc.enter_context(nc.sbuf_tensor(f"{name}{i}", list(shape), dt)), f"{name}{i}") for i in range(n)]
            st = {"i": 0}

            def nxt():
                b = bufs[st["i"] % n]
                st["i"] += 1
                return b
            return nxt

        def sbuf(sc, name, shape, dt=F32):
            uid["i"] += 1
            name = f"{name}_{uid['i']}"
            return Buf(sc.enter_context(nc.sbuf_tensor(name, list(shape), dt)), name)

        def barrier():
            if mute["on"]:
                return
            toks = []
            for e in P.ENG:
                if P.cnt[e] > 0:
                    gen = (P.cnt[e] - 1) // SEM_GEN
                    toks.append((P.sems[e][gen], P.cnt[e] - gen * SEM_GEN, (e, gen), e))
            for q, rg in P.dsem.items():
                for slot, (sem, uses) in enumerate(rg):
                    if uses > 0:
                        toks.append((sem, 16 * uses, ("d", q, slot), "dma"))
            for e in P.ENG:
                for t in toks:
                    if t[3] != e or t[3] == "dma":
                        P._need(e, t)

        def transposes(dst, srcs, n, dt):
            per = 8 if dt == BF16 else 4
            ident = identb if dt == BF16 else identf
            i = 0
            while i < len(srcs):
                grp = srcs[i:i + per]
                bk = pbank()
                bv = bk[:, :].bitcast(BF16) if dt == BF16 else bk[:, :]
                for j, s in enumerate(grp):
                    w = s.ap.shape[1]
                    P.tr(bv[0:w, j * 128:j * 128 + n], s, ident[0:n, 0:n])
                wmax = max(s.ap.shape[1] for s in grp)
                evac(dst[0:wmax, i:i + len(grp), 0:n],
                     bv[0:wmax, 0:len(grp) * 128].re("p (j t) -> p j t", t=128)[:, :, 0:n])
                i += per

        def make_xT(sc_ring_xb, xT, xt, n):
            xb = sc_ring_xb()
            P.copy(xb[0:n, :], xt[0:n, :], e="pool")
            transposes(xT[:, :, :], [xb[0:n, k * 128:(k + 1) * 128] for k in range(8)], n, BF16)

        def xsrc(layer_first, ti):
            if layer_first:
                if ti < 16:
                    return D["x_p"][ti * 128:(ti + 1) * 128, :]
                return D["x_s"][:, :]
            return xd[ti][0:(128 if ti < 16 else 16), :]

        def layernorm(sc_tmp, x_old, mix_views, n, g_bc, b_bc, out_buf):
            s = sc_tmp["ln_s"]()
            for hh in range(2):
                P.stt(s[0:n, hh * 512:(hh + 1) * 512], x_old[0:n, hh * 512:(hh + 1) * 512], ALPHA, mix_views[hh],
                      ALU.mult, ALU.add)
            st = sc_tmp["ln_st"]()
            for hh in range(2):
                P.op("dve", lambda g, hh=hh: g.bn_stats(st.t[0:n, hh * 6:(hh + 1) * 6], s.t[0:n, hh * 512:(hh + 1) * 512]),
                     ins=[s[:, :]], outs=[st[:, :]])
            mv = sc_tmp["ln_mv"]()
            P.op("dve", lambda g: g.bn_aggr(mv.t[0:n, 0:2], st.t[0:n, 0:12]), ins=[st[:, :]], outs=[mv[:, :]])
            P.ts(mv[0:n, 2:3], mv[0:n, 1:2], 1e-5, None, ALU.add)
            P.act(mv[0:n, 3:4], mv[0:n, 2:3], AF.Sqrt)
            P.recip(mv[0:n, 4:5], mv[0:n, 3:4])
            P.ts(s[0:n, :], s[0:n, :], mv[0:n, 0:1], mv[0:n, 4:5], ALU.subtract, ALU.mult)
            P.tt(s[0:n, :], s[0:n, :], g_bc[0:n, :], ALU.mult, e="pool")
            P.tt(s[0:n, :], s[0:n, :], b_bc[0:n, :], ALU.add, e="pool")
            return s

        def even_phase(layer):
            li = layer // 2
            first = layer == 0
            with ExitStack() as sc:
                w_in = sbuf(sc, "e_win", [128, 8, 1280], BF16)
                w_out = sbuf(sc, "e_wout", [128, 8, 1024], BF16)
                w_glu = sbuf(sc, "e_wglu", [128, 4, 512], BF16)
                c_re = sbuf(sc, "e_cre", [128, 512], BF16)
                c_im = sbuf(sc, "e_cim", [128, 512], BF16)
                bb_re = sbuf(sc, "e_bbre", [128, 4, 512], BF16)
                bb_im = sbuf(sc, "e_bbim", [128, 4, 512], BF16)
                d_bc = sbuf(sc, "e_dbc", [128, 512])
                sk_p = sbuf(sc, "e_skp", [128, 8])
                sk_s = sbuf(sc, "e_sks", [128, 1])
                lng = sbuf(sc, "e_lng", [128, 1024]); lnb = sbuf(sc, "e_lnb", [128, 1024])
                bias_p = sbuf(sc, "e_biasp", [128, 8, 256])
                bias_s = sbuf(sc, "e_biass", [128, 128])
                tabs = {k: sbuf(sc, f"e_tab_{k}", [128, 2048]) for k in ("pr", "pi")}
                P.dma("pool", w_in[:, :, :], D["w_in_e"][li].re("(k p) n -> p k n", p=128))
                P.dma("pool", w_out[:, :, :], D["w_out_e"][li].re("(k p) n -> p k n", p=128))
                P.dma("pool", w_glu[:, :, :], D["w_glu"][li].re("(k p) n -> p k n", p=128))
                P.dma("pool", c_re[:, :], D["cblk_re"][li])
                with ExitStack() as s0:
                    cimf = sbuf(s0, "e_cimf", [128, 512])
                    P.dma("sp", cimf[:, :], D["cblk_im"][li])
                    P.ts(c_im[:, :], cimf[:, :], -1.0, None, ALU.mult)
                    barrier()
                P.dma("sp", d_bc[:, :], bcast_rows(D["s5_d"], li, 512))
                P.dma("sp", sk_p[:, :], bcast_rows(D["sinks_p"], li, 8))
                P.dma("sp", sk_s[:, :], D["sinks_s"][li].re("(p o) -> p o", o=1))
                P.dma("sp", lng[:, :], bcast_rows(D["ln1_g"], layer, 1024))
                P.dma("sp", lnb[:, :], bcast_rows(D["ln1_b"], layer, 1024))
                P.dma("sp", bias_p[:, :, :], D["c_bias_p"][:, :].re("p (h c) -> p h c", h=8))
                P.dma("sp", bias_s[:, :], D["c_bias_s"][:, :])

                def build_tables(kk_pos, kk_neg, with_bb):
                    with ExitStack() as s2:
                        T_ = lambda nm: sbuf(s2, f"e_su_{nm}", [128, 256])
                        lr, lim, dt_, ldt, lid, ang, tq, sn, cs, mg, t1, t2, t3 = [T_(n_) for n_ in
                            ("lr", "li", "dt", "ldt", "lid", "ang", "tq", "sn", "cs", "mg", "t1", "t2", "t3")]
                        tqi = sbuf(s2, "e_su_tqi", [128, 256], I32)
                        bsr = sbuf(s2, "e_su_bsr", [128, 256]); bsi = sbuf(s2, "e_su_bsi", [128, 256])
                        bbr_flat = bb_re[:, :, :].re("p c n -> p (c n)")
                        bbi_flat = bb_im[:, :, :].re("p c n -> p (c n)")
                        for c in range(8):
                            cs_ = slice(c * 256, (c + 1) * 256)
                            P.dma("sp", lr[:, :], V(D["lam_re"], D["lam_re"].t[li:li + 1, cs_].to_broadcast([128, 256])))
                            P.dma("sp", lim[:, :], V(D["lam_im"], D["lam_im"].t[li:li + 1, cs_].to_broadcast([128, 256])))
                            P.dma("sp", dt_[:, :], V(D["log_dt"], D["log_dt"].t[li:li + 1, cs_].to_broadcast([128, 256])))
                            P.act(dt_[:, :], dt_[:, :], AF.Exp)
                            P.tt(ldt[:, :], lr[:, :], dt_[:, :], ALU.mult)
                            P.tt(lid[:, :], lim[:, :], dt_[:, :], ALU.mult)

                            def sincos(kk, shift, outt):
                                P.ts(ang[:, :], lid[:, :], kk[:, 0:1], shift, ALU.mult, ALU.add)
                                P.ts(tq[:, :], ang[:, :], 1.0 / TWO_PI, None, ALU.mult)
                                P.copy(tqi[:, :], tq[:, :])
                                P.copy(tq[:, :], tqi[:, :])
                                P.stt(ang[:, :], tq[:, :], -TWO_PI, ang[:, :], ALU.mult, ALU.add)
                                P.ts(ang[:, :], ang[:, :], 3.1415925, -3.1415925, ALU.min, ALU.max)
                                P.act(outt[:, :], ang[:, :], AF.Sin)

                            def cplx_tab(kk, tre, tim):
                                sincos(kk, 0.0, sn)
                                sincos(kk, 1.5707963267948966, cs)
                                P.act(mg[:, :], ldt[:, :], AF.Exp, scale=kk[:, 0:1])
                                P.tt(tre[:, cs_], mg[:, :], cs[:, :], ALU.mult)
                                P.tt(tim[:, cs_], mg[:, :], sn[:, :], ALU.mult)

                            cplx_tab(kk_pos, tabs["pr"], tabs["pi"])
                            if kk_neg is not None:
                                cplx_tab(kk_neg, tabs["nr"], tabs["ni"])
                            if with_bb:
                                sincos(one1, 0.0, sn)
                                sincos(one1, 1.5707963267948966, cs)
                                P.act(mg[:, :], ldt[:, :], AF.Exp)
                                P.tt(t1[:, :], mg[:, :], cs[:, :], ALU.mult)
                                P.tt(t2[:, :], mg[:, :], sn[:, :], ALU.mult)
                                P.ts(t1[:, :], t1[:, :], -1.0, None, ALU.add)
                                P.tt(t3[:, :], lr[:, :], lr[:, :], ALU.mult)
                                P.tt(tq[:, :], lim[:, :], lim[:, :], ALU.mult)
                                P.tt(t3[:, :], t3[:, :], tq[:, :], ALU.add)
                                P.recip(t3[:, :], t3[:, :])
                                P.tt(tq[:, :], t1[:, :], lr[:, :], ALU.mult)
                                P.tt(ang[:, :], t2[:, :], lim[:, :], ALU.mult)
                                P.tt(tq[:, :], tq[:, :], ang[:, :], ALU.add)
                                P.tt(cs[:, :], tq[:, :], t3[:, :], ALU.mult)
                                P.tt(tq[:, :], t2[:, :], lr[:, :], ALU.mult)
                                P.tt(ang[:, :], t1[:, :], lim[:, :], ALU.mult)
                                P.tt(tq[:, :], tq[:, :], ang[:, :], ALU.subtract)
                                P.tt(sn[:, :], tq[:, :], t3[:, :], ALU.mult)
                                P.dma("sp", bsr[:, :], D["bblk_re"][li, :, cs_])
                                P.dma("sp", bsi[:, :], D["bblk_im"][li, :, cs_])
                                P.tt(t1[:, :], cs[:, :], bsr[:, :], ALU.mult)
                                P.tt(t2[:, :], sn[:, :], bsi[:, :], ALU.mult)
                                P.tt(bbr_flat[:, cs_], t1[:, :], t2[:, :], ALU.subtract)
                                P.tt(t1[:, :], cs[:, :], bsi[:, :], ALU.mult)
                                P.tt(t2[:, :], sn[:, :], bsr[:, :], ALU.mult)
                                P.tt(bbi_flat[:, cs_], t1[:, :], t2[:, :], ALU.add)
                    barrier()

                x_r = ring(sc, "e_x", [128, 1024], F32, 2)
                xb_r = ring(sc, "e_xb", [128, 1024], BF16, 2)
                xT_r = ring(sc, "e_xT", [128, 8, 128], BF16, 2)
                u_r = ring(sc, "e_u", [128, 512], F32, 2)
                ub_r = ring(sc, "e_ub", [128, 512], BF16, 2)
                uT_r = ring(sc, "e_uT", [128, 4, 128], BF16, 2)
                tmp_r = ring(sc, "e_tmp", [128, 512], F32, 4)
                w_r = ring(sc, "e_w", [128, 512], F32, 2)
                hre = [sbuf(sc, f"e_hre{c}", [128, 512]) for c in range(4)]
                him = [sbuf(sc, f"e_him{c}", [128, 512]) for c in range(4)]
                hT_re = sbuf(sc, "e_hTre", [128, 16, 128], BF16)
                hT_im = sbuf(sc, "e_hTim", [128, 16, 128], BF16)
                y_r = ring(sc, "e_y", [128, 512], F32, 3)
                gb_r = ring(sc, "e_gb", [128, 512], BF16, 1)
                gT_r = ring(sc, "e_gT", [128, 4, 128], BF16, 1)
                mix_r = ring(sc, "e_mix", [128, 1024], BF16, 1)
                mixT_r = ring(sc, "e_mixT", [128, 8, 128], BF16, 1)
                qb_r = ring(sc, "e_qb", [128, 512], BF16, 2)
                kvf_r = ring(sc, "e_kvf", [128, 256], F32, 2)
                sm_r = ring(sc, "e_sm", [128, 8], F32, 8)
                lnr = {"ln_s": ring(sc, "e_lns", [128, 1024], F32, 1), "ln_st": ring(sc, "e_lnst", [128, 12], F32, 2),
                       "ln_mv": ring(sc, "e_lnmv", [128, 8], F32, 2)}
                smid = ExitStack()
                sc.callback(smid.close)
                for k in ("nr", "ni"):
                    tabs[k] = sbuf(smid, f"e_tab_{k}", [128, 2048])
                build_tables(kkp, kkn, True)
                ck("e_setup")

                with ExitStack() as s3:
                    qT_r = ring(s3, "e_qT", [128, 4, 128], BF16, 2)
                    kvb = [sbuf(s3, f"e_kvb{i}", [128, 256], BF16) for i in range(2)]
                    kT = [sbuf(s3, f"e_kT{i}", [128, 1, 128], BF16) for i in range(2)]
                    sb_r = ring(s3, "e_sb", [128, 256], F32, 4)
                    pb_r = ring(s3, "e_pb", [128, 256], BF16, 4)
                    pT_r = ring(s3, "e_pT", [128, 2, 128], BF16, 4)

                    def s5_tail_and_rest(ti, n, xt, xT, pj_q, pj_kv, u, sample):
                        transposes(hT_re[:, :, :], [hre[c][0:n, k * 128:(k + 1) * 128] for c in range(4) for k in range(4)], n, F32)
                        transposes(hT_im[:, :, :], [him[c][0:n, k * 128:(k + 1) * 128] for c in range(4) for k in range(4)], n, F32)
                        py = pbank()
                        for kc in range(16):
                            P.mm(py[0:n, kc * 32:(kc + 1) * 32], hT_re[:, kc, 0:n], c_re[:, kc * 32:(kc + 1) * 32], start=True, stop=False)
                            P.mm(py[0:n, kc * 32:(kc + 1) * 32], hT_im[:, kc, 0:n], c_im[:, kc * 32:(kc + 1) * 32], start=False, stop=True)
                        du = y_r()
                        P.tt(du[0:n, :], d_bc[0:n, :], u[0:n, :], ALU.mult, e="pool")
                        y2 = y_r()
                        P.tt(y2[0:n, :], py[0:n, :], du[0:n, :], ALU.add)
                        g = y_r()
                        P.act(g[0:n, :], y2[0:n, :], AF.Gelu_apprx_tanh)
                        gb = gb_r()
                        P.copy(gb[0:n, :], g[0:n, :], e="pool")
                        gT = gT_r()
                        transposes(gT[:, :, :], [gb[0:n, k * 128:(k + 1) * 128] for k in range(4)], n, BF16)
                        pz = pbank()
                        for k in range(4):
                            P.mm(pz[0:n, :], gT[:, k, 0:n], w_glu[:, k, :], start=(k == 0), stop=(k == 3))
                        sg = tmp_r()
                        P.act(sg[0:n, :], pz[0:n, :], AF.Sigmoid)
                        mix = mix_r()
                        P.tt(mix[0:n, 0:512], g[0:n, :], sg[0:n, :], ALU.mult)
                        return mix

                    def out_and_ln(ti, n, xt, mixT):
                        po = [banks[6], banks[7]]
                        for hh in range(2):
                            for k in range(8):
                                P.mm(po[hh][0:n, :], mixT[:, k, 0:n], w_out[:, k, hh * 512:(hh + 1) * 512], start=(k == 0), stop=(k == 7))
                        xo = layernorm(lnr, xt, [po[0][0:n, :], po[1][0:n, :]], n, lng, lnb, None)
                        P.dma("sp", xd[ti][0:n, :], xo[0:n, :])

                    for ti in range(16):
                        if debug and debug.startswith("e_t") and ti >= int(debug[3:]):
                            mute["on"] = True
                        n = 128
                        xt = x_r()
                        P.dma("sp", xt[0:n, :], xsrc(first, ti))
                        xT = xT_r()
                        make_xT(xb_r, xT, xt, n)
                        ck("ck_xT")
                        pj = [pbank(), pbank(), pbank()]
                        for j, (c0, cw) in enumerate(((0, 512), (512, 512), (1024, 256))):
                            for k in range(8):
                                P.mm(pj[j][0:n, 0:cw], xT[:, k, 0:n], w_in[:, k, c0:c0 + cw], start=(k == 0), stop=(k == 7))
                        ck("ck_mm")
                        u = u_r(); ub = ub_r()
                        if debug != "ck_ev_b":
                            evac(u[0:n, :], pj[0][0:n, :], e="act")
                        if debug != "ck_ev_a":
                            evac(ub[0:n, :], pj[0][0:n, :], e="dve")
                        ck("ck_ev"); ck("ck_ev_a"); ck("ck_ev_b")
                        qb = qb_r()
                        P.act(qb[0:n, :], pj[1][0:n, :], AF.Copy, scale=0.125)
                        kvf = kvf_r()
                        evac(kvf[0:n, :], pj[2][0:n, 0:256], e="act")
                        cur = ti % 2
                        evac(kvb[cur][0:n, :], pj[2][0:n, 0:256], e="dve")
                        uT = uT_r()
                        transposes(uT[:, :, :], [ub[0:n, k * 128:(k + 1) * 128] for k in range(4)], n, BF16)
                        ck("ck_proj")
                        for c in range(4):
                            cs_ = slice(c * 512, (c + 1) * 512)
                            pre, pim = pbank(), pbank()
                            P.mm(pre[0:n, :], uT[:, c, 0:n], bb_re[:, c, :])
                            P.mm(pim[0:n, :], uT[:, c, 0:n], bb_im[:, c, :])
                            t1, t2, wre, wim = tmp_r(), tmp_r(), w_r(), w_r()
                            P.tt(t1[:, :], tabs["nr"][:, cs_], pre[:, :], ALU.mult)
                            P.tt(t2[:, :], tabs["ni"][:, cs_], pim[:, :], ALU.mult)
                            P.tt(wre[:, :], t1[:, :], t2[:, :], ALU.subtract, e="pool")
                            t3, t4 = tmp_r(), tmp_r()
                            P.tt(t3[:, :], tabs["nr"][:, cs_], pim[:, :], ALU.mult)
                            P.tt(t4[:, :], tabs["ni"][:, cs_], pre[:, :], ALU.mult)
                            P.tt(wim[:, :], t3[:, :], t4[:, :], ALU.add, e="pool")
                            cre, cim = pbank(), pbank()
                            P.mm(cre[:, :], triT[:, :], wre[:, :], start=True, stop=(ti == 0))
                            P.mm(cim[:, :], triT[:, :], wim[:, :], start=True, stop=(ti == 0))
                            if ti > 0:
                                P.mm(cre[:, :], e127[:, :], hre[c][:, :], start=False, stop=True)
                                P.mm(cim[:, :], e127[:, :], him[c][:, :], start=False, stop=True)
                            t1, t2 = tmp_r(), tmp_r()
                            P.tt(t1[:, :], tabs["pr"][:, cs_], cre[:, :], ALU.mult)
                            P.tt(t2[:, :], tabs["pi"][:, cs_], cim[:, :], ALU.mult)
                            P.tt(hre[c][:, :], t1[:, :], t2[:, :], ALU.subtract, e="pool")
                            t3, t4 = tmp_r(), tmp_r()
                            P.tt(t3[:, :], tabs["pr"][:, cs_], cim[:, :], ALU.mult)
                            P.tt(t4[:, :], tabs["pi"][:, cs_], cre[:, :], ALU.mult)
                            P.tt(him[c][:, :], t3[:, :], t4[:, :], ALU.add, e="pool")
                        ck("ck_scan")
                        mix = s5_tail_and_rest(ti, n, xt, xT, None, None, u, False)
                        ck("ck_s5")
                        qT = qT_r()
                        transposes(qT[:, :, :], [qb[0:n, k * 128:(k + 1) * 128] for k in range(4)], n, BF16)
                        transposes(kT[cur][:, :, :], [kvb[cur][0:n, 0:128]], n, BF16)
                        for j8 in range(8):
                            jp, half = j8 // 2, j8 % 2
                            ps_ = slice(half * 64, (half + 1) * 64)
                            S = pbank()
                            c0 = 0 if ti > 0 else 128
                            if ti > 0:
                                P.mm(S[:, 0:128], qT[ps_, jp, :], kT[1 - cur][ps_, 0, :])
                            P.mm(S[:, 128:256], qT[ps_, jp, :], kT[cur][ps_, 0, :])
                            sbv = sb_r()
                            P.tt(sbv[:, c0:256], S[:, c0:256], bias_p[:, j8, c0:256], ALU.add)
                            sm = sm_r()
                            P.red(sm[:, 0:1], sbv[:, c0:256], ALU.max)
                            P.ts(sm[:, 1:2], sm[:, 0:1], sk_p[:, j8:j8 + 1], -1.0, ALU.max, ALU.mult)
                            pbv = pb_r()
                            P.act(pbv[:, c0:256], sbv[:, c0:256], AF.Exp, bias=sm[:, 1:2], accum=sm[:, 2:3])
                            P.act(sm[:, 3:4], sm[:, 1:2], AF.Exp, bias=sk_p[:, j8:j8 + 1])
                            P.tt(sm[:, 4:5], sm[:, 2:3], sm[:, 3:4], ALU.add)
                            P.recip(sm[:, 5:6], sm[:, 4:5])
                            pT = pT_r()
                            srcs = ([pbv[:, 0:128]] if ti > 0 else []) + [pbv[:, 128:256]]
                            transposes(pT[:, (0 if ti > 0 else 1):2, :], srcs, n, BF16)
                            po_ = pbank()
                            if ti > 0:
                                P.mm(po_[:, 0:64], pT[:, 0, :], kvb[1 - cur][:, 128 + half * 64:128 + (half + 1) * 64], start=True, stop=False)
                            P.mm(po_[:, 0:64], pT[:, 1, :], kvb[cur][:, 128 + half * 64:128 + (half + 1) * 64], start=(ti == 0), stop=True)
                            P.ts(mix[:, 512 + j8 * 64:512 + (j8 + 1) * 64], po_[:, 0:64], sm[:, 5:6], None, ALU.mult)
                        ck("ck_swa")
                        mixT = mixT_r()
                        transposes(mixT[:, :, :], [mix[0:n, k * 128:(k + 1) * 128] for k in range(8)], n, BF16)
                        out_and_ln(ti, n, xt, mixT)
                        ck("ck_ln")
                        if ti == 15:
                            for c in range(4):
                                P.dma("sp", D["p_s5_re"][li:li + 1, c * 512:(c + 1) * 512], hre[c][127:128, :], is_output=True)
                                P.dma("sp", D["p_s5_im"][li:li + 1, c * 512:(c + 1) * 512], him[c][127:128, :], is_output=True)
                            P.dma("sp", D["p_swa_k"][li], kvf[:, 0:128], is_output=True)
                            P.dma("sp", D["p_swa_v"][li], kvf[:, 128:256], is_output=True)

                    barrier()
                    s3.close()
                    smid.close()
                    build_tables(one1, None, False)
                    n = 16
                    ti = 16
                    xt = x_r()
                    P.dma("sp", xt[0:n, :], xsrc(first, ti))
                    xT = xT_r()
                    make_xT(xb_r, xT, xt, n)
                    pj = [pbank(), pbank(), pbank()]
                    for j, (c0, cw) in enumerate(((0, 512), (512, 512), (1024, 256))):
                        for k in range(8):
                            P.mm(pj[j][0:n, 0:cw], xT[:, k, 0:n], w_in[:, k, c0:c0 + cw], start=(k == 0), stop=(k == 7))
                    u = u_r(); ub = ub_r()
                    evac(u[0:n, :], pj[0][0:n, :], e="act")
                    evac(ub[0:n, :], pj[0][0:n, :], e="dve")
                    qb = qb_r()
                    P.act(qb[0:n, :], pj[1][0:n, :], AF.Copy, scale=0.125)
                    kvf = kvf_r()
                    evac(kvf[0:n, :], pj[2][0:n, 0:256], e="act")
                    uT = uT_r()
                    transposes(uT[:, :, :], [ub[0:n, k * 128:(k + 1) * 128] for k in range(4)], n, BF16)
                    for c in range(4):
                        cs_ = slice(c * 512, (c + 1) * 512)
                        pre, pim = pbank(), pbank()
                        P.mm(pre[0:n, :], uT[:, c, 0:n], bb_re[:, c, :])
                        P.mm(pim[0:n, :], uT[:, c, 0:n], bb_im[:, c, :])
                        h0r, h0i = w_r(), w_r()
                        P.dma("sp", h0r[0:n, :], D["st_s5_re"][li, :, cs_])
                        P.dma("sp", h0i[0:n, :], D["st_s5_im"][li, :, cs_])
                        t1, t2, t3, t4 = tmp_r(), tmp_r(), tmp_r(), tmp_r()
                        P.tt(t1[0:n, :], tabs["pr"][0:n, cs_], h0r[0:n, :], ALU.mult)
                        P.tt(t2[0:n, :], tabs["pi"][0:n, cs_], h0i[0:n, :], ALU.mult)
                        P.tt(t1[0:n, :], t1[0:n, :], t2[0:n, :], ALU.subtract)
                        P.tt(hre[c][0:n, :], t1[0:n, :], pre[0:n, :], ALU.add)
                        P.tt(t3[0:n, :], tabs["pr"][0:n, cs_], h0i[0:n, :], ALU.mult)
                        P.tt(t4[0:n, :], tabs["pi"][0:n, cs_], h0r[0:n, :], ALU.mult)
                        P.tt(t3[0:n, :], t3[0:n, :], t4[0:n, :], ALU.add)
                        P.tt(him[c][0:n, :], t3[0:n, :], pim[0:n, :], ALU.add)
                        P.dma("sp", D["s_s5_re"][li, :, cs_], hre[c][0:n, :], is_output=True)
                        P.dma("sp", D["s_s5_im"][li, :, cs_], him[c][0:n, :], is_output=True)
                    mix = s5_tail_and_rest(ti, n, xt, xT, None, None, u, True)
                    mixT = mixT_r()
                    transposes(mixT[:, 0:4, :], [mix[0:n, k * 128:(k + 1) * 128] for k in range(4)], n, BF16)
                    with ExitStack() as s4:
                        KN = sbuf(s4, "e_KN", [128, 16, 128]); VN = KN
                        KNb = sbuf(s4, "e_KNb", [128, 16, 128], BF16); VNb = sbuf(s4, "e_VNb", [128, 16, 128], BF16)
                        KTs = sbuf(s4, "e_KTs", [128, 16, 128], BF16)
                        qTs = sbuf(s4, "e_qTs", [128, 4, 16], BF16)
                        sst = sbuf(s4, "e_sst", [128, 128]); ssb = sbuf(s4, "e_ssb", [128, 128])
                        Pn = sbuf(s4, "e_Pn", [128, 128], BF16); PT = sbuf(s4, "e_PT", [128, 1, 128], BF16)
                        for (XN, cache, outn, c0) in ((KN, "cache_k", "s_swa_k", 0), (VN, "cache_v", "s_swa_v", 128)):
                            P.dma("sp", XN[0:127, :, :], D[cache][li].re("b r f -> r b f")[1:128])
                            P.dma("sp", D[outn][li].re("b r f -> r b f")[0:127], XN[0:127, :, :], is_output=True)
                            P.dma("sp", D[outn][li].re("b r f -> r b f")[127], kvf[0:16, c0:c0 + 128], is_output=True)
                            P.dma("sp", XN[127:128, :, :], D[outn][li].re("b r f -> r b f")[127:128])
                            P.copy((KNb if c0 == 0 else VNb)[:, :, :], XN[:, :, :], e="pool")
                        transposes(KTs[:, :, :], [KNb[:, b, :] for b in range(16)], 128, BF16)
                        transposes(qTs[:, :, :], [qb[0:n, k * 128:(k + 1) * 128] for k in range(4)], n, BF16)
                        pst = pbank()
                        for b in range(16):
                            for half in range(2):
                                ps_ = slice(half * 64, (half + 1) * 64)
                                P.mm(pst[:, b * 8 + half:b * 8 + 8:2], KTs[ps_, b, :], qTs[ps_, :, b])
                        evac(sst[:, :], pst[:, 0:128])
                        pS = pbank()
                        P.tr(pS[:, 0:128], sst[:, :], identf[:, :])
                        P.tt(ssb[:, :], pS[:, 0:128], bias_s[:, :], ALU.add)
                        sm = sm_r()
                        P.red(sm[:, 0:1], ssb[:, :], ALU.max)
                        P.ts(sm[:, 1:2], sm[:, 0:1], sk_s[:, 0:1], -1.0, ALU.max, ALU.mult)
                        P.act(ssb[:, :], ssb[:, :], AF.Exp, bias=sm[:, 1:2], accum=sm[:, 2:3])
                        P.act(sm[:, 3:4], sm[:, 1:2], AF.Exp, bias=sk_s[:, 0:1])
                        P.tt(sm[:, 4:5], sm[:, 2:3], sm[:, 3:4], ALU.add)
                        P.recip(sm[:, 5:6], sm[:, 4:5])
                        P.ts(Pn[:, :], ssb[:, :], sm[:, 5:6], None, ALU.mult)
                        transposes(PT[:, :, :], [Pn[:, :]], 128, BF16)
                        poT = pbank()
                        for b in range(16):
                            for half in range(2):
                                ps_ = slice(half * 64, (half + 1) * 64)
                                P.mm(poT[ps_, 0:64].re("p (j b) -> p j b", b=16)[:, :, b], VNb[:, b, ps_], PT[:, 0, b * 8 + half:b * 8 + 8:2])
                        evac(mixT[:, 4:8, 0:16], poT[:, 0:64].re("p (j b) -> p j b", b=16))
                    out_and_ln(ti, n, xt, mixT)
            barrier()

        def peer_phase(layer):
            last = layer == 3
            with ExitStack() as sc:
                wq = sbuf(sc, "p_wq", [128, 8, 2048], BF16)
                k1t = sbuf(sc, "p_k1t", [128, 8, 128], BF16); k2t = sbuf(sc, "p_k2t", [128, 8, 128], BF16)
                lng = sbuf(sc, "p_lng", [128, 1024]); lnb = sbuf(sc, "p_lnb", [128, 1024])
                io16 = sbuf(sc, "p_io16", [128, 16])
                P.dma("pool", wq[:, :, :], D["w_q"][layer].re("(k p) n -> p k n", p=128))
                P.dma("pool", k1t[:, :, :], D["k1t"][layer].re("d (h k) -> d h k", h=8))
                P.dma("pool", k2t[:, :, :], D["k2t"][layer].re("d (h k) -> d h k", h=8))
                P.dma("sp", lng[:, :], bcast_rows(D["ln2_g"], layer, 1024))
                P.dma("sp", lnb[:, :], bcast_rows(D["ln2_b"], layer, 1024))
                P.dma("sp", io16[:, :], D["c_iota16"][:, :])
                x_r = ring(sc, "p_x", [128, 1024], F32, 2)
                xb_r = ring(sc, "p_xb", [128, 1024], BF16, 2)
                xT_r = ring(sc, "p_xT", [128, 8, 128], BF16, 2)
                qT = sbuf(sc, "p_qT", [128, 16, 128], BF16)
                scs = sbuf(sc, "p_sc", [128, 16, 128]); sc2 = sbuf(sc, "p_sc2", [128, 128])
                v16 = sbuf(sc, "p_v16", [128, 16, 16]); i16 = sbuf(sc, "p_i16", [128, 16, 16], U32); i16f = sbuf(sc, "p_i16f", [128, 16, 16])
                cand = sbuf(sc, "p_cand", [128, 8, 256]); cand2 = sbuf(sc, "p_cand2", [128, 256])
                big1 = sbuf(sc, "p_big1", [128, 8, 256]); big2 = sbuf(sc, "p_big2", [128, 8, 256])
                sv = sbuf(sc, "p_sv", [128, 8, 16]); pos = sbuf(sc, "p_pos", [128, 8, 16], U32)
                posf = sbuf(sc, "p_posf", [128, 8, 16]); ikf = sbuf(sc, "p_ikf", [128, 8, 16]); jkf = sbuf(sc, "p_jkf", [128, 8, 16])
                iki = sbuf(sc, "p_iki", [128, 8, 16], I32)
                e1 = sbuf(sc, "p_e1", [128, 8, 16]); e2 = sbuf(sc, "p_e2", [128, 8, 16])
                gat = sbuf(sc, "p_gat", [128, 8, 16]); gsum = sbuf(sc, "p_gsum", [128, 8])
                idxT = sbuf(sc, "p_idxT", [128, 128], I32); gateT = sbuf(sc, "p_gateT", [128, 128])
                hTall = sbuf(sc, "p_hT", [128, 128]); actT = sbuf(sc, "p_actT", [128, 128])
                junk_r = ring(sc, "p_junk", [128, 1024], BF16, 2)
                lhs_r = ring(sc, "p_lhs", [128, 128], BF16, 4)
                U_r = ring(sc, "p_U", [128, 1024], BF16, 6)
                V_r = ring(sc, "p_V", [128, 1024], BF16, 6)
                lnr = {"ln_s": ring(sc, "p_lns", [128, 1024], F32, 2), "ln_st": ring(sc, "p_lnst", [128, 12], F32, 2),
                       "ln_mv": ring(sc, "p_lnmv", [128, 8], F32, 2)}
                utab = D["peer_u"].t.ap().rearrange("l e d -> (l e) d")
                vtab = D["peer_v"].t.ap().rearrange("l e d -> (l e) d")
                nexp = D["peer_u"].t.shape[1]

                for ti in range(NTILES):
                    if debug is not None and debug.startswith("peerfirst") and len(debug) > 9 and ti not in (0, 16):
                        continue
                    n = 128 if ti < 16 else 16
                    xt = x_r()
                    P.dma("sp", xt[0:n, :], xsrc(debug is not None and debug.startswith("peerfirst"), ti))
                    xT = xT_r()
                    xb = xb_r()
                    P.copy(xb[0:n, :], xt[0:n, :], e="pool")
                    transposes(xT[:, :, :], [xb[0:n, k * 128:(k + 1) * 128] for k in range(8)], n, BF16)
                    for f0 in range(0, 16, 4):
                        bk = pbank()
                        for f in range(f0, f0 + 4):
                            for k in range(8):
                                P.mm(bk[:, (f - f0) * 128:(f - f0) * 128 + n], wq[:, k, f * 128:(f + 1) * 128], xT[:, k, 0:n],
                                     start=(k == 0), stop=(k == 7))
                        evac(qT[:, f0:f0 + 4, 0:n], bk[:, :].re("p (j t) -> p j t", t=128)[:, :, 0:n])
                    for f0 in range(0, 16, 4):
                        bk = pbank()
                        for f in range(f0, f0 + 4):
                            kt = k1t if f % 2 == 0 else k2t
                            P.mm(bk[0:n, (f - f0) * 128:(f - f0 + 1) * 128], qT[:, f, 0:n], kt[:, f // 2, :])
                        evac(scs[0:n, f0:f0 + 4, :], bk[0:n, :].re("p (j t) -> p j t", t=128))
                    for f in range(16):
                        P.op("dve", lambda g, f=f: g.max(v16.t[0:n, f, 0:8], scs.t[0:n, f, :]), ins=[scs[:, :, :]], outs=[v16[:, :, :]])
                        P.op("dve", lambda g, f=f: g.max_index(i16.t[0:n, f, 0:8], v16.t[0:n, f, 0:8], scs.t[0:n, f, :]),
                             ins=[scs[:, :, :], v16[:, :, :]], outs=[i16[:, :, :]])
                        P.op("dve", lambda g, f=f: g.match_replace(sc2.t[0:n, :], v16.t[0:n, f, 0:8], scs.t[0:n, f, :], NEG),
                             ins=[scs[:, :, :], v16[:, :, :]], outs=[sc2[:, :]])
                        P.op("dve", lambda g, f=f: g.max(v16.t[0:n, f, 8:16], sc2.t[0:n, :]), ins=[sc2[:, :]], outs=[v16[:, :, :]])
                        P.op("dve", lambda g, f=f: g.max_index(i16.t[0:n, f, 8:16], v16.t[0:n, f, 8:16], sc2.t[0:n, :]),
                             ins=[sc2[:, :], v16[:, :, :]], outs=[i16[:, :, :]])
                    P.copy(i16f[0:n, :, :], i16[0:n, :, :])
                    v4 = v16[0:n, :, :].re("p (h two) k -> p h two k", two=2)
                    i4 = i16f[0:n, :, :].re("p (h two) k -> p h two k", two=2)
                    P.ts(i4[:, :, 0, :], i4[:, :, 0, :], 128.0, None, ALU.mult)
                    c4 = cand[0:n, :, :].re("p h (i j) -> p h i j", j=16)
                    P.tt(c4, v4[:, :, 0, :].re("p h (i o) -> p h i o", o=1).bc([n, 8, 16, 16]),
                         v4[:, :, 1, :].re("p h (o j) -> p h o j", o=1).bc([n, 8, 16, 16]), ALU.add)
                    for h in range(8):
                        P.op("dve", lambda g, h=h: g.max(sv.t[0:n, h, 0:8], cand.t[0:n, h, :]), ins=[cand[:, :, :]], outs=[sv[:, :, :]])
                        P.op("dve", lambda g, h=h: g.max_index(pos.t[0:n, h, 0:8], sv.t[0:n, h, 0:8], cand.t[0:n, h, :]),
                             ins=[cand[:, :, :], sv[:, :, :]], outs=[pos[:, :, :]])
                        P.op("dve", lambda g, h=h: g.match_replace(cand2.t[0:n, :], sv.t[0:n, h, 0:8], cand.t[0:n, h, :], NEG),
                             ins=[cand[:, :, :], sv[:, :, :]], outs=[cand2[:, :]])
                        P.op("dve", lambda g, h=h: g.max(sv.t[0:n, h, 8:16], cand2.t[0:n, :]), ins=[cand2[:, :]], outs=[sv[:, :, :]])
                        P.op("dve", lambda g, h=h: g.max_index(pos.t[0:n, h, 8:16], sv.t[0:n, h, 8:16], cand2.t[0:n, :]),
                             ins=[cand2[:, :], sv[:, :, :]], outs=[pos[:, :, :]])
                    P.copy(posf[0:n, :, :], pos[0:n, :, :])
                    P.ts(ikf[0:n, :, :], posf[0:n, :, :], -7.5, 1.0 / 16, ALU.add, ALU.mult)
                    P.copy(iki[0:n, :, :], ikf[0:n, :, :])
                    P.copy(ikf[0:n, :, :], iki[0:n, :, :])
                    P.stt(jkf[0:n, :, :], ikf[0:n, :, :], -16.0, posf[0:n, :, :], ALU.mult, ALU.add)
                    b1 = big1[0:n, :, :].re("p h (k i) -> p h k i", i=16)
                    b2 = big2[0:n, :, :].re("p h (k i) -> p h k i", i=16)
                    io4 = io16[0:n, :].re("p (a b i) -> p a b i", a=1, b=1).bc([n, 8, 16, 16])
                    for (kf, tab, eo) in ((ikf, 0, e1), (jkf, 1, e2)):
                        P.tt(b1, kf[0:n, :, :].re("p h (k o) -> p h k o", o=1).bc([n, 8, 16, 16]), io4, ALU.is_equal)
                        P.tt(b2, b1, i4[:, :, tab, :].re("p h (o i) -> p h o i", o=1).bc([n, 8, 16, 16]), ALU.mult, e="pool")
                        P.red(eo[0:n, :, :], b2, ALU.add)
                    P.stt(e1[0:n, :, :], e1[0:n, :, :], float(layer * nexp), e2[0:n, :, :], ALU.add, ALU.add)
                    P.tt(gat[0:n, :, :], sv[0:n, :, :], sv[0:n, :, 0:1].bc([n, 8, 16]), ALU.subtract)
                    P.act(gat[0:n, :, :], gat[0:n, :, :], AF.Exp)
                    P.red(gsum[0:n, :], gat[0:n, :, :], ALU.add)
                    P.recip(gsum[0:n, :], gsum[0:n, :])
                    P.tt(gat[0:n, :, :], gat[0:n, :, :], gsum[0:n, :].re("p (h o) -> p h o", o=1).bc([n, 8, 16]), ALU.mult)
                    bk = pbank()
                    P.tr(bk[:, 0:n], e1[0:n, :, :].re("p h k -> p (h k)"), identf[0:n, 0:n])
                    P.tr(bk[:, 128:128 + n], gat[0:n, :, :].re("p h k -> p (h k)"), identf[0:n, 0:n])
                    evac(idxT[:, 0:n], bk[:, 0:n], e="dve")
                    evac(gateT[:, 0:n], bk[:, 128:128 + n], e="act")
                    for t in range(n):
                        Us = U_r()
                        P.dma("pool", Us[:, :], D["peer_u"][layer], extra_ins=[idxT[:, :]],
                              fn=lambda g, Us=Us, t=t: g.indirect_dma_start(
                                  out=Us.t[:, :], out_offset=None, in_=utab,
                                  in_offset=bass.IndirectOffsetOnAxis(ap=idxT.t[:, t:t + 1], axis=0)))
                        px = [pbank(), pbank()]
                        for hh in range(2):
                            P.mm(px[hh][:, :], identb[0:n, t:t + 1].bc([n, 128]), xb[0:n, hh * 512:(hh + 1) * 512])
                        jk = junk_r()
                        P.stt(jk[:, 0:512], Us[:, 0:512], 1.0, px[0][:, :], ALU.mult, ALU.mult, accum=hTall[:, t:t + 1])
                        P.stt(jk[:, 512:1024], Us[:, 512:1024], 1.0, px[1][:, :], ALU.mult, ALU.mult, accum=actT[:, t:t + 1])
                    P.tt(hTall[:, 0:n], hTall[:, 0:n], actT[:, 0:n], ALU.add)
                    P.act(actT[:, 0:n], hTall[:, 0:n], AF.Gelu_apprx_tanh)
                    P.tt(actT[:, 0:n], actT[:, 0:n], gateT[:, 0:n], ALU.mult)
                    po = [banks[6], banks[7]]
                    for t in range(n):
                        Vs = V_r()
                        P.dma("pool", Vs[:, :], D["peer_v"][layer], extra_ins=[idxT[:, :]],
                              fn=lambda g, Vs=Vs, t=t: g.indirect_dma_start(
                                  out=Vs.t[:, :], out_offset=None, in_=vtab,
                                  in_offset=bass.IndirectOffsetOnAxis(ap=idxT.t[:, t:t + 1], axis=0)))
                        lh = lhs_r()
                        P.ts(lh[:, :], b255[:, 127 - t:255 - t], actT[:, t:t + 1], None, ALU.mult, e="pool")
                        for hh in range(2):
                            P.mm(po[hh][:, :], lh[:, :], Vs[:, hh * 512:(hh + 1) * 512], start=(t == 0), stop=(t == n - 1))
                    xo = layernorm(lnr, xt, [po[0][0:n, :], po[1][0:n, :]], n, lng, lnb, None)
                    if last:
                        if ti < 16:
                            P.dma("sp", D["y_p"][ti * 128:(ti + 1) * 128, :], xo[0:n, :], is_output=True)
                        else:
                            P.dma("sp", D["y_s"][:, :], xo[0:n, :], is_output=True)
                    else:
                        P.dma("sp", xd[ti][0:n, :], xo[0:n, :])
            barrier()

        def peer_phase_dense(layer):
            last = layer == 3
            TL = [0, 16] if (debug is not None and debug.startswith("peerfirst") and len(debug) > 9) else list(range(NTILES))
            NTL = len(TL)
            NCOL = NTL * 128
            with ExitStack() as sc:
                lng = sbuf(sc, "p_lng", [128, 1024]); lnb = sbuf(sc, "p_lnb", [128, 1024])
                P.dma("sp", lng[:, :], bcast_rows(D["ln2_g"], layer, 1024))
                P.dma("sp", lnb[:, :], bcast_rows(D["ln2_b"], layer, 1024))
                xT_all = sbuf(sc, "p_xTall", [128, 8, NCOL], BF16)
                P.memset(xT_all[:, :, (NTL - 1) * 128:NCOL], 0.0, e="pool")
                sa = ExitStack()
                sc.callback(sa.close)
                wq = sbuf(sa, "p_wq", [128, 8, 2048], BF16)
                k1t = sbuf(sa, "p_k1t", [128, 8, 128], BF16); k2t = sbuf(sa, "p_k2t", [128, 8, 128], BF16)
                io16 = sbuf(sa, "p_io16", [128, 16])
                io128 = sbuf(sa, "p_io128", [128, 128]); ioc128 = sbuf(sa, "p_ioc128", [128, 128])
                P.dma("pool", wq[:, :, :], D["w_q"][layer].re("(k p) n -> p k n", p=128))
                P.dma("pool", k1t[:, :, :], D["k1t"][layer].re("d (h k) -> d h k", h=8))
                P.dma("pool", k2t[:, :, :], D["k2t"][layer].re("d (h k) -> d h k", h=8))
                P.dma("sp", io16[:, :], D["c_iota16"][:, :])
                P.dma("sp", io128[:, :], D["c_iota128"][:, :])
                P.ts(ioc128[:, :], io128[:, :], 128.0, None, ALU.mult)
                x_r = ring(sa, "p_x", [128, 1024], F32, 2)
                xb_r = ring(sa, "p_xb", [128, 1024], BF16, 2)
                sc2 = sbuf(sa, "p_sc2", [128, 128])
                v16 = sbuf(sa, "p_v16", [128, 16, 16]); i16 = sbuf(sa, "p_i16", [128, 16, 16], U32); i16f = sbuf(sa, "p_i16f", [128, 16, 16])
                cand = sbuf(sa, "p_cand", [128, 8, 256]); cand2 = sbuf(sa, "p_cand2", [128, 256])
                big1 = sbuf(sa, "p_big1", [128, 8, 256]); big2 = sbuf(sa, "p_big2", [128, 8, 256])
                sv = sbuf(sa, "p_sv", [128, 8, 16]); pos = sbuf(sa, "p_pos", [128, 8, 16], U32)
                posf = sbuf(sa, "p_posf", [128, 8, 16]); ikf = sbuf(sa, "p_ikf", [128, 8, 16]); jkf = sbuf(sa, "p_jkf", [128, 8, 16])
                iki = sbuf(sa, "p_iki", [128, 8, 16], I32)
                e1 = sbuf(sa, "p_e1", [128, 8, 16]); e2 = sbuf(sa, "p_e2", [128, 8, 16])
                gat = sbuf(sa, "p_gat", [128, 8, 16]); gsum = sbuf(sa, "p_gsum", [128, 8])
                slotT = sbuf(sa, "p_slotT", [128, 3, 128])
                nslot = sbuf(sa, "p_nslot", [128, 128])
                ab_r = ring(sa, "p_ab", [128, 128], F32, 12)
                At_r = ring(sa, "p_At", [128, 128], BF16, 12)
                Bt_r = ring(sa, "p_Bt", [128, 128], BF16, 12)
                WGt_r = ring(sa, "p_WGt", [128, 128, 128], BF16, 1)
                WG = P.dram(f"wg_scratch_{layer}", [NTL, 128, 128, 128], BF16)

                qT_r2 = ring(sa, "p_qT2", [128, 16, 128], BF16, 2)
                scs_r2 = ring(sa, "p_sc2b", [128, 16, 128], F32, 2)

                def front(tix):
                    ti = TL[tix]
                    n = 128 if ti < 16 else 16
                    qT = qT_r2(); scs = scs_r2()
                    xt = x_r()
                    P.dma("sp", xt[0:n, :], xsrc(debug is not None and debug.startswith("peerfirst"), ti))
                    xb = xb_r()
                    P.copy(xb[0:n, :], xt[0:n, :], e="pool")
                    xT = xT_all[:, :, tix * 128:(tix + 1) * 128]
                    transposes(xT, [xb[0:n, k * 128:(k + 1) * 128] for k in range(8)], n, BF16)
                    for f0 in range(0, 16, 4):
                        bk = pbank()
                        for f in range(f0, f0 + 4):
                            for k in range(8):
                                P.mm(bk[:, (f - f0) * 128:(f - f0) * 128 + n], wq[:, k, f * 128:(f + 1) * 128], xT[:, k, 0:n],
                                     start=(k == 0), stop=(k == 7))
                        evac(qT[:, f0:f0 + 4, 0:n], bk[:, :].re("p (j t) -> p j t", t=128)[:, :, 0:n])
                    for f0 in range(0, 16, 4):
                        bk = pbank()
                        for f in range(f0, f0 + 4):
                            kt = k1t if f % 2 == 0 else k2t
                            P.mm(bk[0:n, (f - f0) * 128:(f - f0 + 1) * 128], qT[:, f, 0:n], kt[:, f // 2, :])
                        evac(scs[0:n, f0:f0 + 4, :], bk[0:n, :].re("p (j t) -> p j t", t=128))
                    return dict(n=n, scs=scs, tix=tix)

                def mid(cx):
                    n, scs, tix = cx["n"], cx["scs"], cx["tix"]
                    for f in range(16):
                        P.op("dve", lambda g, f=f: g.max(v16.t[0:n, f, 0:8], scs.t[0:n, f, :]), ins=[scs[:, :, :]], outs=[v16[:, :, :]])
                        P.op("dve", lambda g, f=f: g.max_index(i16.t[0:n, f, 0:8], v16.t[0:n, f, 0:8], scs.t[0:n, f, :]),
                             ins=[scs[:, :, :], v16[:, :, :]], outs=[i16[:, :, :]])
                        P.op("dve", lambda g, f=f: g.match_replace(sc2.t[0:n, :], v16.t[0:n, f, 0:8], scs.t[0:n, f, :], NEG),
                             ins=[scs[:, :, :], v16[:, :, :]], outs=[sc2[:, :]])
                        P.op("dve", lambda g, f=f: g.max(v16.t[0:n, f, 8:16], sc2.t[0:n, :]), ins=[sc2[:, :]], outs=[v16[:, :, :]])
                        P.op("dve", lambda g, f=f: g.max_index(i16.t[0:n, f, 8:16], v16.t[0:n, f, 8:16], sc2.t[0:n, :]),
                             ins=[sc2[:, :], v16[:, :, :]], outs=[i16[:, :, :]])
                    P.copy(i16f[0:n, :, :], i16[0:n, :, :])
                    v4 = v16[0:n, :, :].re("p (h two) k -> p h two k", two=2)
                    i4 = i16f[0:n, :, :].re("p (h two) k -> p h two k", two=2)
                    c4 = cand[0:n, :, :].re("p h (i j) -> p h i j", j=16)
                    P.tt(c4, v4[:, :, 0, :].re("p h (i o) -> p h i o", o=1).bc([n, 8, 16, 16]),
                         v4[:, :, 1, :].re("p h (o j) -> p h o j", o=1).bc([n, 8, 16, 16]), ALU.add)
                    for h in range(8):
                        P.op("dve", lambda g, h=h: g.max(sv.t[0:n, h, 0:8], cand.t[0:n, h, :]), ins=[cand[:, :, :]], outs=[sv[:, :, :]])
                        P.op("dve", lambda g, h=h: g.max_index(pos.t[0:n, h, 0:8], sv.t[0:n, h, 0:8], cand.t[0:n, h, :]),
                             ins=[cand[:, :, :], sv[:, :, :]], outs=[pos[:, :, :]])
                        P.op("dve", lambda g, h=h: g.match_replace(cand2.t[0:n, :], sv.t[0:n, h, 0:8], cand.t[0:n, h, :], NEG),
                             ins=[cand[:, :, :], sv[:, :, :]], outs=[cand2[:, :]])
                        P.op("dve", lambda g, h=h: g.max(sv.t[0:n, h, 8:16], cand2.t[0:n, :]), ins=[cand2[:, :]], outs=[sv[:, :, :]])
                        P.op("dve", lambda g, h=h: g.max_index(pos.t[0:n, h, 8:16], sv.t[0:n, h, 8:16], cand2.t[0:n, :]),
                             ins=[cand2[:, :], sv[:, :, :]], outs=[pos[:, :, :]])
                    P.copy(posf[0:n, :, :], pos[0:n, :, :])
                    P.ts(ikf[0:n, :, :], posf[0:n, :, :], -7.5, 1.0 / 16, ALU.add, ALU.mult)
                    P.copy(iki[0:n, :, :], ikf[0:n, :, :])
                    P.copy(ikf[0:n, :, :], iki[0:n, :, :])
                    P.stt(jkf[0:n, :, :], ikf[0:n, :, :], -16.0, posf[0:n, :, :], ALU.mult, ALU.add)
                    b1 = big1[0:n, :, :].re("p h (k i) -> p h k i", i=16)
                    b2 = big2[0:n, :, :].re("p h (k i) -> p h k i", i=16)
                    io4 = io16[0:n, :].re("p (a b i) -> p a b i", a=1, b=1).bc([n, 8, 16, 16])
                    for (kf, tab, eo) in ((ikf, 0, e1), (jkf, 1, e2)):
                        P.tt(b1, kf[0:n, :, :].re("p h (k o) -> p h k o", o=1).bc([n, 8, 16, 16]), io4, ALU.is_equal)
                        P.tt(b2, b1, i4[:, :, tab, :].re("p h (o i) -> p h o i", o=1).bc([n, 8, 16, 16]), ALU.mult, e="pool")
                        P.red(eo[0:n, :, :], b2, ALU.add)
                    P.tt(gat[0:n, :, :], sv[0:n, :, :], sv[0:n, :, 0:1].bc([n, 8, 16]), ALU.subtract)
                    P.act(gat[0:n, :, :], gat[0:n, :, :], AF.Exp)
                    P.red(gsum[0:n, :], gat[0:n, :, :], ALU.add)
                    P.recip(gsum[0:n, :], gsum[0:n, :])
                    P.tt(gat[0:n, :, :], gat[0:n, :, :], gsum[0:n, :].re("p (h o) -> p h o", o=1).bc([n, 8, 16]), ALU.mult)

                def back(cx):
                    n, tix = cx["n"], cx["tix"]
                    bk = pbank()
                    P.tr(bk[:, 0:n], e1[0:n, :, :].re("p h k -> p (h k)"), identf[0:n, 0:n])
                    P.tr(bk[:, 128:128 + n], e2[0:n, :, :].re("p h k -> p (h k)"), identf[0:n, 0:n])
                    P.tr(bk[:, 256:256 + n], gat[0:n, :, :].re("p h k -> p (h k)"), identf[0:n, 0:n])
                    evac(slotT[:, :, 0:n], bk[:, 0:384].re("p (j t) -> p j t", t=128)[:, :, 0:n], e="act")
                    P.ts(nslot[:, 0:n], slotT[:, 1, 0:n], -1.0, None, ALU.mult)
                    WGt = WGt_r()
                    if n < 128:
                        P.memset(WGt[:, :, :], 0.0, e="pool")
                    for t0 in range(0, n, 4):
                        pG = pbank()
                        for t in range(t0, t0 + 4):
                            At = At_r(); Bt = Bt_r()
                            P.ts(At[:, :], io128[:, :], slotT[:, 0, t:t + 1], slotT[:, 2, t:t + 1], ALU.is_equal, ALU.mult)
                            if t % 3 == 2:
                                P.ts(Bt[:, :], io128[:, :], slotT[:, 1, t:t + 1], None, ALU.is_equal)
                            else:
                                ab = ab_r()
                                P.act(ab[:, :], io128[:, :], AF.Abs, bias=nslot[:, t:t + 1])
                                P.act(Bt[:, :], ab[:, :], AF.Relu, scale=-1.0, bias=1.0)
                            P.mm(pG[:, (t - t0) * 128:(t - t0 + 1) * 128], Bt[:, :], At[:, :])
                        evac(WGt[:, :, t0:t0 + 4], pG[:, :].re("p (t c) -> p c t", c=128))
                    P.dma("sp", WG[tix], WGt[:, :, :])

                ev["force"] = "act"
                cxs = {0: front(0)}
                ev["force"] = None
                for tix in range(NTL):
                    if tix + 1 < NTL:
                        ev["force"] = "act"
                        cxs[tix + 1] = front(tix + 1)
                        ev["force"] = None
                    mid(cxs[tix])
                    back(cxs[tix])
                barrier()
                sa.close()

                G = 4
                acc = [sbuf(sc, f"p_acc{i}", [128, 1024]) for i in range(NTL)]
                UT_r = ring(sc, "p_UT", [128, 8, G * 128], BF16, 2)
                Vg_r = ring(sc, "p_Vg", [128, G, 1024], BF16, 2)
                WGg_r = ring(sc, "p_WGg", [128, NTL, G, 128], BF16, 1)
                gb_r = ring(sc, "p_gb", [128, NCOL], BF16, 2)
                Ag_r = ring(sc, "p_Ag", [128, NCOL], BF16, G + 1)
                x_r = ring(sc, "p_x2", [128, 1024], F32, 2)
                lnr = {"ln_s": ring(sc, "p_lns", [128, 1024], F32, 1), "ln_st": ring(sc, "p_lnst", [128, 12], F32, 2),
                       "ln_mv": ring(sc, "p_lnmv", [128, 8], F32, 2)}
                ut_l = D["peer_ut"][layer].re("(k p) e -> p k e", p=128)
                v_l = D["peer_v"][layer]
                nblk = [(c0, min(512, NCOL - c0)) for c0 in range(0, NCOL, 512)]
                tblocks = [list(range(i, min(i + 3, NTL))) for i in range(0, NTL, 3)]
                for gi in range(128 // G):
                    c0 = gi * G
                    UTg = UT_r(); Vg = Vg_r(); WGg = WGg_r()
                    P.dma("pool", UTg[:, :, :], ut_l[:, :, c0 * 128:(c0 + G) * 128])
                    P.dma("pool", Vg[:, :, :], v_l[c0 * 128:(c0 + G) * 128, :].re("(c e) d -> e c d", e=128))
                    P.dma("sp", WGg[:, :, :, :], WG[:, :, c0:c0 + G, :].re("i e c t -> e i c t"))
                    Ags = []
                    for ci in range(G):
                        gb = gb_r()
                        for bi, (n0, nw) in enumerate(nblk):
                            ph = banks[6 + bi % 2]
                            for k in range(8):
                                P.mm(ph[:, 0:nw], UTg[:, k, ci * 128:(ci + 1) * 128], xT_all[:, k, n0:n0 + nw], start=(k == 0), stop=(k == 7))
                            P.act(gb[:, n0:n0 + nw], ph[:, 0:nw], AF.Gelu_apprx_tanh)
                        Ag = Ag_r()
                        P.tt(Ag[:, :].re("p (i t) -> p i t", t=128), gb[:, :].re("p (i t) -> p i t", t=128), WGg[:, :, ci, :], ALU.mult, e="pool")
                        Ags.append(Ag)
                    for tb in tblocks:
                        for j, tix in enumerate(tb):
                            for hh in range(2):
                                for ci in range(G):
                                    P.mm(banks[j * 2 + hh][:, :], Ags[ci][:, tix * 128:(tix + 1) * 128], Vg[:, ci, hh * 512:(hh + 1) * 512],
                                         start=(ci == 0), stop=(ci == G - 1))
                        for j, tix in enumerate(tb):
                            for hh in range(2):
                                hs = slice(hh * 512, (hh + 1) * 512)
                                if gi == 0:
                                    evac(acc[tix][:, hs], banks[j * 2 + hh][:, :])
                                else:
                                    P.tt(acc[tix][:, hs], acc[tix][:, hs], banks[j * 2 + hh][:, :], ALU.add)
                for tix, ti in enumerate(TL):
                    n = 128 if ti < 16 else 16
                    xt = x_r()
                    P.dma("sp", xt[0:n, :], xsrc(debug is not None and debug.startswith("peerfirst"), ti))
                    xo = layernorm(lnr, xt, [acc[tix][0:n, 0:512], acc[tix][0:n, 512:1024]], n, lng, lnb, None)
                    if last:
                        if ti < 16:
                            P.dma("sp", D["y_p"][ti * 128:(ti + 1) * 128, :], xo[0:n, :], is_output=True)
                        else:
                            P.dma("sp", D["y_s"][:, :], xo[0:n, :], is_output=True)
                    else:
                        P.dma("sp", xd[ti][0:n, :], xo[0:n, :])
            barrier()

        def odd_phase(layer):
            li = layer // 2
            NB = 2
            TB = NB * 128
            win = D["w_in_o"][li]

            with ExitStack() as sc:
                w_out = sbuf(sc, "o_wout", [128, 16, 1024], BF16)
                cw = sbuf(sc, "o_cw", [128, 96]); cb = sbuf(sc, "o_cb", [128, 24])
                dtb = sbuf(sc, "o_dtb", [128, 32]); aneg = sbuf(sc, "o_aneg", [128, 32]); dsk = sbuf(sc, "o_dsk", [128, 32])
                normw = sbuf(sc, "o_normw", [128, 2048])
                lng = sbuf(sc, "o_lng", [128, 1024]); lnb = sbuf(sc, "o_lnb", [128, 1024])
                maskneg = sbuf(sc, "o_mask", [128, 128])
                P.dma("pool", w_out[:, :, :], D["w_out_o"][li].re("(k p) n -> p k n", p=128))
                P.dma("sp", cw[:, :], D["conv_w"][li]); P.dma("sp", cb[:, :], D["conv_b"][li])
                P.dma("sp", dtb[:, :], bcast_rows(D["dt_bias"], li, 32))
                P.dma("sp", aneg[:, :], bcast_rows(D["a_log"], li, 32))
                P.act(aneg[:, :], aneg[:, :], AF.Exp)
                P.ts(aneg[:, :], aneg[:, :], -1.0, None, ALU.mult)
                P.dma("sp", dsk[:, :], bcast_rows(D["ssd_d"], li, 32))
                P.dma("sp", normw[:, :], bcast_rows(D["norm_w"], li, 2048))
                P.dma("sp", lng[:, :], bcast_rows(D["ln1_g"], layer, 1024))
                P.dma("sp", lnb[:, :], bcast_rows(D["ln1_b"], layer, 1024))
                P.dma("sp", maskneg[:, :], D["c_maskneg"][:, :])
                Wb_r = ring(sc, "o_wb", [128, 8, 512], BF16, 2)
                dg_r = ring(sc, "o_dg", [128, 4, 128], BF16, 4)
                lnr = {"ln_s": ring(sc, "o_lns", [128, 1024], F32, 1), "ln_st": ring(sc, "o_lnst", [128, 12], F32, 2),
                       "ln_mv": ring(sc, "o_lnmv", [128, 8], F32, 2)}
                x_r = ring(sc, "o_x", [128, 1024], F32, 2)
                xb_r = ring(sc, "o_xb", [128, 1024], BF16, 2)
                sm_r = ring(sc, "o_sm", [128, 32], F32, 12)
                y_r = ring(sc, "o_y", [128, 2048], F32, 2)
                yb_r = ring(sc, "o_yb", [128, 2048], BF16, 1)
                ynT_r = ring(sc, "o_ynT", [128, 16, 128], BF16, 1)

                wbf = P.dram(f"wino_bf16_{layer}", [1024, 5152], BF16)
                with ExitStack() as s0:
                    stg_r = ring(s0, "o_stg", [128, 8, 512], F32, 2)
                    stb_r = ring(s0, "o_stb", [128, 8, 512], BF16, 2)
                    for blk in range(11):
                        c0 = blk * 512
                        cwid = min(512, 5152 - c0)
                        stg = stg_r(); stb = stb_r()
                        P.dma("sp", stg[:, :, 0:cwid], win[:, c0:c0 + cwid].re("(k p) n -> p k n", p=128))
                        P.copy(stb[:, :, 0:cwid], stg[:, :, 0:cwid], e=("act" if blk % 2 == 0 else "dve"))
                        P.dma("sp", wbf[:, c0:c0 + cwid].re("(k p) n -> p k n", p=128), stb[:, :, 0:cwid])
                    barrier()

                def load_w(c0, cwid):
                    Wb = Wb_r()
                    P.dma("sp", Wb[:, :, 0:cwid], wbf[:, c0:c0 + cwid].re("(k p) n -> p k n", p=128))
                    return Wb

                def softplus_dt(pdt, n):
                    a, b_, c_, d_ = sm_r(), sm_r(), sm_r(), sm_r()
                    P.tt(a[0:n, :], pdt, dtb[0:n, :], ALU.add)
                    P.stt(b_[0:n, :], a[0:n, :], -1.0, a[0:n, :], ALU.mult, ALU.max)
                    P.act(c_[0:n, :], b_[0:n, :], AF.Exp, scale=-1.0)
                    P.act(c_[0:n, :], c_[0:n, :], AF.Ln, bias=1.0)
                    P.stt(d_[0:n, :], a[0:n, :], 0.0, c_[0:n, :], ALU.max, ALU.add)
                    return d_

                def make_dg(ct):
                    dg = dg_r()
                    for tap in range(4):
                        P.ts(dg[:, tap, :], identb[:, :], cw[:, ct * 4 + tap:ct * 4 + tap + 1], None, ALU.mult)
                    return dg

                def gate_norm_out(ti, n, y, xs_view, zs_view, xt):
                    t = y_r()
                    P.tt(t[0:n, :].re("p (h q) -> p h q", q=64), xs_view, dsk[0:n, :].re("p (h o) -> p h o", o=1).bc([n, 32, 64]),
                         ALU.mult, e="pool")
                    P.tt(y[0:n, :], y[0:n, :], t[0:n, :], ALU.add, e="pool")
                    P.tt(y[0:n, :], y[0:n, :], zs_view, ALU.mult)
                    ss = sm_r()
                    P.act(t[0:n, :], y[0:n, :], AF.Square, accum=ss[0:n, 0:1])
                    P.ts(ss[0:n, 1:2], ss[0:n, 0:1], 1.0 / 2048, 1e-5, ALU.mult, ALU.add)
                    P.act(ss[0:n, 2:3], ss[0:n, 1:2], AF.Sqrt)
                    P.recip(ss[0:n, 3:4], ss[0:n, 2:3])
                    yb = yb_r()
                    P.stt(yb[0:n, :], y[0:n, :], ss[0:n, 3:4], normw[0:n, :], ALU.mult, ALU.mult)
                    ynT = ynT_r()
                    transposes(ynT[:, :, :], [yb[0:n, k * 128:(k + 1) * 128] for k in range(16)], n, BF16)
                    po = [banks[6], banks[7]]
                    for hh in range(2):
                        for k in range(16):
                            P.mm(po[hh][0:n, :], ynT[:, k, 0:n], w_out[:, k, hh * 512:(hh + 1) * 512], start=(k == 0), stop=(k == 15))
                    xo = layernorm(lnr, xt, [po[0][0:n, :], po[1][0:n, :]], n, lng, lnb, None)
                    P.dma("sp", xd[ti][0:n, :], xo[0:n, :])

                with ExitStack() as s3:
                    xT = sbuf(s3, "o_xT", [128, 8, TB], BF16)
                    xbc_r = ring(s3, "o_xbc", [128, 3 + TB], BF16, 2)
                    halo = sbuf(s3, "o_halo", [128, 24, 3], BF16)
                    P.memset(halo[:, :, :], 0.0)
                    xaT = sbuf(s3, "o_xaT", [128, 24, TB], BF16)
                    zs = sbuf(s3, "o_zs", [128, NB, 2048], BF16)
                    raw3 = sbuf(s3, "o_raw3", [128, 24, 3])
                    dts = [sbuf(s3, f"o_dt{j}", [128, 32]) for j in range(NB)]
                    ST = [sbuf(s3, f"o_ST{g}", [128, 512]) for g in range(4)]
                    STb = [sbuf(s3, f"o_STb{g}", [128, 512], BF16) for g in range(4)]
                    xs_r = ring(s3, "o_xs", [128, 2048], BF16, 2)
                    Bt_r = ring(s3, "o_Bt", [128, 512], BF16, 2)
                    dxb_r = ring(s3, "o_dxb", [128, 2048], BF16, 2)
                    dxs_r = ring(s3, "o_dxs", [128, 2048], BF16, 2)
                    cbT_r = ring(s3, "o_cbT", [128, 128], F32, 2)
                    R_r = ring(s3, "o_R", [128, 4, 128], F32, 4)
                    seg_r = ring(s3, "o_seg", [128, 512], F32, 2)
                    L_r = ring(s3, "o_L", [128, 512], F32, 2)
                    MT_r = ring(s3, "o_MT", [128, 4, 128], BF16, 2)
                    ones_f = sbuf(s3, "o_ones", [128, 128]); P.memset(ones_f[:, :], 1.0)
                    ntriT = sbuf(s3, "o_ntri", [128, 128]); P.ts(ntriT[:, :], triT[:, :], -1.0, None, ALU.mult)
                    t1_r = ring(s3, "o_t1", [128, 512], F32, 2)

                    def ssd_tile(ti, j, xt):
                        tsl = slice(j * 128, (j + 1) * 128)
                        dt = dts[j]
                        xs = xs_r()
                        transposes(xs[:, :].re("t (c q) -> t c q", q=128), [xaT[:, ct, tsl] for ct in range(16)], 128, BF16)
                        Bt = Bt_r()
                        transposes(Bt[:, :].re("t (c q) -> t c q", q=128), [xaT[:, 16 + g, tsl] for g in range(4)], 128, BF16)
                        dA, acs, nacs, eacs, eal, decs = sm_r(), sm_r(), sm_r(), sm_r(), sm_r(), sm_r()
                        P.tt(dA[:, :], dt[:, :], aneg[:, :], ALU.mult)
                        pa = pbank()
                        P.mm(pa[:, 0:32], triT[:, :], dA[:, :])
                        evac(acs[:, :], pa[:, 0:32], e="dve")
                        P.ts(nacs[:, :], acs[:, :], -1.0, None, ALU.mult)
                        P.act(eacs[:, :], acs[:, :], AF.Exp)
                        pl = pbank()
                        P.mm(pl[:, 0:32], e127[:, :], acs[:, :])
                        P.act(eal[:, :], pl[:, 0:32], AF.Exp)
                        P.tt(decs[:, :], pl[:, 0:32], acs[:, :], ALU.subtract)
                        P.act(decs[:, :], decs[:, :], AF.Exp)
                        xs3 = xs[:, :].re("p (h q) -> p h q", q=64)
                        dxb = dxb_r(); dxs = dxs_r()
                        P.tt(dxb[:, :].re("p (h q) -> p h q", q=64), xs3, dt[:, :].re("p (h o) -> p h o", o=1).bc([128, 32, 64]), ALU.mult)
                        P.tt(dxs[:, :].re("p (h q) -> p h q", q=64), dxb[:, :].re("p (h q) -> p h q", q=64),
                             decs[:, :].re("p (h o) -> p h o", o=1).bc([128, 32, 64]), ALU.mult, e="pool")
                        y = y_r()
                        for g in range(4):
                            pcb = pbank()
                            P.mm(pcb[:, 0:128], xaT[:, 16 + g, tsl], xaT[:, 20 + g, tsl])
                            cbT = cbT_r()
                            evac(cbT[:, :], pcb[:, 0:128])
                            pyd, pyo = banks[6], banks[7]
                            for hg in range(2):
                                h0 = g * 8 + hg * 4
                                R2 = R_r(); R1 = R_r()
                                P.copy(R2[:, :, :], dA[:, h0:h0 + 4].re("p (h o) -> p h o", o=1).bc([128, 4, 128]), e="pool")
                                P.tt(R1[:, :, :], R2[:, :, :], triT[:, :].re("p (o t) -> p o t", o=1).bc([128, 4, 128]), ALU.mult)
                                pbc = pbank()
                                P.mm(pbc[:, :], ones_f[:, :], R1[:, :, :].re("p h t -> p (h t)"), start=True, stop=False)
                                P.mm(pbc[:, :], ntriT[:, :], R2[:, :, :].re("p h t -> p (h t)"), start=False, stop=True)
                                seg = seg_r()
                                P.tt(seg[:, :].re("p (h t) -> p h t", t=128), pbc[:, :].re("p (h t) -> p h t", t=128),
                                     maskneg[:, :].re("p (o t) -> p o t", o=1).bc([128, 4, 128]), ALU.add)
                                L = L_r()
                                P.act(L[:, :], seg[:, :], AF.Exp)
                                MT = MT_r()
                                P.tt(MT[:, :, :], L[:, :].re("p (h t) -> p h t", t=128),
                                     cbT[:, :].re("p (o t) -> p o t", o=1).bc([128, 4, 128]), ALU.mult, e="pool")
                                for hl4 in range(4):
                                    h = h0 + hl4
                                    hl = hg * 4 + hl4
                                    P.mm(pyd[:, hl * 64:(hl + 1) * 64], MT[:, hl4, :], dxb[:, h * 64:(h + 1) * 64])
                            gs = slice(g * 512, (g + 1) * 512)
                            if ti > 0:
                                P.mm(pyo[:, :], xaT[:, 20 + g, tsl], STb[g][:, :])
                                t1 = t1_r()
                                P.tt(t1[:, :].re("p (h q) -> p h q", q=64), pyo[:, :].re("p (h q) -> p h q", q=64),
                                     eacs[:, g * 8:(g + 1) * 8].re("p (h o) -> p h o", o=1).bc([128, 8, 64]), ALU.mult)
                                P.tt(y[:, gs], t1[:, :], pyd[:, :], ALU.add)
                            else:
                                evac(y[:, gs], pyd[:, :])
                            pst = pbank()
                            P.mm(pst[:, :], Bt[:, g * 128:(g + 1) * 128], dxs[:, gs])
                            if ti > 0:
                                t2 = t1_r()
                                P.tt(t2[:, :].re("p (h q) -> p h q", q=64), ST[g][:, :].re("p (h q) -> p h q", q=64),
                                     eal[:, g * 8:(g + 1) * 8].re("p (h o) -> p h o", o=1).bc([128, 8, 64]), ALU.mult, e="pool")
                                P.tt(ST[g][:, :], t2[:, :], pst[:, :], ALU.add)
                            else:
                                evac(ST[g][:, :], pst[:, :], e="dve")
                            P.copy(STb[g][:, :], ST[g][:, :], e="pool")
                        gate_norm_out(ti, 128, y, xs3, zs[:, j, :], xt)

                    for bi in range(16 // NB):
                        tiles_ = [bi * NB + j for j in range(NB)]
                        xts = []
                        for j, ti in enumerate(tiles_):
                            xt = x_r()
                            P.dma("sp", xt[:, :], xsrc(debug == "oddfirst", ti))
                            xb = xb_r()
                            P.copy(xb[:, :], xt[:, :], e="pool")
                            transposes(xT[:, :, j * 128:(j + 1) * 128], [xb[:, k * 128:(k + 1) * 128] for k in range(8)], 128, BF16)
                            xts.append(xt)
                        for blk in range(4):
                            Wb = load_w(blk * 512, 512)
                            for j in range(NB):
                                pz = pbank()
                                for k in range(8):
                                    P.mm(pz[:, :], xT[:, k, j * 128:(j + 1) * 128], Wb[:, k, :], start=(k == 0), stop=(k == 7))
                                P.act(zs[:, j, blk * 512:(blk + 1) * 512], pz[:, :], AF.Silu)
                        Wd = load_w(5120, 32)
                        for j in range(NB):
                            pdt = pbank()
                            for k in range(8):
                                P.mm(pdt[:, 0:32], xT[:, k, j * 128:(j + 1) * 128], Wd[:, k, 0:32], start=(k == 0), stop=(k == 7))
                            d_ = softplus_dt(pdt[:, 0:32], 128)
                            P.copy(dts[j][:, :], d_[:, :])
                        for blk in range(4, 10):
                            Wb = load_w(blk * 512, 512)
                            for sub in range(4):
                                ct = (blk - 4) * 4 + sub
                                px = pbank()
                                for k in range(8):
                                    P.mm(px[:, 0:TB], Wb[:, k, sub * 128:(sub + 1) * 128], xT[:, k, :], start=(k == 0), stop=(k == 7))
                                xbc = xbc_r()
                                P.copy(xbc[:, 0:3], halo[:, ct, :], e="pool")
                                evac(xbc[:, 3:3 + TB], px[:, 0:TB])
                                if bi == 16 // NB - 1:
                                    P.copy(raw3[:, ct, :], px[:, TB - 3:TB])
                                P.copy(halo[:, ct, :], xbc[:, TB:TB + 3], e="pool")
                                dg = make_dg(ct)
                                pc = pbank()
                                for tap in range(4):
                                    P.mm(pc[:, 0:TB], dg[:, tap, :], xbc[:, tap:tap + TB], start=(tap == 0), stop=(tap == 3))
                                P.act(xaT[:, ct, :], pc[:, 0:TB], AF.Silu, bias=cb[:, ct:ct + 1])
                        for j, ti in enumerate(tiles_):
                            ssd_tile(ti, j, xts[j])
                    osb_r = ring(s3, "o_osb", [128, 4, 128], F32, 2)
                    for g in range(4):
                        osb = osb_r()
                        transposes(osb[:, :, :], [ST[g][:, k * 128:(k + 1) * 128] for k in range(4)], 128, F32)
                        P.dma("sp", D["p_ssd"][li, g * 512:(g + 1) * 512, :].re("(k p) n -> p k n", p=128), osb[:, :, :], is_output=True)
                    for g in range(6):
                        osb = osb_r()
                        transposes(osb[:, :, :], [raw3[:, ct, :] for ct in range(4 * g, 4 * g + 4)], 128, F32)
                        P.dma("sp", D["p_conv"][li][:, g * 512:(g + 1) * 512].re("r (c q) -> r c q", q=128), osb[0:3, :, :], is_output=True)
                barrier()

                with ExitStack() as s4:
                    n = 16
                    ti = 16
                    xT = sbuf(s4, "os_xT", [128, 8, 16], BF16)
                    zs = sbuf(s4, "os_zs", [128, 2048], BF16)
                    xaT = sbuf(s4, "os_xaT", [128, 24, 16], BF16)
                    hist_r = ring(s4, "os_hist", [48, 512], F32, 2)
                    histT_r = ring(s4, "os_histT", [128, 4, 48], BF16, 2)
                    raw_r = ring(s4, "os_raw", [16, 512], F32, 2)
                    xbs_r = ring(s4, "os_xbs", [128, 16], BF16, 2)
                    xs = sbuf(s4, "os_xs", [16, 3072], BF16)
                    xsf = sbuf(s4, "os_xsf", [16, 3072])
                    dtx = sbuf(s4, "os_dtx", [16, 2048]); eexp = sbuf(s4, "os_eexp", [16, 2048])
                    eT = sbuf(s4, "os_eT", [128, 16, 16]); dtxT = sbuf(s4, "os_dtxT", [128, 16, 16])
                    yT = sbuf(s4, "os_yT", [128, 16, 16])
                    e16 = sbuf(s4, "os_e16", [16, 16, 128])
                    P.dma("sp", e16[:, :, :], D["c_e16"][:, :].re("p (b m) -> p b m", m=128))
                    S0_r = ring(s4, "os_S0", [128, 16, 128], F32, 2)
                    Sn_r = ring(s4, "os_Sn", [128, 16, 128], F32, 1)
                    tt_r = ring(s4, "os_tt", [128, 128], F32, 3)
                    jk_r = ring(s4, "os_jk", [128, 128], F32, 2)
                    xt = x_r()
                    P.dma("sp", xt[0:n, :], xsrc(debug == "oddfirst", ti))
                    xb = xb_r()
                    P.copy(xb[0:n, :], xt[0:n, :], e="pool")
                    transposes(xT[:, :, :], [xb[0:n, k * 128:(k + 1) * 128] for k in range(8)], n, BF16)
                    for blk in range(4):
                        Wb = load_w(blk * 512, 512)
                        pz = pbank()
                        for k in range(8):
                            P.mm(pz[0:n, :], xT[:, k, 0:n], Wb[:, k, :], start=(k == 0), stop=(k == 7))
                        P.act(zs[0:n, blk * 512:(blk + 1) * 512], pz[0:n, :], AF.Silu)
                    Wd = load_w(5120, 32)
                    pdt = pbank()
                    for k in range(8):
                        P.mm(pdt[0:n, 0:32], xT[:, k, 0:n], Wd[:, k, 0:32], start=(k == 0), stop=(k == 7))
                    dt = softplus_dt(pdt[0:n, 0:32], n)
                    P.dma("sp", D["s_conv"][li, :, 0:2, :], D["st_conv"][li, :, 1:3, :], is_output=True)
                    for blk in range(4, 10):
                        Wb = load_w(blk * 512, 512)
                        c0 = (blk - 4) * 512
                        praw = pbank()
                        for k in range(8):
                            P.mm(praw[0:n, :], xT[:, k, 0:n], Wb[:, k, :], start=(k == 0), stop=(k == 7))
                        raw = raw_r()
                        evac(raw[0:n, :], praw[0:n, :])
                        P.dma("sp", D["s_conv"][li, :, 2, c0:c0 + 512], raw[0:n, :], is_output=True)
                        hist = hist_r()
                        P.dma("sp", hist[:, :], D["st_conv"][li].re("b r c -> (b r) c")[:, c0:c0 + 512])
                        histT = histT_r()
                        transposes(histT[:, :, :], [hist[0:48, s_ * 128:(s_ + 1) * 128] for s_ in range(4)], 48, F32)
                        for sub in range(4):
                            ct = (blk - 4) * 4 + sub
                            px = pbank()
                            for k in range(8):
                                P.mm(px[:, 0:n], Wb[:, k, sub * 128:(sub + 1) * 128], xT[:, k, 0:n], start=(k == 0), stop=(k == 7))
                            xbs = xbs_r()
                            evac(xbs[:, :], px[:, 0:n])
                            dg = make_dg(ct)
                            pc = pbank()
                            for tap in range(3):
                                P.mm(pc[:, 0:n], dg[:, tap, :], histT[:, sub, tap:48:3], start=(tap == 0), stop=False)
                            P.mm(pc[:, 0:n], dg[:, 3, :], xbs[:, :], start=False, stop=True)
                            P.act(xaT[:, ct, :], pc[:, 0:n], AF.Silu, bias=cb[:, ct:ct + 1])
                    transposes(xs[:, :].re("t (c q) -> t c q", q=128), [xaT[:, ct, :] for ct in range(24)], 128, BF16)
                    P.copy(xsf[0:n, :], xs[0:n, :])
                    dA, ee = sm_r(), sm_r()
                    P.tt(dA[0:n, :], dt[0:n, :], aneg[0:n, :], ALU.mult)
                    P.act(ee[0:n, :], dA[0:n, :], AF.Exp)
                    xs3 = xsf[0:n, 0:2048].re("p (h q) -> p h q", q=64)
                    P.tt(dtx[0:n, :].re("p (h q) -> p h q", q=64), xs3, dt[0:n, :].re("p (h o) -> p h o", o=1).bc([n, 32, 64]), ALU.mult)
                    P.copy(eexp[0:n, :].re("p (h q) -> p h q", q=64), ee[0:n, :].re("p (h o) -> p h o", o=1).bc([n, 32, 64]), e="pool")
                    transposes(eT[:, :, :], [eexp[0:n, k * 128:(k + 1) * 128] for k in range(16)], n, F32)
                    transposes(dtxT[:, :, :], [dtx[0:n, k * 128:(k + 1) * 128] for k in range(16)], n, F32)
                    for b in range(16):
                        pB, pC = pbank(), pbank()
                        P.mm(pB[:, :], e16[:, b, :], xsf[0:n, 2048:2560])
                        P.mm(pC[:, :], e16[:, b, :], xsf[0:n, 2560:3072])
                        S0 = S0_r(); Sn = Sn_r()
                        P.dma("sp", S0[:, :, :], D["st_ssd"][li, b].re("(j q) n -> q j n", q=128))
                        for j in range(16):
                            g = j // 4
                            t1 = tt_r()
                            P.act(t1[:, :], S0[:, j, :], AF.Copy, scale=eT[:, j, b:b + 1])
                            P.stt(Sn[:, j, :], pB[:, g * 128:(g + 1) * 128], dtxT[:, j, b:b + 1], t1[:, :], ALU.mult, ALU.add)
                            jk = jk_r()
                            P.stt(jk[:, :], Sn[:, j, :], 1.0, pC[:, g * 128:(g + 1) * 128], ALU.mult, ALU.mult, accum=yT[:, j, b:b + 1])
                        P.dma("sp", D["s_ssd"][li, b].re("(j q) n -> q j n", q=128), Sn[:, :, :], is_output=True)
                    y = y_r()
                    transposes(y[:, :].re("t (c q) -> t c q", q=128), [yT[:, j, :] for j in range(16)], 128, F32)
                    gate_norm_out(ti, n, y, xs3, zs[0:n, :], xt)
            barrier()

        stop = debug
        try:
          if debug == "oddfirst":
              odd_phase(1)
              raise _Stop()
          if debug is not None and debug.startswith("peerfirst"):
              peer_phase_dense(0)
              raise _Stop()
          for layer in range(n_layers):
            if layer % 2 == 0:
                if even_phase(layer):
                    break
            else:
                odd_phase(layer)
            ck(f"mix{layer}")
            peer_phase_dense(layer)
            ck(f"peer{layer}")
        except _Stop:
            pass
        P.finish()
        print("program: instr", P.ninstr, "sems", P.nsem, {e: P.cnt[e] for e in P.ENG})
    return nc

def _consts():
    c = {}
    c["c_ident"] = np.eye(128, dtype=np.float32)
    s = np.arange(128)
    c["c_tri"] = (s[:, None] <= s[None, :]).astype(np.float32)
    e = np.zeros((128, 128), np.float32); e[127, :] = 1.0
    c["c_e127"] = e
    c["c_maskneg"] = np.where(s[:, None] <= s[None, :], 0.0, NEG).astype(np.float32)
    b = np.zeros((128, 255), np.float32); b[:, 127] = 1.0
    c["c_b255"] = b
    c["c_kk"] = (s + 1).astype(np.float32).reshape(128, 1)
    slopes = 2.0 ** (-8.0 * np.arange(1, 9, dtype=np.float32) / 8)
    col = np.arange(256)
    dist = 128 + s[:, None] - col[None, :]
    bp = np.zeros((128, 8, 256), np.float32)
    for j8 in range(8):
        bp[:, j8, :] = np.where((dist >= 0) & (dist < 128), -slopes[HP[j8]] * dist, NEG)
    c["c_bias_p"] = bp.reshape(128, 2048)
    bs = np.zeros((128, 128), np.float32)
    for p in range(128):
        bs[p, :] = -slopes[HP[p % 8]] * (127 - np.arange(128))
    c["c_bias_s"] = bs
    e16 = np.zeros((16, 16, 128), np.float32)
    for b_ in range(16):
        e16[b_, b_, :] = 1.0
    c["c_e16"] = e16.reshape(16, 2048)
    c["c_iota16"] = np.broadcast_to(np.arange(16, dtype=np.float32), (128, 16)).copy()
    c["c_iota128"] = np.broadcast_to(np.arange(128, dtype=np.float32), (128, 128)).copy()
    return c


IN_SHAPES = [
    ("x_p", [2048, 1024]), ("x_s", [16, 1024]), ("st_s5_re", [2, 16, 2048]), ("st_s5_im", [2, 16, 2048]),
    ("cache_k", [2, 16, 128, 128]), ("cache_v", [2, 16, 128, 128]), ("st_ssd", [2, 16, 2048, 128]), ("st_conv", [2, 16, 3, 3072]),
    ("w_in_e", [2, 1024, 1280]), ("lam_re", [2, 2048]), ("lam_im", [2, 2048]), ("log_dt", [2, 2048]),
    ("bblk_re", [2, 128, 2048]), ("bblk_im", [2, 128, 2048]), ("cblk_re", [2, 128, 512]), ("cblk_im", [2, 128, 512]),
    ("s5_d", [2, 512]), ("w_glu", [2, 512, 512]), ("sinks_p", [2, 8]), ("sinks_s", [2, 128]), ("w_out_e", [2, 1024, 1024]),
    ("w_in_o", [2, 1024, 5152]), ("conv_w", [2, 128, 96]), ("conv_b", [2, 128, 24]), ("dt_bias", [2, 32]), ("a_log", [2, 32]),
    ("ssd_d", [2, 32]), ("norm_w", [2, 2048]), ("w_out_o", [2, 2048, 1024]),
    ("ln1_g", [4, 1024]), ("ln1_b", [4, 1024]), ("ln2_g", [4, 1024]), ("ln2_b", [4, 1024]),
    ("w_q", [4, 1024, 2048]), ("k1t", [4, 128, 1024]), ("k2t", [4, 128, 1024]),
    ("peer_ut", [4, 1024, 16384]), ("peer_v", [4, 16384, 1024]),
    ("c_ident", [128, 128]), ("c_tri", [128, 128]), ("c_e127", [128, 128]), ("c_maskneg", [128, 128]), ("c_b255", [128, 255]),
    ("c_kk", [128, 1]), ("c_bias_p", [128, 2048]), ("c_bias_s", [128, 128]), ("c_e16", [16, 2048]), ("c_iota16", [128, 16]), ("c_iota128", [128, 128]),
]
OUT_SHAPES = [
    ("y_p", [2048, 1024]), ("y_s", [16, 1024]), ("p_s5_re", [2, 2048]), ("p_s5_im", [2, 2048]),
    ("p_swa_k", [2, 128, 128]), ("p_swa_v", [2, 128, 128]), ("p_ssd", [2, 2048, 128]), ("p_conv", [2, 3, 3072]),
    ("s_s5_re", [2, 16, 2048]), ("s_s5_im", [2, 16, 2048]), ("s_swa_k", [2, 16, 128, 128]), ("s_swa_v", [2, 16, 128, 128]),
    ("s_ssd", [2, 16, 2048, 128]), ("s_conv", [2, 16, 3, 3072]),
]


def _shared_inputs(inp):
    f = lambda a: np.ascontiguousarray(np.asarray(a, dtype=np.float32))
    sh = {}
    w_in_e = f(inp["w_in_even"])
    perm_q = np.concatenate([np.arange(512)] + [512 + HP[j] * 64 + np.arange(64) for j in range(8)] + [np.arange(1024, 1280)])
    sh["w_in_e"] = np.ascontiguousarray(w_in_e[:, :, perm_q])
    sh["lam_re"] = f(inp["s5_lambda_re"]).reshape(2, 2048)
    sh["lam_im"] = f(inp["s5_lambda_im"]).reshape(2, 2048)
    sh["log_dt"] = np.ascontiguousarray(np.repeat(f(inp["s5_log_dt"]), 64, axis=1))
    for nm, src in (("bblk_re", "s5_b_re"), ("bblk_im", "s5_b_im")):
        b = f(inp[src])
        blk = np.zeros((2, 128, 4, 8, 64), np.float32)
        for ch in range(4):
            for gl in range(8):
                g = ch * 8 + gl
                blk[:, gl * 16:(gl + 1) * 16, ch, gl, :] = np.transpose(b[:, g], (0, 2, 1))
        sh[nm] = blk.reshape(2, 128, 2048)
    for nm, src in (("cblk_re", "s5_c_re"), ("cblk_im", "s5_c_im")):
        cc = f(inp[src])
        blk = np.zeros((2, 128, 16, 2, 16), np.float32)
        for kc in range(16):
            for gl in range(2):
                g = kc * 2 + gl
                blk[:, gl * 64:(gl + 1) * 64, kc, gl, :] = np.transpose(cc[:, g], (0, 2, 1))
        sh[nm] = blk.reshape(2, 128, 512)
    sh["s5_d"] = f(inp["s5_d"])
    sh["w_glu"] = f(inp["s5_w_glu"])
    sk = f(inp["swa_sinks"])[:, HP]
    sh["sinks_p"] = np.ascontiguousarray(sk)
    sh["sinks_s"] = np.ascontiguousarray(np.tile(sk, (1, 16)))
    w_out_e = f(inp["w_out_even"])
    perm_o = np.concatenate([np.arange(512)] + [512 + HP[j] * 64 + np.arange(64) for j in range(8)])
    sh["w_out_e"] = np.ascontiguousarray(w_out_e[:, perm_o, :])
    sh["w_in_o"] = f(inp["w_in_odd"])
    cw = f(inp["ssd_conv_w"])
    sh["conv_w"] = np.ascontiguousarray(cw.reshape(2, 4, 24, 128).transpose(0, 3, 2, 1).reshape(2, 128, 96))
    sh["conv_b"] = np.ascontiguousarray(f(inp["ssd_conv_b"]).reshape(2, 24, 128).transpose(0, 2, 1))
    sh["dt_bias"] = f(inp["ssd_dt_bias"]); sh["a_log"] = f(inp["ssd_a_log"]); sh["ssd_d"] = f(inp["ssd_d"])
    sh["norm_w"] = f(inp["ssd_norm_w"]); sh["w_out_o"] = f(inp["w_out_odd"])
    for k in ("ln1_g", "ln1_b", "ln2_g", "ln2_b"):
        sh[k] = f(inp[k])
    sh["w_q"] = f(inp["peer_w_q"])
    sh["k1t"] = np.ascontiguousarray(f(inp["peer_k1"]).transpose(0, 3, 1, 2).reshape(4, 128, 1024))
    sh["k2t"] = np.ascontiguousarray(f(inp["peer_k2"]).transpose(0, 3, 1, 2).reshape(4, 128, 1024))
    sh["peer_ut"] = np.ascontiguousarray(f(inp["peer_u"]).transpose(0, 2, 1)); sh["peer_v"] = f(inp["peer_v"])
    sh.update(_consts())
    if globals().get("_TINY"):
        for nm in ("w_in_o", "w_q", "w_out_o"):
            sh[nm] = np.ascontiguousarray(sh[nm][:, :128])
    return sh


def _core_inputs(inp, sh, c):
    f = lambda a: np.ascontiguousarray(np.asarray(a, dtype=np.float32))
    b0 = c * 16
    m = dict(sh)
    m["x_p"] = f(inp["x_prompt"][c])
    m["x_s"] = f(inp["x_sample"][b0:b0 + 16, 0])
    m["st_s5_re"] = f(inp["state_s5_re"][:, b0:b0 + 16]).reshape(2, 16, 2048)
    m["st_s5_im"] = f(inp["state_s5_im"][:, b0:b0 + 16]).reshape(2, 16, 2048)
    m["cache_k"] = f(inp["cache_swa_k"][:, b0:b0 + 16]).reshape(2, 16, 128, 128)
    m["cache_v"] = f(inp["cache_swa_v"][:, b0:b0 + 16]).reshape(2, 16, 128, 128)
    m["st_ssd"] = f(inp["state_ssd"][:, b0:b0 + 16]).reshape(2, 16, 2048, 128)
    m["st_conv"] = f(inp["state_conv"][:, b0:b0 + 16])
    if globals().get("_TINY"):
        m["st_ssd"] = np.ascontiguousarray(m["st_ssd"][:, :, :128])
    return m


_NC_CACHE = {}
_LAST = None


def kernel(**inp):
    key = "full"
    if key not in _NC_CACHE:
        _NC_CACHE[key] = build_nc()
    nc = _NC_CACHE[key]
    sh = _shared_inputs(inp)
    in_maps = [_core_inputs(inp, sh, c) for c in range(8)]
    res = run_bass_kernel_spmd(nc, in_maps, core_ids=list(range(8)))
    R = res.results
    global _LAST
    _LAST = R
    g = lambda name: [np.asarray(R[c][name], dtype=np.float32) for c in range(8)]
    y_p = np.stack(g("y_p"), 0)
    y_s = np.concatenate(g("y_s"), 0).reshape(128, 1, 1024)
    p_s5_re = np.stack(g("p_s5_re"), 1).reshape(2, 8, 32, 64)
    p_s5_im = np.stack(g("p_s5_im"), 1).reshape(2, 8, 32, 64)
    p_swa_k = np.stack(g("p_swa_k"), 1).reshape(2, 8, 128, 2, 64)
    p_swa_v = np.stack(g("p_swa_v"), 1).reshape(2, 8, 128, 2, 64)
    p_ssd = np.stack(g("p_ssd"), 1).reshape(2, 8, 32, 64, 128)
    p_conv = np.stack(g("p_conv"), 1).reshape(2, 8, 3, 3072)
    s_s5_re = np.concatenate(g("s_s5_re"), 1).reshape(2, 128, 32, 64)
    s_s5_im = np.concatenate(g("s_s5_im"), 1).reshape(2, 128, 32, 64)
    s_swa_k = np.concatenate(g("s_swa_k"), 1).reshape(2, 128, 128, 2, 64)
    s_swa_v = np.concatenate(g("s_swa_v"), 1).reshape(2, 128, 128, 2, 64)
    s_ssd = np.concatenate(g("s_ssd"), 1).reshape(2, 128, 32, 64, 128)
    s_conv = np.concatenate(g("s_conv"), 1).reshape(2, 128, 3, 3072)
    return (y_p, y_s, p_s5_re, p_s5_im, p_swa_k, p_swa_v, p_ssd, p_conv,
            s_s5_re, s_s5_im, s_swa_k, s_swa_v, s_ssd, s_conv)
```
